# Optimizing a Trainium2 kernel written in Bass

```python
import math
import jax, jax.numpy as jnp
from jax import lax
import numpy as np

D_MODEL = 1024
BATCH = 8
SEQ = 2048
DEPTH = 4
DEC_BATCH = 128
DEC_SEQ = 4
PAST_LEN = 16384
PAGE_SIZE = 128

M_MIXERS = 2
N_A = (DEPTH + M_MIXERS - 1) // M_MIXERS
N_B = DEPTH // M_MIXERS
SC_DIM = 768
SC_W = 3
GDN_H = 6
GDN_DK = 128
GDN_DV = 128
GDN_DIM = GDN_H * GDN_DK
GDN_CONV_W = 4
GDN_CHUNK = 64
N_MEM = 256
XH = 4
XD = 64
XDIM = XH * XD
D_FF = 2816
FFN_W = 3
ALPHA = (2.0 * DEPTH) ** 0.25
BETA_DN = (8.0 * DEPTH) ** -0.25
LN_EPS = 1e-5
RMS_EPS = 1e-6

kernel_name = "hybrid_shortconv_gdn_memxattn_convffn_step"


def layer_norm(x, g, b):
    xf = x.astype(jnp.float32)
    mu = jnp.mean(xf, -1, keepdims=True)
    var = jnp.mean(jnp.square(xf - mu), -1, keepdims=True)
    y = (xf - mu) * lax.rsqrt(var + LN_EPS)
    return (y * g.astype(jnp.float32) + b.astype(jnp.float32)).astype(x.dtype)


def l2norm(x):
    return x * lax.rsqrt(jnp.sum(jnp.square(x), -1, keepdims=True) + RMS_EPS)


def causal_dwconv(x_full, w):
    width = w.shape[0]
    t = x_full.shape[1] - (width - 1)
    out = x_full[:, 0:t] * w[0]
    for j in range(1, width):
        out = out + x_full[:, j:j + t] * w[j]
    return out


def mem_attention(qm, mem_k, mem_v):
    bsz, t, _ = qm.shape
    q = qm.reshape(bsz, t, XH, XD)
    s = jnp.einsum('bthd,bmhd->bhtm', q, mem_k).astype(jnp.float32) * (XD ** -0.5)
    p = jax.nn.softmax(s, axis=-1)
    o = jnp.einsum('bhtm,bmhd->bthd', p.astype(mem_v.dtype), mem_v)
    return o.reshape(bsz, t, XDIM).astype(qm.dtype)


def gated_delta_chunked(q, k, v, g, beta, s0):
    bsz, t, h, dk = q.shape
    dv = v.shape[-1]
    c = GDN_CHUNK if t % GDN_CHUNK == 0 else t
    n = t // c

    def blocks(a):
        a = jnp.moveaxis(a, 2, 1)
        return a.reshape(bsz, h, n, c, *a.shape[3:])

    q = blocks(q * (dk ** -0.5))
    k = blocks(k)
    v = blocks(v)
    g = blocks(g)
    beta = blocks(beta)
    gc = jnp.cumsum(g, axis=-1)
    tril = jnp.tril(jnp.ones((c, c), dtype=bool))
    strict = jnp.tril(jnp.ones((c, c), dtype=bool), -1)
    diff = gc[..., :, None] - gc[..., None, :]
    decay = jnp.exp(jnp.where(tril, diff, -jnp.inf))
    kb = k * beta[..., None]
    a = jnp.einsum('bhnid,bhnjd->bhnij', kb, k) * decay
    a = jnp.where(strict, a, 0.0) + jnp.eye(c, dtype=a.dtype)
    rhs = jnp.concatenate([v * beta[..., None], kb * jnp.exp(gc)[..., None]], axis=-1)
    sol = lax.linalg.triangular_solve(a, rhs, left_side=True, lower=True, unit_diagonal=True)
    u, w = sol[..., :dv], sol[..., dv:]
    qk = jnp.einsum('bhnid,bhnjd->bhnij', q, k) * decay
    q_dec = q * jnp.exp(gc)[..., None]
    k_dec = k * jnp.exp(gc[..., -1:] - gc)[..., None]
    g_last = gc[..., -1]

    def step(s, inp):
        qk_i, qd_i, kd_i, u_i, w_i, gl_i = inp
        v_new = u_i - jnp.einsum('bhcd,bhde->bhce', w_i, s)
        o = jnp.einsum('bhcd,bhde->bhce', qd_i, s) + jnp.einsum('bhcj,bhje->bhce', qk_i, v_new)
        s = s * jnp.exp(gl_i)[..., None, None] + jnp.einsum('bhcd,bhce->bhde', kd_i, v_new)
        return s, o

    xs = tuple(jnp.moveaxis(a_, 2, 0) for a_ in (qk, q_dec, k_dec, u, w, g_last))
    s_fin, o = lax.scan(step, s0, xs)
    o = jnp.transpose(o, (1, 0, 3, 2, 4)).reshape(bsz, t, h, dv)
    return o, s_fin


def shortconv_mixer(x, hist, w_in, w_conv, w_out, mem_k, mem_v):
    h = x @ w_in
    xin, bg, cg, qm = jnp.split(h, [SC_DIM, 2 * SC_DIM, 3 * SC_DIM], axis=-1)
    u_full = jnp.concatenate([hist.astype(x.dtype), cg * xin], axis=1)
    y = bg * causal_dwconv(u_full, w_conv)
    o_mem = mem_attention(qm, mem_k, mem_v)
    out = jnp.concatenate([y, o_mem], axis=-1) @ w_out
    return out, u_full[:, -(SC_W - 1):]


def gdn_mixer(x, hist, s0, w_in, w_conv, a_log, dt_bias, norm_w, w_out, mem_k, mem_v):
    bsz, t, _ = x.shape
    h = x @ w_in
    qkv, z, b, a, qm = jnp.split(
        h, [3 * GDN_DIM, 4 * GDN_DIM, 4 * GDN_DIM + GDN_H, 4 * GDN_DIM + 2 * GDN_H], axis=-1)
    qkv_full = jnp.concatenate([hist.astype(x.dtype), qkv], axis=1)
    qkv_c = jax.nn.silu(causal_dwconv(qkv_full, w_conv)).astype(jnp.float32)
    q, k, v = jnp.split(qkv_c, 3, axis=-1)
    q = l2norm(q.reshape(bsz, t, GDN_H, GDN_DK))
    k = l2norm(k.reshape(bsz, t, GDN_H, GDN_DK))
    v = v.reshape(bsz, t, GDN_H, GDN_DV)
    beta = jax.nn.sigmoid(b.astype(jnp.float32))
    g = -jnp.exp(a_log.astype(jnp.float32)) * jax.nn.softplus(
        a.astype(jnp.float32) + dt_bias.astype(jnp.float32))
    o, s_new = gated_delta_chunked(q, k, v, g, beta, s0.astype(jnp.float32))
    o = o * lax.rsqrt(jnp.mean(jnp.square(o), -1, keepdims=True) + RMS_EPS)
    o = o * norm_w.astype(jnp.float32) * jax.nn.silu(z.astype(jnp.float32).reshape(bsz, t, GDN_H, GDN_DV))
    o = o.reshape(bsz, t, GDN_DIM).astype(x.dtype)
    o_mem = mem_attention(qm, mem_k, mem_v)
    out = jnp.concatenate([o, o_mem], axis=-1) @ w_out
    return out, qkv_full[:, -(GDN_CONV_W - 1):], s_new.astype(s0.dtype)


def channel_mixer(x, hist, w_up, w_conv, w_down):
    h_full = jnp.concatenate([hist.astype(x.dtype), x @ w_up], axis=1)
    hc = causal_dwconv(h_full, w_conv)
    gate, up = jnp.split(hc, 2, axis=-1)
    return (jax.nn.silu(gate) * up) @ w_down, h_full[:, -(FFN_W - 1):]


def trunk(x, mem_k, mem_v, sc_hist, gdn_hist, gdn_s, ffn_hist,
          w_in_a, conv_a, w_out_a, w_in_b, conv_b, a_log, dt_bias, gdn_norm_w, w_out_b,
          ln1_g, ln1_b, ln2_g, ln2_b, w_up, w_conv_ffn, w_down):
    new_sc, new_gc, new_gs, new_ffn = [], [], [], []
    for i in range(DEPTH):
        j = i // M_MIXERS
        if i % M_MIXERS == 0:
            mix, hs = shortconv_mixer(x, sc_hist[j], w_in_a[j], conv_a[j], w_out_a[j], mem_k[i], mem_v[i])
            new_sc.append(hs)
        else:
            mix, hg, sg = gdn_mixer(x, gdn_hist[j], gdn_s[j], w_in_b[j], conv_b[j], a_log[j], dt_bias[j],
                                    gdn_norm_w[j], w_out_b[j], mem_k[i], mem_v[i])
            new_gc.append(hg)
            new_gs.append(sg)
        x = layer_norm(ALPHA * x + mix, ln1_g[i], ln1_b[i])
        f, hf = channel_mixer(x, ffn_hist[i], w_up[i], w_conv_ffn[i], w_down[i])
        new_ffn.append(hf)
        x = layer_norm(ALPHA * x + f, ln2_g[i], ln2_b[i])
    return x, jnp.stack(new_sc), jnp.stack(new_gc), jnp.stack(new_gs), jnp.stack(new_ffn)


def setup_inputs(seed: int = 0) -> dict:
    key = jax.random.key(seed)
    ks = jax.random.split(key, 32)
    nrm = jax.random.normal
    f32 = jnp.float32
    w_in_b_cols = 4 * GDN_DIM + 2 * GDN_H + XDIM
    dt = jnp.exp(jax.random.uniform(ks[15], (N_B, GDN_H), f32, math.log(1e-3), math.log(1e-1)))
    return {
        "x_prompt": nrm(ks[0], (BATCH, SEQ, D_MODEL), f32),
        "x_sample": nrm(ks[1], (DEC_BATCH, DEC_SEQ, D_MODEL), f32),
        "mem_prompt": nrm(ks[2], (BATCH, N_MEM, D_MODEL), f32),
        "cache_mem_k": nrm(ks[3], (DEPTH, DEC_BATCH, N_MEM, XH, XD), f32),
        "cache_mem_v": nrm(ks[4], (DEPTH, DEC_BATCH, N_MEM, XH, XD), f32),
        "state_shortconv": nrm(ks[5], (N_A, DEC_BATCH, SC_W - 1, SC_DIM), f32),
        "state_gdn_conv": nrm(ks[6], (N_B, DEC_BATCH, GDN_CONV_W - 1, 3 * GDN_DIM), f32),
        "state_gdn": 0.1 * nrm(ks[7], (N_B, DEC_BATCH, GDN_H, GDN_DK, GDN_DV), f32),
        "state_ffn_conv": nrm(ks[8], (DEPTH, DEC_BATCH, FFN_W - 1, 2 * D_FF), f32),
        "w_in_a": nrm(ks[9], (N_A, D_MODEL, 3 * SC_DIM + XDIM), f32) * D_MODEL ** -0.5,
        "conv_a": nrm(ks[10], (N_A, SC_W, SC_DIM), f32) * SC_W ** -0.5,
        "w_out_a": nrm(ks[11], (N_A, SC_DIM + XDIM, D_MODEL), f32) * (SC_DIM + XDIM) ** -0.5 * BETA_DN,
        "w_in_b": nrm(ks[12], (N_B, D_MODEL, w_in_b_cols), f32) * D_MODEL ** -0.5,
        "conv_b": nrm(ks[13], (N_B, GDN_CONV_W, 3 * GDN_DIM), f32) * GDN_CONV_W ** -0.5,
        "a_log": jnp.log(jax.random.uniform(ks[14], (N_B, GDN_H), f32, 1.0, 16.0)),
        "dt_bias": dt + jnp.log(-jnp.expm1(-dt)),
        "gdn_norm_w": 1.0 + 0.02 * nrm(ks[16], (N_B, GDN_DV), f32),
        "w_out_b": nrm(ks[17], (N_B, GDN_DIM + XDIM, D_MODEL), f32) * (GDN_DIM + XDIM) ** -0.5 * BETA_DN,
        "w_mem_kv": nrm(ks[18], (DEPTH, D_MODEL, 2 * XDIM), f32) * D_MODEL ** -0.5,
        "ln1_g": 1.0 + 0.02 * nrm(ks[19], (DEPTH, D_MODEL), f32),
        "ln1_b": 0.02 * nrm(ks[20], (DEPTH, D_MODEL), f32),
        "ln2_g": 1.0 + 0.02 * nrm(ks[21], (DEPTH, D_MODEL), f32),
        "ln2_b": 0.02 * nrm(ks[22], (DEPTH, D_MODEL), f32),
        "w_up": nrm(ks[23], (DEPTH, D_MODEL, 2 * D_FF), f32) * D_MODEL ** -0.5,
        "w_conv_ffn": nrm(ks[24], (DEPTH, FFN_W, 2 * D_FF), f32) * FFN_W ** -0.5,
        "w_down": nrm(ks[25], (DEPTH, D_FF, D_MODEL), f32) * D_FF ** -0.5 * BETA_DN,
    }


def reference(x_prompt, x_sample, mem_prompt, cache_mem_k, cache_mem_v, state_shortconv, state_gdn_conv,
              state_gdn, state_ffn_conv, w_in_a, conv_a, w_out_a, w_in_b, conv_b, a_log, dt_bias,
              gdn_norm_w, w_out_b, w_mem_kv, ln1_g, ln1_b, ln2_g, ln2_b, w_up, w_conv_ffn, w_down):
    weights = (w_in_a, conv_a, w_out_a, w_in_b, conv_b, a_log, dt_bias, gdn_norm_w, w_out_b,
               ln1_g, ln1_b, ln2_g, ln2_b, w_up, w_conv_ffn, w_down)
    bsz = x_prompt.shape[0]
    dtp = x_prompt.dtype
    kv = jnp.einsum('bmd,lde->lbme', mem_prompt, w_mem_kv)
    mem_k_prompt = kv[..., :XDIM].reshape(DEPTH, bsz, N_MEM, XH, XD)
    mem_v_prompt = kv[..., XDIM:].reshape(DEPTH, bsz, N_MEM, XH, XD)
    z_sc = jnp.zeros((N_A, bsz, SC_W - 1, SC_DIM), dtp)
    z_gc = jnp.zeros((N_B, bsz, GDN_CONV_W - 1, 3 * GDN_DIM), dtp)
    z_gs = jnp.zeros((N_B, bsz, GDN_H, GDN_DK, GDN_DV), dtp)
    z_ffn = jnp.zeros((DEPTH, bsz, FFN_W - 1, 2 * D_FF), dtp)
    y_prompt, sc_p, gc_p, gs_p, ffn_p = trunk(x_prompt, mem_k_prompt, mem_v_prompt, z_sc, z_gc, z_gs, z_ffn,
                                              *weights)
    y_sample, sc_s, gc_s, gs_s, ffn_s = trunk(x_sample, cache_mem_k, cache_mem_v, state_shortconv,
                                              state_gdn_conv, state_gdn, state_ffn_conv, *weights)
    return (y_prompt, y_sample, mem_k_prompt, mem_v_prompt, sc_p, gc_p, gs_p, ffn_p, sc_s, gc_s, gs_s, ffn_s)
```

```python
import contextlib
import os
import numpy as np
import concourse.bass as bass
import concourse.mybir as mybir
from concourse.bass_utils import run_bass_kernel_spmd

F32 = mybir.dt.float32
BF16 = mybir.dt.bfloat16
AF = mybir.ActivationFunctionType
ALU = mybir.AluOpType
AX = mybir.AxisListType

D = 1024
SEQ = 2048
NCP = 1024
NSC = 64
W = NCP + NSC
NB = 16
TS = 4
DEPTH = 4
SC_DIM = 768
GDN_H = 6
D_FF = 2816
NMEM = 256
ALPHA = (2.0 * DEPTH) ** 0.25
LN_EPS = 1e-5
RMS_EPS = 1e-6
NEG = -30000.0

SP_CONVA = 0
SP_CONVB = SP_CONVA + 36
SP_CONVF = SP_CONVB + 144
SP_LN1G = SP_CONVF + 528
SP_LN1B = SP_LN1G + 32
SP_LN2G = SP_LN1B + 32
SP_LN2B = SP_LN2G + 32
SP_NORMW = SP_LN2B + 32
SP_ALOG = SP_NORMW + 2
SP_DTB = SP_ALOG + 12
NSP = SP_DTB + 12
C_ID = 0
C_ONE = 128
C_TRI = 256
C_MASK = 320
C_STR = 384
NCST = 448


class Res:
    __slots__ = ("name", "lw", "rd", "excl")

    def __init__(self, name, excl=False):
        self.name = name
        self.lw = None
        self.rd = {}
        self.excl = excl


class DSem:
    def __init__(self, name, sem):
        self.name = name
        self.sem = sem
        self.cnt = 0


def fence(new, olds):
    for n in new:
        for o in olds:
            if o.lw is not None and n.rd.get(o.lw[0], 0) < o.lw[1]:
                n.rd[o.lw[0]] = o.lw[1]
            for s, v in o.rd.items():
                if n.rd.get(s, 0) < v:
                    n.rd[s] = v


class Sched:
    def __init__(self, nc, es):
        self.nc = nc
        self.es = es
        self.eng = {}
        self.sems = {}
        self.dry = False
        for name, h in [("pe", nc.tensor), ("act", nc.scalar), ("dve", nc.vector),
                        ("pool", nc.gpsimd), ("sp", nc.sync)]:
            sem = es.enter_context(nc.semaphore("sem_" + name))
            self.eng[name] = dict(h=h, sem=sem, cnt=0, waited={})
            self.sems[name] = sem
        self.ndsem = 0
        self.dsems = {}
        self.ninst = 0

    def dsem(self):
        name = f"dsem{self.ndsem}"
        self.ndsem += 1
        sem = self.es.enter_context(self.nc.semaphore(name))
        self.sems[name] = sem
        d = DSem(name, sem)
        self.dsems[name] = d
        return d

    def _waits(self, en, reads, writes):
        e = self.eng[en]
        deps = {}
        for r in reads:
            if r.lw is not None and deps.get(r.lw[0], 0) < r.lw[1]:
                deps[r.lw[0]] = r.lw[1]
        for w in writes:
            if w.lw is not None and deps.get(w.lw[0], 0) < w.lw[1]:
                deps[w.lw[0]] = w.lw[1]
            for s, v in w.rd.items():
                if deps.get(s, 0) < v:
                    deps[s] = v
        for s, v in deps.items():
            if en == "pe" and s == "pe":
                continue
            if s in self.dsems:
                v = self.dsems[s].cnt
            if e["waited"].get(s, 0) < v:
                e["h"].wait_ge(self.sems[s], v)
                e["waited"][s] = v

    def emit(self, en, fn, reads=(), writes=()):
        if self.dry:
            return None
        e = self.eng[en]
        if any(r.excl for r in reads):
            writes = list(writes) + [r for r in reads if r.excl]
            reads = [r for r in reads if not r.excl]
        self._waits(en, reads, writes)
        ins = fn(e["h"])
        e["cnt"] += 1
        ins.then_inc(e["sem"], 1)
        c = e["cnt"]
        for r in reads:
            if r.rd.get(en, 0) < c:
                r.rd[en] = c
        for w in writes:
            w.lw = (en, c)
            w.rd = {}
        self.ninst += 1
        return ins

    def dma(self, qn, out, in_, reads, writes, ds):
        if self.dry:
            return None
        e = self.eng[qn]
        self._waits(qn, reads, writes)
        ins = e["h"].dma_start(out=out, in_=in_)
        ds.cnt += 16
        ins.then_inc(ds.sem, 16)
        for r in reads:
            if r.rd.get(ds.name, 0) < ds.cnt:
                r.rd[ds.name] = ds.cnt
        for w in writes:
            w.lw = (ds.name, ds.cnt)
            w.rd = {}
        self.ninst += 1
        return ins


GDN_PIPE = os.environ.get("GDN_PIPE", "1") == "1"
CONV_ACT = os.environ.get("CONV_ACT", "1") == "1"
KPARTS = os.environ.get("KPARTS", "att,att2,mix,out,ln,ffn").split(",")


def build(nlayers=DEPTH, npass=2):
    nc = bass.Bass("TRN2", target_bir_lowering=False)

    def din(name, shape):
        return nc.dram_tensor(name, list(shape), F32, kind="ExternalInput").ap()

    def dout(name, shape):
        return nc.dram_tensor(name, list(shape), F32, kind="ExternalOutput").ap()

    xpT = din("xpT", [D, SEQ])
    xsT = din("xsT", [D, NSC])
    memT = din("memT", [D, NMEM])
    ckT = din("ckT", [DEPTH, 256, NB, NMEM])
    cv = din("cv", [DEPTH, NB, NMEM, 256])
    scT = din("scT", [2, SC_DIM, 2, NB])
    gcT = din("gcT", [2, 2304, 3, NB])
    gs = din("gs", [2, NB, 128, GDN_H, 128])
    wbaT = din("wbaT", [2, 128, 8, 12])
    ffT = din("ffT", [DEPTH, 2 * D_FF, 2, NB])
    w_in_a = din("w_in_a", [2, D, 2560])
    w_out_a = din("w_out_a", [2, D, D])
    w_in_b = din("w_in_b", [2, D, 3340])
    w_out_b = din("w_out_b", [2, D, D])
    w_mem_kv = din("w_mem_kv", [DEPTH, D, 512])
    w_up = din("w_up", [DEPTH, D, 2 * D_FF])
    w_down = din("w_down", [DEPTH, D_FF, D])
    spd = din("spd", [128, NSP])
    cstd = din("cstd", [128, NCST])

    ypT = dout("ypT", [D, SEQ])
    ysT = dout("ysT", [D, NSC])
    mk = dout("mk", [DEPTH, NMEM, 256])
    mv = dout("mv", [DEPTH, NMEM, 256])
    scpT = dout("scpT", [2, 128, 6, 2])
    gcpT = dout("gcpT", [2, 128, 18, 3])
    gsp = dout("gsp", [2, 128, GDN_H, 128])
    ffpT = dout("ffpT", [DEPTH, 128, 44, 2])
    scsT = dout("scsT", [2, SC_DIM, 2, NB])
    gcsT = dout("gcsT", [2, 2304, 3, NB])
    gss = dout("gss", [2, NB, 128, GDN_H, 128])
    ffsT = dout("ffsT", [DEPTH, 2 * D_FF, 2, NB])

    es = contextlib.ExitStack()
    with es:
        S = Sched(nc, es)

        def sb(name, shape, dt=F32):
            return es.enter_context(nc.sbuf_tensor(name, list(shape), dt))

        xres = sb("xres", [128, 8, W]); r_xres = Res("xres")
        xbf = sb("xbf", [128, 8, W], BF16); r_xbf = Res("xbf")
        obuf = sb("obuf", [128, 8, W], BF16); r_obuf = Res("obuf")
        cst = sb("cst", [128, NCST]); r_cst = Res("cst")
        spt = sb("spt", [128, NSP]); r_sp = Res("sp")
        mask6 = sb("mask6", [64, 6, 64]); r_mask6 = Res("mask6")
        identb = sb("identb", [128, 128], BF16)
        onesb = sb("onesb", [128, 128], BF16)
        meanb = sb("meanb", [128, 128], BF16)
        r_cb = Res("constb")
        sc_carry = sb("sc_carry", [128, 2, 6, 2]); r_scc = Res("scc")
        gc_carry = sb("gc_carry", [128, 2, 18, 3]); r_gcc = Res("gcc")
        ff_carry = sb("ff_carry", [128, DEPTH, 44, 2]); r_ffc = Res("ffc")
        Sst = sb("Sst", [128, 2, GDN_H, 128]); r_Sst = [Res("Sst0"), Res("Sst1")]
        wba = sb("wba", [128, 8, 12], BF16); r_wba = Res("wba")
        negA = sb("negA", [128, 6]); r_negA = Res("negA")
        g_beta = sb("g_beta", [64, 96]); g_g = sb("g_g", [64, 96]); g_gc = sb("g_gc", [64, 96])
        g_egc = sb("g_egc", [64, 96]); g_negegc = sb("g_negegc", [64, 96]); g_kds = sb("g_kds", [64, 96])
        g_tmp = sb("g_tmp", [64, 96]); g_gam = sb("g_gam", [128, 96])
        r_gsc = Res("gsc")
        NSLOT = 3
        wslots = [(sb(f"wslot{i}", [128, 4096], BF16), Res(f"wslot{i}"), S.dsem()) for i in range(NSLOT)]
        BIGA = sb("BIGA", [128, 24 * W], BF16)
        SCRN = 10240
        SCR = sb("SCR", [128, SCRN])

        psum = [(es.enter_context(nc.psum_tensor(f"ps{i}", [128, 512], F32)), Res(f"ps{i}", True)) for i in range(7)]
        psbf = (es.enter_context(nc.psum_tensor("psbf", [128, 1024], BF16)), Res("psbf", True))
        psi = [0]

        def ps_next():
            b = psum[psi[0] % 7]
            psi[0] += 1
            return b

        d_in = S.dsem()
        d_x = S.dsem()
        d_h = [S.dsem(), S.dsem()]
        d_ss = [S.dsem(), S.dsem()]
        d_memb = S.dsem(); d_ckv = S.dsem(); d_wba = S.dsem(); d_kvn = S.dsem()
        d_oh = [S.dsem(), S.dsem()]; d_so = [S.dsem(), S.dsem()]; d_y = S.dsem(); d_fin = S.dsem()
        out_dsems = [d_kvn, d_oh[0], d_oh[1], d_so[0], d_so[1], d_y, d_fin]

        ident = cst[:, C_ID:C_ID + 128]
        ones = cst[:, C_ONE:C_ONE + 128]
        tri = cst[:, C_TRI:C_TRI + 64]
        maskT = cst[:, C_MASK:C_MASK + 64]
        strU = cst[:, C_STR:C_STR + 64]

        def spv(off, n=1):
            return spt[:, off:off + n]

        qkvz = BIGA[:, :].rearrange("p (c w) -> p c w", c=24)
        r_qkvz = Res("qkvz")
        actb = BIGA[:, 0:22 * W].rearrange("p (c w) -> p c w", c=22)
        r_actb = Res("actb")
        ln_zb = BIGA[:, 0:4096].rearrange("p (k n) -> p k n", k=8)
        ln_sqb = BIGA[:, 4096:8192].rearrange("p (k n) -> p k n", k=8)
        ln_f32 = BIGA[:, 8192:8192 + 2 * (4096 + 1024)].bitcast(F32)
        ln_t1 = ln_f32[:, 0:4096].rearrange("p (k n) -> p k n", k=8)
        ln_mean = ln_f32[:, 4096:4608]
        ln_rstd = ln_f32[:, 4608:5120]
        r_ln = Res("ln")
        ckb = BIGA[:, 0:8192].rearrange("p (c b m) -> p c b m", c=2, b=NB)
        cvb = BIGA[:, 8192:16384].rearrange("p (c b m) -> p c b m", c=2, b=NB)
        r_ckv = Res("ckv")
        biga_groups = {"qkvz": [r_qkvz], "act": [r_actb], "ln": [r_ln], "ckv": [r_ckv]}
        biga_cur = [None]

        def biga_phase(name):
            if biga_cur[0] is not None and biga_cur[0] != name:
                fence(biga_groups[name], biga_groups[biga_cur[0]])
            biga_cur[0] = name

        scr_res = {}

        def scrv(phase, name, off, n):
            key = (phase, name)
            if key not in scr_res:
                scr_res[key] = Res(f"scr_{phase}_{name}")
            assert off + n <= SCRN, (phase, name, off, n)
            return SCR[:, off:off + n], scr_res[key]

        scr_cur = [None]

        def scr_phase(name):
            if scr_cur[0] is not None and scr_cur[0] != name:
                new = [r for (ph, _), r in scr_res.items() if ph == name]
                old = [r for (ph, _), r in scr_res.items() if ph == scr_cur[0]]
                fence(new, old)
            scr_cur[0] = name

        RAWN = 3 + NCP + 1
        conv_raw = [scrv("conv", f"raw{i}", i * RAWN, RAWN) for i in range(2)]
        o = 2 * RAWN
        conv_raws = [scrv("conv", f"raws{i}", o + i * 112, 112) for i in range(2)]
        o += 224
        conv_acc = [scrv("conv", f"acc{i}", o + i * W, W) for i in range(2)]
        o += 2 * W
        conv_tmp = [scrv("conv", f"tmp{i}", o + i * 512, 512) for i in range(2)]
        o += 1024
        conv_bgb = scrv("conv", "bgb", o, W)
        o += W
        conv_sq = scrv("conv", "sq", o, W // 2)
        o += W // 2
        conv_rs = scrv("conv", "rs", o, 512)
        o += 512
        assert o <= SCRN, o
        o = 0
        att_qbuf = scrv("att", "qbuf", o, W); o += W
        att_e = [scrv("att", f"e{i}", o + i * 256, 256) for i in range(2)]; o += 512
        att_rden = scrv("att", "rden", o, 512); o += 512
        att_kvn = scrv("att", "kvn", o, 512); o += 512
        att_KT = scrv("att", "KT", o, 256); o += 256
        att_V = scrv("att", "V", o, 256); o += 256
        att_memb = scrv("att", "memb", o, 1024); o += 1024
        att_es = scrv("att", "es", o, 256); o += 256
        o = 0
        gd = {}
        for nm, n in [("gtri", 384), ("dT", 384), ("U", 192), ("Pa", 192), ("Pb", 192), ("PTa", 192), ("PTb", 192),
                      ("Va", 192), ("Vb", 192),
                      ("Vf0", 192), ("Vf1", 192), ("qkT0", 192), ("qkT1", 192), ("kdec0", 384), ("kdec1", 384),
                      ("vtm0", 384), ("vtm1", 384),
                      ("Y", 384), ("vnew", 384), ("ot", 768), ("sq", 768), ("ssq", 16),
                      ("Ss0", 768), ("Ss1", 768), ("Ssb0", 384), ("Ssb1", 384), ("Spb", 384)]:
            gd[nm] = scrv("gdn", nm, o, n)
            o += n
        assert o <= SCRN, o

        class WS:
            def __init__(self):
                self.plan = []
                self.issued = 0
                self.popped = 0
                self.planning = True
                self.PREF = 2

            def _issue(self, s):
                tile_, res, ds = wslots[s % NSLOT]
                KC, parts = self.plan[s]
                nw = sum(n for _, n, _ in parts)
                v = tile_[:, 0:KC * nw].rearrange("p (k n) -> p k n", k=KC)
                for off, n, src in parts:
                    S.dma("pool", v[:, :, off:off + n], src.rearrange("(k p) n -> p k n", p=128), [], [res], ds)

            def stage(self, KC, parts):
                if self.planning:
                    self.plan.append((KC, parts))
                    return None, None
                while self.issued < min(len(self.plan), self.popped + self.PREF + 1):
                    self._issue(self.issued)
                    self.issued += 1
                s = self.popped
                self.popped += 1
                KCp, partsp = self.plan[s]
                assert KCp == KC and len(partsp) == len(parts)
                tile_, res, ds = wslots[s % NSLOT]
                nw = sum(n for _, n, _ in parts)
                return tile_[:, 0:KC * nw].rearrange("p (k n) -> p k n", k=KC), res

        ws = WS()

        def mm_group(out_ap, pres, pairs, reads):
            def fn(e):
                ins = None
                n = len(pairs)
                for i, (l, r) in enumerate(pairs):
                    ins = e.matmul(out_ap, lhsT=l, rhs=r, start=(i == 0), stop=(i == n - 1))
                return ins
            S.emit("pe", fn, reads, [pres])

        def mm_multi(items, pres_list, reads):
            def fn(e):
                ins = None
                for out_ap, pairs in items:
                    n = len(pairs)
                    for i, (l, r) in enumerate(pairs):
                        ins = e.matmul(out_ap, lhsT=l, rhs=r, start=(i == 0), stop=(i == n - 1))
                return ins
            S.emit("pe", fn, reads, pres_list)

        def tr_multi(items, pres_list, reads):
            def fn(e):
                ins = None
                for out_ap, in_ap, id_ap in items:
                    ins = e.transpose(out_ap, in_ap, id_ap)
                return ins
            S.emit("pe", fn, reads, pres_list)

        def coltiles(p):
            ct = [(0, 512), (512, 512)]
            if p == 1:
                ct.append((NCP, NSC))
            return ct

        def is_s(c0):
            return c0 >= NCP

        def program():
            psi[0] = 0
            biga_cur[0] = None
            scr_cur[0] = None
            S.dma("sp", cst[:, :], cstd[:, :], [], [r_cst], d_in)
            S.dma("sp", spt[:, :], spd[:, :], [], [r_sp], d_in)
            S.emit("dve", lambda e: e.tensor_copy(out=identb[:, :], in_=ident), [r_cst], [r_cb])
            S.emit("dve", lambda e: e.tensor_copy(out=onesb[:, :], in_=ones), [r_cst], [r_cb])
            S.emit("dve", lambda e: e.tensor_scalar(out=meanb[:, :], in0=ones, scalar1=1.0 / D, scalar2=None,
                                                    op0=ALU.mult), [r_cst], [r_cb])
            S.emit("dve", lambda e: e.tensor_copy(
                out=mask6[:, :, :], in_=maskT[0:64, :].unsqueeze(1).to_broadcast([64, 6, 64])), [r_cst], [r_mask6])
            S.emit("dve", lambda e: e.memset(obuf[:, :, :], 0.0), [], [r_obuf])
            S.emit("dve", lambda e: e.memset(sc_carry[:, :, :, :], 0.0), [], [r_scc])
            S.emit("dve", lambda e: e.memset(gc_carry[:, :, :, :], 0.0), [], [r_gcc])
            S.emit("dve", lambda e: e.memset(ff_carry[:, :, :, :], 0.0), [], [r_ffc])
            for j in range(2):
                S.emit("dve", lambda e, j=j: e.memset(Sst[:, j, :, :], 0.0), [], [r_Sst[j]])

            for p in range(npass):
                ct = coltiles(p)
                ncol = NCP + (NSC if p == 1 else 0)
                S.dma("sp", xres[:, :, 0:NCP], xpT[:, p * NCP:(p + 1) * NCP].rearrange("(k q) t -> q k t", q=128),
                      [], [r_xres], d_x)
                if p == 1:
                    S.dma("sp", xres[:, :, NCP:W], xsT[:, :].rearrange("(k q) t -> q k t", q=128), [], [r_xres], d_x)
                for kc in range(8):
                    en = ("act", "dve")[kc % 2]
                    if en == "act":
                        S.emit("act", lambda e, kc=kc: e.activation(out=xbf[:, kc, 0:ncol], in_=xres[:, kc, 0:ncol],
                                                                  func=AF.Copy), [r_xres], [r_xbf])
                    else:
                        S.emit(en, lambda e, kc=kc: e.tensor_copy(out=xbf[:, kc, 0:ncol], in_=xres[:, kc, 0:ncol]),
                               [r_xres], [r_xbf])
                for l in range(nlayers):
                    if l % 2 == 0:
                        mixer_a(l, p)
                    else:
                        mixer_b(l, p)
                    if "ln" in KPARTS:
                        layernorm(SP_LN1G + l * 8, SP_LN1B + l * 8, p)
                    if "ffn" in KPARTS:
                        ffn(l, p)
                    if "ln" in KPARTS:
                        layernorm(SP_LN2G + l * 8, SP_LN2B + l * 8, p)
                S.dma("sp", ypT[:, p * NCP:(p + 1) * NCP].rearrange("(k q) t -> q k t", q=128), xres[:, :, 0:NCP],
                      [r_xres], [], d_y)
                if p == 1:
                    S.dma("sp", ysT[:, :].rearrange("(k q) t -> q k t", q=128), xres[:, :, NCP:W], [r_xres], [], d_y)
            for j in range(2):
                S.dma("sp", scpT[j], sc_carry[:, j, :, :], [r_scc], [], d_fin)
                S.dma("sp", gcpT[j], gc_carry[:, j, :, :], [r_gcc], [], d_fin)
                S.dma("sp", gsp[j], Sst[:, j, :, :], [r_Sst[j]], [], d_fin)
            for l in range(DEPTH):
                S.dma("sp", ffpT[l], ff_carry[:, l, :, :], [r_ffc], [], d_fin)
            if not S.dry:
                for ds in out_dsems:
                    if ds.cnt > 0:
                        S.eng["sp"]["h"].wait_ge(ds.sem, ds.cnt)

        def conv_taps(p, accv, racc, rawv, rraw, rawsv, rraws, woff, Wd):
            Hh = Wd - 1
            a_p = accv[:, 0:NCP]
            for jj in range(Wd):
                src = rawv[:, 3 - Hh + jj:3 - Hh + jj + NCP]
                wap = spv(woff + jj)
                if jj == 0 and CONV_ACT:
                    S.emit("act", lambda e, src=src, wap=wap: e.activation(
                        out=a_p, in_=src, func=AF.Copy, scale=wap), [rraw, r_sp], [racc])
                elif jj == 0:
                    S.emit("dve", lambda e, src=src, wap=wap: e.tensor_scalar(
                        out=a_p, in0=src, scalar1=wap, scalar2=None, op0=ALU.mult), [rraw, r_sp], [racc])
                else:
                    S.emit("dve", lambda e, src=src, wap=wap: e.scalar_tensor_tensor(
                        out=a_p, in0=src, scalar=wap, in1=a_p, op0=ALU.mult, op1=ALU.add), [rraw, r_sp, racc], [racc])
            if p == 1:
                a_s = accv[:, NCP:W].rearrange("p (b t) -> p b t", b=NB)
                rs3 = rawsv.rearrange("p (t b) -> p b t", b=NB)
                for jj in range(Wd):
                    src = rs3[:, :, 3 - Hh + jj:3 - Hh + jj + TS]
                    wap = spv(woff + jj)
                    if jj == 0 and CONV_ACT:
                        S.emit("act", lambda e, src=src, wap=wap: e.activation(
                            out=a_s, in_=src, func=AF.Copy, scale=wap), [rraws, r_sp], [racc])
                    elif jj == 0:
                        S.emit("dve", lambda e, src=src, wap=wap: e.tensor_scalar(
                            out=a_s, in0=src, scalar1=wap, scalar2=None, op0=ALU.mult), [rraws, r_sp], [racc])
                    else:
                        S.emit("dve", lambda e, src=src, wap=wap: e.scalar_tensor_tensor(
                            out=a_s, in0=src, scalar=wap, in1=a_s, op0=ALU.mult, op1=ALU.add),
                            [rraws, r_sp, racc], [racc])

        def raw_dst(c0, n, rawv, rawsv):
            if is_s(c0):
                return rawsv.rearrange("p (t b) -> p b t", b=NB)[:, :, 3:3 + TS]
            return rawv[:, 3 + c0:3 + c0 + n]

        def ps_src(psap, c0, n):
            if is_s(c0):
                return psap[:, 0:NSC].rearrange("p (b t) -> p b t", b=NB)
            return psap[:, 0:n]

        def attention(l, p, w_in, qcol0):
            scr_phase("att")
            qbuf, r_q = att_qbuf
            qb = qbuf.bitcast(BF16).rearrange("p (c w) -> p c w", c=2)
            KT = att_KT[0].bitcast(BF16).rearrange("p (c m) -> p c m", c=2); r_KT = att_KT[1]
            Vt = att_V[0].bitcast(BF16).rearrange("p (c m) -> p c m", c=2); r_V = att_V[1]
            memb = att_memb[0].bitcast(BF16).rearrange("p (k m) -> p k m", k=8); r_memb = att_memb[1]
            kvn, r_kvn = att_kvn
            S.dma("pool", memb, memT[:, :].rearrange("(k q) m -> q k m", q=128), [], [r_memb], d_memb)
            wt, rw = ws.stage(8, [(0, 512, w_mem_kv[l])])
            if not S.dry:
                for mc in range(2):
                    pt, pr = ps_next()
                    mm_group(pt[:, :], pr, [(memb[:, kc, mc * 128:(mc + 1) * 128], wt[:, kc, :]) for kc in range(8)],
                             [r_memb, rw])
                    S.emit("dve", lambda e, pt=pt, mc=mc: e.tensor_copy(out=Vt[:, mc, :], in_=pt[:, 256:512]), [pr], [r_V])
                    if p == 0:
                        S.emit("act", lambda e, pt=pt: e.activation(out=kvn, in_=pt[:, :], func=AF.Copy), [pr], [r_kvn])
                        S.dma("sp", mk[l, mc * 128:(mc + 1) * 128, :], kvn[:, 0:256], [r_kvn], [], d_kvn)
                        S.dma("sp", mv[l, mc * 128:(mc + 1) * 128, :], kvn[:, 256:512], [r_kvn], [], d_kvn)
                for c in range(2):
                    pt, pr = ps_next()
                    mm_group(pt[:, 0:256], pr, [(wt[:, kc, c * 128:(c + 1) * 128], memb[:, kc, :]) for kc in range(8)],
                             [r_memb, rw])
                    S.emit("act", lambda e, pt=pt, c=c: e.activation(out=KT[:, c, :], in_=pt[:, 0:256], func=AF.Copy),
                           [pr], [r_KT])
            if p == 1:
                biga_phase("ckv")
                for c_ in range(2):
                    S.dma("pool", ckb[:, c_, :, :], ckT[l, c_ * 128:(c_ + 1) * 128, :, :], [], [r_ckv], d_ckv)
                    S.dma("pool", cvb[:, c_, :, :], cv[l, :, c_ * 128:(c_ + 1) * 128, :].rearrange("b q e -> q b e"), [], [r_ckv], d_ckv)
            wt, rw = ws.stage(8, [(0, 256, w_in[:, qcol0:qcol0 + 256])])
            if S.dry:
                return
            for c in range(2):
                for (c0, n) in coltiles(p):
                    pt, pr = ps_next()
                    mm_group(pt[:, 0:n], pr, [(wt[:, kc, c * 128:(c + 1) * 128], xbf[:, kc, c0:c0 + n]) for kc in range(8)],
                             [r_xbf, rw])
                    S.emit("act", lambda e, pt=pt, c=c, c0=c0, n=n: e.activation(
                        out=qb[:, c, c0:c0 + n], in_=pt[:, 0:n], func=AF.Copy, scale=0.125), [pr], [r_q])
            rden, r_rden = att_rden
            for h in range(4 if "att2" in KPARTS else 0):
                c = h // 2
                pb = (h % 2) * 64
                for (c0, n) in ((0, 512), (512, 512)):
                    evs = []
                    for mc in range(2):
                        pt, pr = ps_next()
                        mm_group(pt[:, 0:n], pr, [(KT[pb:pb + 64, c, mc * 128:(mc + 1) * 128], qb[pb:pb + 64, c, c0:c0 + n])],
                                 [r_KT, r_q])
                        ev, r_ev = att_e[mc]
                        evb = ev.bitcast(BF16)
                        S.emit("act", lambda e, pt=pt, evb=evb, n=n: e.activation(out=evb[:, 0:n], in_=pt[:, 0:n], func=AF.Exp),
                               [pr], [r_ev])
                        evs.append((evb, r_ev))
                    pso, pro = ps_next()
                    mm_group(pso[pb:pb + 64, 0:n], pro,
                             [(Vt[:, mc, h * 64:(h + 1) * 64], evs[mc][0][:, 0:n]) for mc in range(2)],
                             [r_V, evs[0][1], evs[1][1]])
                    psd, prd = ps_next()
                    mm_group(psd[pb:pb + 64, 0:n], prd, [(onesb[:, 0:64], evs[mc][0][:, 0:n]) for mc in range(2)],
                             [r_cb, evs[0][1], evs[1][1]])
                    S.emit("dve", lambda e, psd=psd, pb=pb, n=n: e.reciprocal(out=rden[pb:pb + 64, 0:n], in_=psd[pb:pb + 64, 0:n]),
                           [prd], [r_rden])
                    S.emit("dve", lambda e, pso=pso, pb=pb, n=n, c=c, c0=c0: e.tensor_tensor(
                        out=obuf[pb:pb + 64, 6 + c, c0:c0 + n], in0=pso[pb:pb + 64, 0:n], in1=rden[pb:pb + 64, 0:n],
                        op=ALU.mult), [pro, r_rden], [r_obuf])
            if p == 1:
                esb = att_es[0].bitcast(BF16); r_es = att_es[1]
                pt, pr = ps_next()
                items = []
                for b in range(NB):
                    for h in range(4):
                        c = h // 2
                        pb = (h % 2) * 64
                        for mc in range(2):
                            idx = ((b * 4 + h) * 2 + mc) * TS
                            items.append((pt[:, idx:idx + TS],
                                          [(ckb[pb:pb + 64, c, b, mc * 128:(mc + 1) * 128],
                                            qb[pb:pb + 64, c, NCP + b * TS:NCP + (b + 1) * TS])]))
                mm_multi(items, [pr], [r_ckv, r_q])
                S.emit("act", lambda e: e.activation(out=esb[:, :], in_=pt[:, :], func=AF.Exp), [pr], [r_es])
                pso, pro = ps_next()
                psd, prd = ps_next()
                items = []
                for b in range(NB):
                    for h in range(4):
                        c = h // 2
                        pb = (h % 2) * 64
                        oc0 = c * NSC + b * TS
                        prs_o = []
                        prs_d = []
                        for mc in range(2):
                            idx = ((b * 4 + h) * 2 + mc) * TS
                            prs_o.append((cvb[:, mc, b, h * 64:(h + 1) * 64], esb[:, idx:idx + TS]))
                            prs_d.append((onesb[:, 0:64], esb[:, idx:idx + TS]))
                        items.append((pso[pb:pb + 64, oc0:oc0 + TS], prs_o))
                        items.append((psd[pb:pb + 64, oc0:oc0 + TS], prs_d))
                mm_multi(items, [pro, prd], [r_ckv, r_es, r_cb])
                S.emit("dve", lambda e: e.reciprocal(out=rden[:, 0:128], in_=psd[:, 0:128]), [prd], [r_rden])
                S.emit("dve", lambda e: e.tensor_tensor(
                    out=obuf[:, 6:8, NCP:W], in0=pso[:, 0:128].rearrange("p (c t) -> p c t", c=2),
                    in1=rden[:, 0:128].rearrange("p (c t) -> p c t", c=2), op=ALU.mult), [pro, r_rden], [r_obuf])

        def out_proj(w_out, p):
            for st in range(2):
                wt, rw = ws.stage(8, [(0, 512, w_out[:, st * 512:(st + 1) * 512])])
                if S.dry:
                    continue
                for cc in range(4):
                    oc = st * 4 + cc
                    for (c0, n) in coltiles(p):
                        pt, pr = ps_next()
                        mm_group(pt[:, 0:n], pr, [(wt[:, kc, cc * 128:(cc + 1) * 128], obuf[:, kc, c0:c0 + n]) for kc in range(8)],
                                 [r_obuf, rw])
                        S.emit("dve", lambda e, pt=pt, oc=oc, c0=c0, n=n: e.scalar_tensor_tensor(
                            out=xres[:, oc, c0:c0 + n], in0=xres[:, oc, c0:c0 + n], scalar=ALPHA, in1=pt[:, 0:n],
                            op0=ALU.mult, op1=ALU.add), [pr, r_xres], [r_xres])

        def mixer_a(l, p):
            j = l // 2
            Wm = w_in_a[j]
            if "att" in KPARTS:
                attention(l, p, Wm, 3 * SC_DIM)
            scr_phase("conv")
            for c in range(6 if "mix" in KPARTS else 0):
                wt, rw = ws.stage(8, [(0, 128, Wm[:, c * 128:(c + 1) * 128]),
                                      (128, 128, Wm[:, SC_DIM + c * 128:SC_DIM + (c + 1) * 128]),
                                      (256, 128, Wm[:, 2 * SC_DIM + c * 128:2 * SC_DIM + (c + 1) * 128])])
                if S.dry:
                    continue
                rawv, rraw = conv_raw[c % 2]
                rawsv, rraws = conv_raws[c % 2]
                accv, racc = conv_acc[c % 2]
                bgb, rbg = conv_bgb
                S.emit("dve", lambda e, c=c: e.tensor_copy(out=rawv[:, 1:3], in_=sc_carry[:, j, c, :]), [r_scc], [rraw])
                if p == 1:
                    S.dma("sp", rawsv[:, NB:3 * NB].rearrange("p (r b) -> p r b", b=NB), scT[j, c * 128:(c + 1) * 128, :, :],
                          [], [rraws], d_h[c % 2])
                for ti, (c0, n) in enumerate(coltiles(p)):
                    pss = []
                    for part in range(3):
                        pt, pr = ps_next()
                        mm_group(pt[:, 0:n], pr, [(wt[:, kc, part * 128:(part + 1) * 128], xbf[:, kc, c0:c0 + n])
                                                 for kc in range(8)], [r_xbf, rw])
                        pss.append((pt, pr))
                    tmpv, rtmp = conv_tmp[ti % 2]
                    S.emit("act", lambda e, pt=pss[0][0], n=n: e.activation(out=tmpv[:, 0:n], in_=pt[:, 0:n], func=AF.Copy),
                           [pss[0][1]], [rtmp])
                    dst = raw_dst(c0, n, rawv, rawsv)
                    in1 = tmpv[:, 0:NSC].rearrange("p (b t) -> p b t", b=NB) if is_s(c0) else tmpv[:, 0:n]
                    S.emit("dve", lambda e, pt=pss[2][0], dst=dst, in1=in1, c0=c0, n=n: e.tensor_tensor(
                        out=dst, in0=ps_src(pt, c0, n), in1=in1, op=ALU.mult),
                        [pss[2][1], rtmp], [rraws if is_s(c0) else rraw])
                    S.emit("act", lambda e, pt=pss[1][0], c0=c0, n=n: e.activation(out=bgb[:, c0:c0 + n], in_=pt[:, 0:n],
                                                                                 func=AF.Copy), [pss[1][1]], [rbg])
                conv_taps(p, accv, racc, rawv, rraw, rawsv, rraws, SP_CONVA + (j * 6 + c) * 3, 3)
                S.emit("dve", lambda e, c=c: e.tensor_copy(out=sc_carry[:, j, c, :], in_=rawv[:, 3 + NCP - 2:3 + NCP]),
                       [rraw], [r_scc])
                if p == 1:
                    S.dma("sp", scsT[j, c * 128:(c + 1) * 128, :, :],
                          rawsv[:, 5 * NB:7 * NB].rearrange("p (r b) -> p r b", b=NB), [rraws], [], d_oh[c % 2])
                ncol = NCP + (NSC if p == 1 else 0)
                S.emit("dve", lambda e, c=c, ncol=ncol: e.tensor_tensor(out=obuf[:, c, 0:ncol], in0=bgb[:, 0:ncol],
                                                                         in1=accv[:, 0:ncol], op=ALU.mult),
                       [rbg, racc], [r_obuf])
            if "out" in KPARTS:
                out_proj(w_out_a[j], p)

        def mixer_b(l, p):
            j = l // 2
            Wm = w_in_b[j]
            attention(l, p, Wm, 4 * SC_DIM + 12)
            scr_phase("conv")
            biga_phase("qkvz")
            ncol = NCP + (NSC if p == 1 else 0)
            S.dma("pool", wba[:, :, :], wbaT[j], [], [r_wba], d_wba)
            S.emit("act", lambda e: e.activation(out=negA[:, :], in_=spv(SP_ALOG + j * 6, 6), func=AF.Exp), [r_sp], [r_negA])
            S.emit("dve", lambda e: e.tensor_scalar(out=negA[:, :], in0=negA[:, :], scalar1=-1.0, scalar2=None, op0=ALU.mult),
                   [r_negA], [r_negA])
            sqv = conv_sq[0].bitcast(BF16); rsq = conv_sq[1]
            rsv, rrs = conv_rs
            for st in range(6):
                wt, rw = ws.stage(8, [(0, 512, Wm[:, st * 512:(st + 1) * 512])])
                if S.dry:
                    continue
                for cc in range(4):
                    c = st * 4 + cc
                    if c >= 18:
                        for (c0, n) in coltiles(p):
                            pt, pr = ps_next()
                            mm_group(pt[:, 0:n], pr, [(wt[:, kc, cc * 128:(cc + 1) * 128], xbf[:, kc, c0:c0 + n])
                                                     for kc in range(8)], [r_xbf, rw])
                            S.emit("act", lambda e, pt=pt, c=c, c0=c0, n=n: e.activation(
                                out=qkvz[:, c, c0:c0 + n], in_=pt[:, 0:n], func=AF.Silu), [pr], [r_qkvz])
                        continue
                    rawv, rraw = conv_raw[c % 2]
                    rawsv, rraws = conv_raws[c % 2]
                    accv, racc = conv_acc[c % 2]
                    S.emit("dve", lambda e, c=c: e.tensor_copy(out=rawv[:, 0:3], in_=gc_carry[:, j, c, :]), [r_gcc], [rraw])
                    if p == 1:
                        S.dma("sp", rawsv[:, 0:3 * NB].rearrange("p (r b) -> p r b", b=NB),
                              gcT[j, c * 128:(c + 1) * 128, :, :], [], [rraws], d_h[c % 2])
                    for (c0, n) in coltiles(p):
                        pt, pr = ps_next()
                        mm_group(pt[:, 0:n], pr, [(wt[:, kc, cc * 128:(cc + 1) * 128], xbf[:, kc, c0:c0 + n])
                                                 for kc in range(8)], [r_xbf, rw])
                        dst = raw_dst(c0, n, rawv, rawsv)
                        S.emit("act", lambda e, pt=pt, dst=dst, c0=c0, n=n: e.activation(
                            out=dst, in_=ps_src(pt, c0, n), func=AF.Copy), [pr], [rraws if is_s(c0) else rraw])
                    conv_taps(p, accv, racc, rawv, rraw, rawsv, rraws, SP_CONVB + (j * 18 + c) * 4, 4)
                    S.emit("dve", lambda e, c=c: e.tensor_copy(out=gc_carry[:, j, c, :], in_=rawv[:, 3 + NCP - 3:3 + NCP]),
                           [rraw], [r_gcc])
                    if p == 1:
                        S.dma("sp", gcsT[j, c * 128:(c + 1) * 128, :, :],
                              rawsv[:, 4 * NB:7 * NB].rearrange("p (r b) -> p r b", b=NB), [rraws], [], d_oh[c % 2])
                    if c >= 12:
                        S.emit("act", lambda e, c=c: e.activation(out=qkvz[:, c, 0:ncol], in_=accv[:, 0:ncol], func=AF.Silu),
                               [racc], [r_qkvz])
                        continue
                    S.emit("act", lambda e: e.activation(out=accv[:, 0:ncol], in_=accv[:, 0:ncol], func=AF.Silu),
                           [racc], [racc])
                    S.emit("act", lambda e: e.activation(out=sqv[:, 0:ncol], in_=accv[:, 0:ncol], func=AF.Square), [racc], [rsq])
                    ebias = float(np.log(128.0 ** -0.5)) if c < 6 else 0.0
                    for (c0, n) in coltiles(p):
                        pt, pr = ps_next()
                        mm_group(pt[:, 0:n], pr, [(onesb[:, :], sqv[:, c0:c0 + n])], [r_cb, rsq])
                        S.emit("dve", lambda e, pt=pt, n=n: e.tensor_scalar(out=rsv[:, 0:n], in0=pt[:, 0:n], scalar1=RMS_EPS,
                                                                           scalar2=None, op0=ALU.add), [pr], [rrs])
                        S.emit("act", lambda e, n=n: e.activation(out=rsv[:, 0:n], in_=rsv[:, 0:n], func=AF.Ln), [rrs], [rrs])
                        if ebias != 0.0:
                            S.emit("dve", lambda e, n=n: e.tensor_scalar(out=rsv[:, 0:n], in0=rsv[:, 0:n], scalar1=-0.5,
                                                                        scalar2=ebias, op0=ALU.mult, op1=ALU.add),
                                   [rrs], [rrs])
                            S.emit("act", lambda e, n=n: e.activation(out=rsv[:, 0:n], in_=rsv[:, 0:n], func=AF.Exp),
                                   [rrs], [rrs])
                        else:
                            S.emit("act", lambda e, n=n: e.activation(out=rsv[:, 0:n], in_=rsv[:, 0:n], func=AF.Exp,
                                                                      scale=-0.5), [rrs], [rrs])
                        S.emit("dve", lambda e, c=c, c0=c0, n=n: e.tensor_tensor(
                            out=qkvz[:, c, c0:c0 + n], in0=accv[:, c0:c0 + n], in1=rsv[:, 0:n], op=ALU.mult),
                            [racc, rrs], [r_qkvz])
            if not S.dry:
                scr_phase("gdn")
                gdn_scalars(p, 64, lambda n: n * 64)
                Spb_ap, r_Spb = gd["Spb"]
                Spb = Spb_ap.bitcast(BF16).rearrange("p (h e) -> p h e", h=GDN_H)
                S.emit("act", lambda e: e.activation(out=Spb, in_=Sst[:, j, :, :], func=AF.Copy), [r_Sst[j]], [r_Spb])
                nxt = gdn_A(0, 0, 64)
                for n in range(16):
                    cur = nxt
                    if n + 1 < 16 and GDN_PIPE:
                        nxt = gdn_A(n + 1, (n + 1) * 64, 64)
                    gdn_B(j, n, n * 64, 64, Sst[:, j, :, :], r_Sst[j], Spb, r_Spb, cur)
                    if n + 1 < 16 and not GDN_PIPE:
                        nxt = gdn_A(n + 1, (n + 1) * 64, 64)
                if p == 1:
                    gdn_scalars(p, TS, lambda n: NCP + n * TS)

                    def load_S(b):
                        St, rS = gd[f"Ss{b % 2}"]
                        Sv = St.rearrange("p (h e) -> p h e", h=GDN_H)
                        Sb_ap, rSb = gd[f"Ssb{b % 2}"]
                        Sb = Sb_ap.bitcast(BF16).rearrange("p (h e) -> p h e", h=GDN_H)
                        S.dma("sp", Sv, gs[j, b], [], [rS], d_ss[b % 2])
                        S.emit("act", lambda e: e.activation(out=Sb, in_=Sv, func=AF.Copy), [rS], [rSb])
                        return Sv, rS, Sb, rSb

                    nxt = gdn_A(0, NCP, TS)
                    nS = load_S(0)
                    for b in range(NB):
                        cur, cS = nxt, nS
                        if b + 1 < NB and GDN_PIPE:
                            nxt = gdn_A(b + 1, NCP + (b + 1) * TS, TS)
                            nS = load_S(b + 1)
                        gdn_B(j, b, NCP + b * TS, TS, cS[0], cS[1], cS[2], cS[3], cur)
                        S.dma("sp", gss[j, b], cS[0], [cS[1]], [], d_so[b % 2])
                        if b + 1 < NB and not GDN_PIPE:
                            nxt = gdn_A(b + 1, NCP + (b + 1) * TS, TS)
                            nS = load_S(b + 1)
            out_proj(w_out_b[j], p)

        def gdn_scalars(p, C, colfn):
            pt, pr = ps_next()
            items = []
            for n in range(16):
                col = colfn(n)
                items.append((pt[0:C, n * 12:(n + 1) * 12], [(xbf[:, kc, col:col + C], wba[:, kc, :]) for kc in range(8)]))
            mm_multi(items, [pr], [r_xbf, r_wba])
            pv = pt[0:C, 0:192].rearrange("p (n k) -> p n k", n=16)
            b3 = g_beta[0:C, :].rearrange("p (n h) -> p n h", n=16)
            t3 = g_tmp[0:C, :].rearrange("p (n h) -> p n h", n=16)
            S.emit("act", lambda e: e.activation(out=b3, in_=pv[:, :, 0:6], func=AF.Exp, scale=-1.0), [pr], [r_gsc])
            S.emit("dve", lambda e: e.tensor_scalar(out=g_beta[0:C, :], in0=g_beta[0:C, :], scalar1=1.0, scalar2=None,
                                                    op0=ALU.add), [r_gsc], [r_gsc])
            S.emit("dve", lambda e: e.reciprocal(out=g_beta[0:C, :], in_=g_beta[0:C, :]), [r_gsc], [r_gsc])
            jdt = spv(SP_DTB + cur_j[0] * 6, 6)
            S.emit("dve", lambda e: e.tensor_tensor(out=t3, in0=pv[:, :, 6:12],
                                                    in1=jdt[0:C, :].unsqueeze(1).to_broadcast([C, 16, 6]), op=ALU.add),
                   [pr, r_sp], [r_gsc])
            S.emit("act", lambda e: e.activation(out=g_tmp[0:C, :], in_=g_tmp[0:C, :], func=AF.Exp), [r_gsc], [r_gsc])
            S.emit("act", lambda e: e.activation(out=g_tmp[0:C, :], in_=g_tmp[0:C, :], func=AF.Ln, bias=1.0), [r_gsc], [r_gsc])
            g3 = g_g[0:C, :].rearrange("p (n h) -> p n h", n=16)
            S.emit("dve", lambda e: e.tensor_tensor(out=g3, in0=t3, in1=negA[0:C, :].unsqueeze(1).to_broadcast([C, 16, 6]),
                                                    op=ALU.mult), [r_gsc, r_negA], [r_gsc])
            pg, prg = ps_next()
            mm_multi([(pg[0:C, 0:96], [(tri[0:C, 0:C], g_g[0:C, :])]),
                      (pg[0:C, 96:192], [(ones[0:C, 0:C], g_g[0:C, :])]),
                      (pg[:, 192:288], [(ones[0:C, 0:128], g_g[0:C, :])])], [prg], [r_cst, r_gsc])
            S.emit("act", lambda e: e.activation(out=g_gc[0:C, :], in_=pg[0:C, 0:96], func=AF.Copy), [prg], [r_gsc])
            S.emit("act", lambda e: e.activation(out=g_egc[0:C, :], in_=pg[0:C, 0:96], func=AF.Exp), [prg], [r_gsc])
            S.emit("dve", lambda e: e.tensor_scalar(out=g_negegc[0:C, :], in0=g_egc[0:C, :], scalar1=-1.0, scalar2=None,
                                                    op0=ALU.mult), [r_gsc], [r_gsc])
            S.emit("dve", lambda e: e.tensor_tensor(out=g_kds[0:C, :], in0=pg[0:C, 96:192], in1=g_gc[0:C, :], op=ALU.subtract),
                   [prg, r_gsc], [r_gsc])
            S.emit("act", lambda e: e.activation(out=g_kds[0:C, :], in_=g_kds[0:C, :], func=AF.Exp), [r_gsc], [r_gsc])
            S.emit("act", lambda e: e.activation(out=g_gam[:, :], in_=pg[:, 192:288], func=AF.Exp), [prg], [r_gsc])

        cur_j = [0]

        def gdn_A(n, col0, C):
            H = GDN_H
            HC = H * C
            par = n % 2

            def t3(nm, inner, dt=F32):
                ap, r = gd[nm]
                if dt == BF16:
                    ap = ap.bitcast(BF16)
                return ap[0:C, 0:H * inner].rearrange("p (h x) -> p h x", h=H), r

            def kT(h):
                return qkvz[:, 6 + h, col0:col0 + C]

            def qT(h):
                return qkvz[:, h, col0:col0 + C]

            kdec, r_kdec = t3(f"kdec{par}", 128, BF16)
            vtm, r_vtm = t3(f"vtm{par}", 128, BF16)
            pb_, prb = psbf
            pbv = pb_[0:C, 0:768].rearrange("p (h x) -> p h x", h=H)
            tr_multi([(pb_[0:C, h * 128:(h + 1) * 128], kT(h), identb[:, :]) for h in range(H)], [prb], [r_qkvz, r_cb])
            S.emit("dve", lambda e: e.tensor_tensor(
                out=kdec, in0=pbv, in1=g_kds[0:C, n * 6:n * 6 + 6].unsqueeze(2).to_broadcast([C, H, 128]), op=ALU.mult),
                [prb, r_gsc], [r_kdec])
            tr_multi([(pb_[0:C, h * 128:(h + 1) * 128], qkvz[:, 12 + h, col0:col0 + C], identb[:, :]) for h in range(H)],
                     [prb], [r_qkvz, r_cb])
            S.emit("act", lambda e: e.activation(out=vtm, in_=pbv, func=AF.Copy), [prb], [r_vtm])
            pK, prK = ps_next()
            mm_multi([(pK[0:C, h * C:(h + 1) * C], [(kT(h), kT(h))]) for h in range(H)], [prK], [r_qkvz])
            pQ, prQ = ps_next()
            mm_multi([(pQ[0:C, h * C:(h + 1) * C], [(kT(h), qT(h))]) for h in range(H)], [prQ], [r_qkvz])
            gtri, r_gtri = t3("gtri", C)
            S.emit("dve", lambda e: e.tensor_tensor(
                out=gtri, in0=tri[0:C, 0:C].unsqueeze(1).to_broadcast([C, H, C]),
                in1=g_g[0:C, n * 6:n * 6 + 6].unsqueeze(2).to_broadcast([C, H, C]), op=ALU.mult), [r_cst, r_gsc], [r_gtri])
            pD, prD = ps_next()
            m6 = mask6[0:C, :, 0:C]
            if C == 64:
                m6f = mask6[0:C, :, :].rearrange("p h x -> p (h x)")
                mm_group(pD[0:C, 0:HC], prD, [(ones[0:C, 0:C], gd["gtri"][0][0:C, 0:HC]), (ident[0:C, 0:C], m6f)],
                         [r_cst, r_gtri, r_mask6])
            else:
                mm_group(pD[0:C, 0:HC], prD, [(ones[0:C, 0:C], gd["gtri"][0][0:C, 0:HC])], [r_cst, r_gtri])
            dT, r_dT = t3("dT", C)
            pD3 = pD[0:C, 0:HC].rearrange("p (h x) -> p h x", h=H)
            S.emit("dve", lambda e: e.tensor_tensor(
                out=dT, in0=pD3, in1=g_gc[0:C, n * 6:n * 6 + 6].unsqueeze(2).to_broadcast([C, H, C]), op=ALU.subtract),
                [prD, r_gsc], [r_dT])
            if C != 64:
                S.emit("dve", lambda e: e.tensor_tensor(out=dT, in0=dT, in1=m6, op=ALU.add), [r_dT, r_mask6], [r_dT])
            S.emit("act", lambda e: e.activation(out=dT, in_=dT, func=AF.Exp), [r_dT], [r_dT])
            bsm = gtri
            S.emit("dve", lambda e: e.tensor_tensor(
                out=bsm, in0=strU[0:C, 0:C].unsqueeze(1).to_broadcast([C, H, C]),
                in1=g_beta[0:C, n * 6:n * 6 + 6].unsqueeze(2).to_broadcast([C, H, C]), op=ALU.mult),
                [r_cst, r_gsc, r_gtri], [r_gtri])
            U, r_U = t3("U", C, BF16)
            pK3 = pK[0:C, 0:HC].rearrange("p (h x) -> p h x", h=H)
            pQ3 = pQ[0:C, 0:HC].rearrange("p (h x) -> p h x", h=H)
            S.emit("dve", lambda e: e.tensor_tensor(out=bsm, in0=bsm, in1=dT, op=ALU.mult), [r_dT, r_gtri], [r_gtri])
            S.emit("dve", lambda e: e.tensor_tensor(out=U, in0=pK3, in1=bsm, op=ALU.mult), [prK, r_gtri], [r_U])
            qkT, r_qkT = t3(f"qkT{par}", C, BF16)
            S.emit("dve", lambda e: e.tensor_tensor(out=qkT, in0=pQ3, in1=dT, op=ALU.mult), [prQ, r_dT], [r_qkT])
            idC = ident[0:C, 0:C]
            idCb = identb[0:C, 0:C]
            PT, r_PT = t3("PTa", C, BF16)
            pT_, prT = ps_next()
            mm_multi([(pT_[0:C, h * C:(h + 1) * C], [(U[:, h, :], idCb)]) for h in range(H)], [prT], [r_U, r_cb])
            S.emit("act", lambda e: e.activation(out=PT, in_=pT_[0:C, 0:HC].rearrange("p (h x) -> p h x", h=H), func=AF.Copy),
                   [prT], [r_PT])
            Vc, r_Vc = t3("Va", C, BF16)
            S.emit("dve", lambda e: e.tensor_tensor(out=Vc, in0=idC.unsqueeze(1).to_broadcast([C, H, C]), in1=U,
                                                     op=ALU.subtract), [r_cst, r_U], [r_Vc])
            P, r_P = U, r_U
            nlev = {64: 5, 4: 1}[C]
            pnames = [("Pa", "PTb"), ("Pb", "PTa")]
            vnames = ["Vb", "Va"]
            for lev in range(nlev):
                last = lev == nlev - 1
                nP, nPT = pnames[lev % 2]
                PTn, r_PTn = t3(nPT, C, BF16)
                pA, prA = ps_next()
                mm_multi([(pA[0:C, h * C:(h + 1) * C], [(P[:, h, :], PT[:, h, :])]) for h in range(H)], [prA], [r_P, r_PT])
                S.emit("act", lambda e, pA=pA, PTn=PTn: e.activation(
                    out=PTn, in_=pA[0:C, 0:HC].rearrange("p (h x) -> p h x", h=H), func=AF.Copy), [prA], [r_PTn])
                if not last:
                    Pn, r_Pn = t3(nP, C, BF16)
                    pB, prB = ps_next()
                    mm_multi([(pB[0:C, h * C:(h + 1) * C], [(PT[:, h, :], P[:, h, :])]) for h in range(H)], [prB], [r_P, r_PT])
                    S.emit("act", lambda e, pB=pB, Pn=Pn: e.activation(
                        out=Pn, in_=pB[0:C, 0:HC].rearrange("p (h x) -> p h x", h=H), func=AF.Copy), [prB], [r_Pn])
                Vn, r_Vn = t3(f"Vf{par}" if last else vnames[lev % 2], C, BF16)
                pV, prV = ps_next()
                mm_multi([(pV[0:C, h * C:(h + 1) * C], [(PTn[:, h, :], Vc[:, h, :])]) for h in range(H)], [prV], [r_PTn, r_Vc])
                S.emit("dve", lambda e, pV=pV, Vn=Vn, Vc=Vc: e.tensor_tensor(
                    out=Vn, in0=pV[0:C, 0:HC].rearrange("p (h x) -> p h x", h=H), in1=Vc, op=ALU.add), [prV, r_Vc], [r_Vn])
                Vc, r_Vc = Vn, r_Vn
                PT, r_PT = PTn, r_PTn
                if not last:
                    P, r_P = Pn, r_Pn
            return dict(V=Vc, r_V=r_Vc, qkT=qkT, r_qkT=r_qkT, kdec=kdec, r_kdec=r_kdec, vtm=vtm, r_vtm=r_vtm)

        def gdn_B(j, n, col0, C, Sv, rS, Sb, rSb, A):
            H = GDN_H
            HC = H * C

            def t3(nm, inner, dt=F32):
                ap, r = gd[nm]
                if dt == BF16:
                    ap = ap.bitcast(BF16)
                return ap[0:C, 0:H * inner].rearrange("p (h x) -> p h x", h=H), r

            def bc(t, hs):
                return t[0:C, n * 6 + hs:n * 6 + hs + 3].unsqueeze(2).to_broadcast([C, 3, 128])

            def kT(h):
                return qkvz[:, 6 + h, col0:col0 + C]

            def qT(h):
                return qkvz[:, h, col0:col0 + C]

            Vc, r_Vc, qkT, r_qkT = A["V"], A["r_V"], A["qkT"], A["r_qkT"]
            kdec, r_kdec, vtm, r_vtm = A["kdec"], A["r_kdec"], A["vtm"], A["r_vtm"]
            Y, r_Y = t3("Y", 128, BF16)
            vnew, r_vn = t3("vnew", 128, BF16)
            ot, r_ot = t3("ot", 128)
            sq, r_sq = t3("sq", 128)
            idC = ident[0:C, 0:C]
            for hg in range(2):
                hs = 3 * hg
                pk, prk = ps_next()
                mm_multi([(pk[0:C, hh * 128:(hh + 1) * 128], [(kT(hs + hh), Sb[:, hs + hh, :])]) for hh in range(3)],
                         [prk], [r_qkvz, rSb])
                pq, prq = ps_next()
                mm_multi([(pq[0:C, hh * 128:(hh + 1) * 128], [(qT(hs + hh), Sb[:, hs + hh, :])]) for hh in range(3)],
                         [prq], [r_qkvz, rSb])
                pk3 = pk[0:C, 0:384].rearrange("p (h x) -> p h x", h=3)
                pq3 = pq[0:C, 0:384].rearrange("p (h x) -> p h x", h=3)
                S.emit("dve", lambda e, pk3=pk3, hs=hs: e.tensor_tensor(out=sq[:, hs:hs + 3, :], in0=pk3,
                                                                         in1=bc(g_negegc, hs), op=ALU.mult),
                       [prk, r_gsc], [r_sq])
                S.emit("dve", lambda e, hs=hs: e.tensor_tensor(out=Y[:, hs:hs + 3, :], in0=sq[:, hs:hs + 3, :],
                                                                in1=vtm[:, hs:hs + 3, :], op=ALU.add), [r_sq, r_vtm], [r_Y])
                px, prx = ps_next()
                mm_multi([(px[0:C, hh * 128:(hh + 1) * 128], [(Vc[:, hs + hh, :], Y[:, hs + hh, :])]) for hh in range(3)],
                         [prx], [r_Vc, r_Y])
                px3 = px[0:C, 0:384].rearrange("p (h x) -> p h x", h=3)
                S.emit("dve", lambda e, px3=px3, hs=hs: e.tensor_tensor(out=vnew[:, hs:hs + 3, :], in0=px3,
                                                                         in1=bc(g_beta, hs), op=ALU.mult),
                       [prx, r_gsc], [r_vn])
                po, pro = ps_next()
                mm_multi([(po[0:C, hh * 128:(hh + 1) * 128], [(qkT[:, hs + hh, :], vnew[:, hs + hh, :])]) for hh in range(3)],
                         [pro], [r_qkT, r_vn])
                psn, prs = ps_next()
                mm_multi([(psn[:, hh * 128:(hh + 1) * 128], [(kdec[:, hs + hh, :], vnew[:, hs + hh, :])]) for hh in range(3)],
                         [prs], [r_kdec, r_vn])
                ps3 = psn[:, 0:384].rearrange("p (h x) -> p h x", h=3)
                gb = g_gam[:, n * 6 + hs:n * 6 + hs + 3].unsqueeze(2).to_broadcast([128, 3, 128])
                S.emit("dve", lambda e, hs=hs, gb=gb: e.tensor_tensor(out=Sv[:, hs:hs + 3, :], in0=Sv[:, hs:hs + 3, :], in1=gb,
                                                                       op=ALU.mult), [rS, r_gsc], [rS])
                S.emit("dve", lambda e, hs=hs, ps3=ps3: e.tensor_tensor(out=Sv[:, hs:hs + 3, :], in0=Sv[:, hs:hs + 3, :],
                                                                         in1=ps3, op=ALU.add), [rS, prs], [rS])
                S.emit("act", lambda e, hs=hs: e.activation(out=Sb[:, hs:hs + 3, :], in_=Sv[:, hs:hs + 3, :], func=AF.Copy),
                       [rS], [rSb])
                po3 = po[0:C, 0:384].rearrange("p (h x) -> p h x", h=3)
                S.emit("dve", lambda e, pq3=pq3, hs=hs: e.tensor_tensor(out=ot[:, hs:hs + 3, :], in0=pq3,
                                                                         in1=bc(g_egc, hs), op=ALU.mult),
                       [prq, r_gsc], [r_ot])
                S.emit("dve", lambda e, po3=po3, hs=hs: e.tensor_tensor(out=ot[:, hs:hs + 3, :], in0=ot[:, hs:hs + 3, :],
                                                                         in1=po3, op=ALU.add), [pro, r_ot], [r_ot])
            ssq_ap, r_ssq = gd["ssq"]
            ssq = ssq_ap[0:C, 0:H]
            S.emit("dve", lambda e: e.tensor_tensor(out=sq, in0=ot, in1=ot, op=ALU.mult), [r_ot, r_sq], [r_sq])
            S.emit("dve", lambda e: e.tensor_reduce(out=ssq, in_=sq, axis=AX.X, op=ALU.add), [r_sq], [r_ssq])
            S.emit("dve", lambda e: e.tensor_scalar(out=ssq, in0=ssq, scalar1=1.0 / 128.0, scalar2=RMS_EPS, op0=ALU.mult,
                                                    op1=ALU.add), [r_ssq], [r_ssq])
            S.emit("act", lambda e: e.activation(out=ssq, in_=ssq, func=AF.Ln), [r_ssq], [r_ssq])
            S.emit("act", lambda e: e.activation(out=ssq, in_=ssq, func=AF.Exp, scale=-0.5), [r_ssq], [r_ssq])
            S.emit("dve", lambda e: e.tensor_tensor(out=ot, in0=ot, in1=ssq.unsqueeze(2).to_broadcast([C, H, 128]),
                                                     op=ALU.mult), [r_ot, r_ssq], [r_ot])
            pz, prz = ps_next()
            tr_multi([(pz[:, h * C:(h + 1) * C], ot[:, h, :], idC) for h in range(H)], [prz], [r_ot, r_cst])
            pz3 = pz[:, 0:HC].rearrange("p (h x) -> p h x", h=H)
            S.emit("dve", lambda e: e.scalar_tensor_tensor(
                out=obuf[:, 0:6, col0:col0 + C], in0=pz3, scalar=spv(SP_NORMW + j), in1=qkvz[:, 18:24, col0:col0 + C],
                op0=ALU.mult, op1=ALU.mult), [prz, r_sp, r_qkvz], [r_obuf])

        def layernorm(goff, boff, p):
            if S.dry:
                return
            biga_phase("ln")
            for (c0, n) in coltiles(p):
                S.emit("act", lambda e, c0=c0, n=n: e.activation(out=ln_zb[:, :, 0:n], in_=xres[:, :, c0:c0 + n], func=AF.Copy),
                       [r_xres], [r_ln])
                S.emit("act", lambda e, c0=c0, n=n: e.activation(out=ln_sqb[:, :, 0:n], in_=xres[:, :, c0:c0 + n],
                                                                func=AF.Square), [r_xres], [r_ln])
                pm, prm = ps_next()
                mm_group(pm[:, 0:n], prm, [(meanb[:, :], ln_zb[:, kc, 0:n]) for kc in range(8)], [r_cb, r_ln])
                pq, prq = ps_next()
                mm_group(pq[:, 0:n], prq, [(meanb[:, :], ln_sqb[:, kc, 0:n]) for kc in range(8)], [r_cb, r_ln])
                S.emit("act", lambda e, pm=pm, n=n: e.activation(out=ln_mean[:, 0:n], in_=pm[:, 0:n], func=AF.Copy), [prm], [r_ln])
                S.emit("dve", lambda e, n=n: e.tensor_tensor(out=ln_rstd[:, 0:n], in0=ln_mean[:, 0:n], in1=ln_mean[:, 0:n],
                                                             op=ALU.mult), [r_ln], [r_ln])
                S.emit("dve", lambda e, pq=pq, n=n: e.tensor_tensor(out=ln_rstd[:, 0:n], in0=pq[:, 0:n], in1=ln_rstd[:, 0:n],
                                                                    op=ALU.subtract), [prq, r_ln], [r_ln])
                S.emit("dve", lambda e, n=n: e.tensor_scalar(out=ln_rstd[:, 0:n], in0=ln_rstd[:, 0:n], scalar1=LN_EPS,
                                                             scalar2=None, op0=ALU.add), [r_ln], [r_ln])
                S.emit("act", lambda e, n=n: e.activation(out=ln_rstd[:, 0:n], in_=ln_rstd[:, 0:n], func=AF.Ln), [r_ln], [r_ln])
                S.emit("act", lambda e, n=n: e.activation(out=ln_rstd[:, 0:n], in_=ln_rstd[:, 0:n], func=AF.Exp, scale=-0.5),
                       [r_ln], [r_ln])
                S.emit("dve", lambda e, c0=c0, n=n: e.tensor_tensor(
                    out=ln_t1[:, :, 0:n], in0=xres[:, :, c0:c0 + n],
                    in1=ln_mean[:, 0:n].unsqueeze(1).to_broadcast([128, 8, n]), op=ALU.subtract), [r_xres, r_ln], [r_ln])
                S.emit("dve", lambda e, n=n: e.tensor_tensor(
                    out=ln_t1[:, :, 0:n], in0=ln_t1[:, :, 0:n],
                    in1=ln_rstd[:, 0:n].unsqueeze(1).to_broadcast([128, 8, n]), op=ALU.mult), [r_ln], [r_ln])
                for kc in range(8):
                    S.emit("act", lambda e, kc=kc, c0=c0, n=n: e.activation(
                        out=xres[:, kc, c0:c0 + n], in_=ln_t1[:, kc, 0:n], func=AF.Identity,
                        scale=spv(goff + kc), bias=spv(boff + kc)), [r_ln, r_sp], [r_xres])
                S.emit("act", lambda e, c0=c0, n=n: e.activation(out=xbf[:, :, c0:c0 + n], in_=xres[:, :, c0:c0 + n], func=AF.Copy),
                       [r_xres], [r_xbf])

        def ffn(l, p):
            scr_phase("conv")
            biga_phase("act")
            Wu = w_up[l]
            ncol = NCP + (NSC if p == 1 else 0)
            for i in range(22):
                wt, rw = ws.stage(8, [(0, 128, Wu[:, i * 128:(i + 1) * 128]),
                                      (128, 128, Wu[:, D_FF + i * 128:D_FF + (i + 1) * 128])])
                if S.dry:
                    continue
                for half in range(2):
                    cidx = half * 22 + i
                    rawv, rraw = conv_raw[half]
                    rawsv, rraws = conv_raws[half]
                    accv, racc = conv_acc[half]
                    S.emit("dve", lambda e, cidx=cidx: e.tensor_copy(out=rawv[:, 1:3], in_=ff_carry[:, l, cidx, :]),
                           [r_ffc], [rraw])
                    if p == 1:
                        S.dma("sp", rawsv[:, NB:3 * NB].rearrange("p (r b) -> p r b", b=NB),
                              ffT[l, cidx * 128:(cidx + 1) * 128, :, :], [], [rraws], d_h[half])
                    for (c0, n) in coltiles(p):
                        pt, pr = ps_next()
                        mm_group(pt[:, 0:n], pr, [(wt[:, kc, half * 128:(half + 1) * 128], xbf[:, kc, c0:c0 + n])
                                                 for kc in range(8)], [r_xbf, rw])
                        dst = raw_dst(c0, n, rawv, rawsv)
                        S.emit("act", lambda e, pt=pt, dst=dst, c0=c0, n=n: e.activation(
                            out=dst, in_=ps_src(pt, c0, n), func=AF.Copy), [pr], [rraws if is_s(c0) else rraw])
                    conv_taps(p, accv, racc, rawv, rraw, rawsv, rraws, SP_CONVF + (l * 44 + cidx) * 3, 3)
                    S.emit("dve", lambda e, cidx=cidx: e.tensor_copy(out=ff_carry[:, l, cidx, :],
                                                                     in_=rawv[:, 3 + NCP - 2:3 + NCP]), [rraw], [r_ffc])
                    if p == 1:
                        S.dma("sp", ffsT[l, cidx * 128:(cidx + 1) * 128, :, :],
                              rawsv[:, 5 * NB:7 * NB].rearrange("p (r b) -> p r b", b=NB), [rraws], [], d_oh[half])
                ag, rag = conv_acc[0]
                au, rau = conv_acc[1]
                S.emit("act", lambda e: e.activation(out=ag[:, 0:ncol], in_=ag[:, 0:ncol], func=AF.Silu), [rag], [rag])
                S.emit("dve", lambda e, i=i: e.tensor_tensor(out=actb[:, i, 0:ncol], in0=ag[:, 0:ncol], in1=au[:, 0:ncol],
                                                              op=ALU.mult), [rag, rau], [r_actb])
            Wd = w_down[l]
            for oc in range(8):
                wt, rw = ws.stage(22, [(0, 128, Wd[:, oc * 128:(oc + 1) * 128])])
                if S.dry:
                    continue
                for (c0, n) in coltiles(p):
                    pt, pr = ps_next()
                    mm_group(pt[:, 0:n], pr, [(wt[:, kc, :], actb[:, kc, c0:c0 + n]) for kc in range(22)], [r_actb, rw])
                    S.emit("dve", lambda e, pt=pt, oc=oc, c0=c0, n=n: e.scalar_tensor_tensor(
                        out=xres[:, oc, c0:c0 + n], in0=xres[:, oc, c0:c0 + n], scalar=ALPHA, in1=pt[:, 0:n],
                        op0=ALU.mult, op1=ALU.add), [pr, r_xres], [r_xres])

        _mixer_b = mixer_b

        def mixer_b(l, p):
            cur_j[0] = l // 2
            _mixer_b(l, p)

        S.dry = True
        ws.planning = True
        program()
        S.dry = False
        ws.planning = False
        program()
        build.stats = dict(ninst=S.ninst, nstages=len(ws.plan), counts={k: v["cnt"] for k, v in S.eng.items()})
    return nc


def _consts():
    c = np.zeros((128, NCST), np.float32)
    c[:, C_ID:C_ID + 128] = np.eye(128, dtype=np.float32)
    c[:, C_ONE:C_ONE + 128] = 1.0
    k = np.arange(64)
    c[0:64, C_TRI:C_TRI + 64] = (k[:, None] <= k[None, :]).astype(np.float32)
    c[0:64, C_MASK:C_MASK + 64] = np.where(k[:, None] <= k[None, :], 0.0, NEG).astype(np.float32)
    c[0:64, C_STR:C_STR + 64] = (k[:, None] < k[None, :]).astype(np.float32)
    return c


def _small_params(conv_a, conv_b, w_conv_ffn, ln1_g, ln1_b, ln2_g, ln2_b, gdn_norm_w, a_log, dt_bias):
    sp = np.zeros((128, NSP), np.float32)

    def fm(w, nchunk):
        L, J, F = w.shape
        return np.ascontiguousarray(w.reshape(L, J, nchunk, 128).transpose(3, 0, 2, 1)).reshape(128, -1)

    sp[:, SP_CONVA:SP_CONVA + 36] = fm(conv_a, 6)
    sp[:, SP_CONVB:SP_CONVB + 144] = fm(conv_b, 18)
    sp[:, SP_CONVF:SP_CONVF + 528] = fm(w_conv_ffn, 44)
    for off, a in ((SP_LN1G, ln1_g), (SP_LN1B, ln1_b), (SP_LN2G, ln2_g), (SP_LN2B, ln2_b)):
        sp[:, off:off + 32] = a.reshape(DEPTH, 8, 128).transpose(2, 0, 1).reshape(128, 32)
    sp[:, SP_NORMW:SP_NORMW + 2] = gdn_norm_w.T
    sp[:, SP_ALOG:SP_ALOG + 12] = a_log.reshape(1, 12)
    sp[:, SP_DTB:SP_DTB + 12] = dt_bias.reshape(1, 12)
    return sp


def make_in_maps(inp, cores):
    f = lambda a: np.ascontiguousarray(np.asarray(a, dtype=np.float32))
    sp = _small_params(f(inp["conv_a"]), f(inp["conv_b"]), f(inp["w_conv_ffn"]), f(inp["ln1_g"]), f(inp["ln1_b"]),
                       f(inp["ln2_g"]), f(inp["ln2_b"]), f(inp["gdn_norm_w"]), f(inp["a_log"]), f(inp["dt_bias"]))
    cst = _consts()
    shared = {k: f(inp[k]) for k in ("w_in_a", "w_out_a", "w_in_b", "w_out_b", "w_mem_kv", "w_up", "w_down")}
    wbaT = f(np.asarray(inp["w_in_b"])[:, :, 3072:3084].reshape(2, 8, 128, 12).transpose(0, 2, 1, 3))
    maps = []
    for c in cores:
        b0, b1 = c * NB, (c + 1) * NB
        m = dict(shared)
        m["xpT"] = f(np.asarray(inp["x_prompt"][c]).T)
        m["xsT"] = f(np.asarray(inp["x_sample"][b0:b1]).reshape(NSC, D).T)
        m["memT"] = f(np.asarray(inp["mem_prompt"][c]).T)
        ck = np.asarray(inp["cache_mem_k"][:, b0:b1]).reshape(DEPTH, NB, NMEM, 256)
        m["ckT"] = f(ck.transpose(0, 3, 1, 2))
        m["cv"] = f(np.asarray(inp["cache_mem_v"][:, b0:b1]).reshape(DEPTH, NB, NMEM, 256))
        m["scT"] = f(np.asarray(inp["state_shortconv"][:, b0:b1]).transpose(0, 3, 2, 1))
        m["gcT"] = f(np.asarray(inp["state_gdn_conv"][:, b0:b1]).transpose(0, 3, 2, 1))
        m["gs"] = f(np.asarray(inp["state_gdn"][:, b0:b1]).transpose(0, 1, 3, 2, 4))
        m["wbaT"] = wbaT
        m["ffT"] = f(np.asarray(inp["state_ffn_conv"][:, b0:b1]).transpose(0, 3, 2, 1))
        m["spd"] = sp
        m["cstd"] = cst
        maps.append(m)
    return maps


def assemble(results, ncores):
    B = ncores
    y_p = np.stack([r["ypT"].T for r in results])
    y_s = np.concatenate([r["ysT"].T.reshape(NB, TS, D) for r in results])
    mk_ = np.stack([r["mk"] for r in results], axis=1).reshape(DEPTH, B, NMEM, 4, 64)
    mv_ = np.stack([r["mv"] for r in results], axis=1).reshape(DEPTH, B, NMEM, 4, 64)
    sc_p = np.stack([r["scpT"].transpose(0, 3, 2, 1).reshape(2, 2, SC_DIM) for r in results], axis=1)
    gc_p = np.stack([r["gcpT"].transpose(0, 3, 2, 1).reshape(2, 3, 2304) for r in results], axis=1)
    gs_p = np.stack([r["gsp"].transpose(0, 2, 1, 3) for r in results], axis=1)
    ff_p = np.stack([r["ffpT"].transpose(0, 3, 2, 1).reshape(DEPTH, 2, 2 * D_FF) for r in results], axis=1)
    sc_s = np.concatenate([r["scsT"].transpose(0, 3, 2, 1) for r in results], axis=1)
    gc_s = np.concatenate([r["gcsT"].transpose(0, 3, 2, 1) for r in results], axis=1)
    gs_s = np.concatenate([r["gss"].transpose(0, 1, 3, 2, 4) for r in results], axis=1)
    ff_s = np.concatenate([r["ffsT"].transpose(0, 3, 2, 1) for r in results], axis=1)
    outs = (y_p, y_s, mk_, mv_, sc_p, gc_p, gs_p, ff_p, sc_s, gc_s, gs_s, ff_s)
    return tuple(np.ascontiguousarray(o, dtype=np.float32) for o in outs)


def kernel(**inputs):
    nc = build()
    maps = make_in_maps(inputs, list(range(8)))
    res = run_bass_kernel_spmd(nc, maps, core_ids=list(range(8)))
    return assemble(res.results, 8)
```

```python
import contextlib
import os
import numpy as np
import concourse.bass as bass
import concourse.mybir as mybir
from concourse.bass_utils import run_bass_kernel_spmd

F32 = mybir.dt.float32
BF16 = mybir.dt.bfloat16
AF = mybir.ActivationFunctionType
ALU = mybir.AluOpType
AX = mybir.AxisListType

D = 1024
SEQ = 2048
NCP = 1024
NSC = 64
W = NCP + NSC
NB = 16
TS = 4
DEPTH = 4
SC_DIM = 768
GDN_H = 6
D_FF = 2816
NMEM = 256
ALPHA = (2.0 * DEPTH) ** 0.25
LN_EPS = 1e-5
RMS_EPS = 1e-6
NEG = -30000.0

SP_CONVA = 0
SP_CONVB = SP_CONVA + 36
SP_CONVF = SP_CONVB + 144
SP_LN1G = SP_CONVF + 528
SP_LN1B = SP_LN1G + 32
SP_LN2G = SP_LN1B + 32
SP_LN2B = SP_LN2G + 32
SP_NORMW = SP_LN2B + 32
SP_ALOG = SP_NORMW + 2
SP_DTB = SP_ALOG + 12
NSP = SP_DTB + 12
C_ID = 0
C_ONE = 128
C_TRI = 256
C_MASK = 320
C_STR = 384
NCST = 448


class Res:
    __slots__ = ("name", "lw", "rd", "excl")

    def __init__(self, name, excl=False):
        self.name = name
        self.lw = None
        self.rd = {}
        self.excl = excl


class DSem:
    def __init__(self, name, sem):
        self.name = name
        self.sem = sem
        self.cnt = 0


def fence(new, olds):
    for n in new:
        for o in olds:
            if o.lw is not None and n.rd.get(o.lw[0], 0) < o.lw[1]:
                n.rd[o.lw[0]] = o.lw[1]
            for s, v in o.rd.items():
                if n.rd.get(s, 0) < v:
                    n.rd[s] = v


class Sched:
    def __init__(self, nc, es):
        self.nc = nc
        self.es = es
        self.eng = {}
        self.sems = {}
        self.dry = False
        for name, h in [("pe", nc.tensor), ("act", nc.scalar), ("dve", nc.vector),
                        ("pool", nc.gpsimd), ("sp", nc.sync)]:
            sem = es.enter_context(nc.semaphore("sem_" + name))
            self.eng[name] = dict(h=h, sem=sem, cnt=0, waited={})
            self.sems[name] = sem
        self.ndsem = 0
        self.dsems = {}
        self.ninst = 0

    def dsem(self):
        name = f"dsem{self.ndsem}"
        self.ndsem += 1
        sem = self.es.enter_context(self.nc.semaphore(name))
        self.sems[name] = sem
        d = DSem(name, sem)
        self.dsems[name] = d
        return d

    def _waits(self, en, reads, writes):
        e = self.eng[en]
        deps = {}
        for r in reads:
            if r.lw is not None and deps.get(r.lw[0], 0) < r.lw[1]:
                deps[r.lw[0]] = r.lw[1]
        for w in writes:
            if w.lw is not None and deps.get(w.lw[0], 0) < w.lw[1]:
                deps[w.lw[0]] = w.lw[1]
            for s, v in w.rd.items():
                if deps.get(s, 0) < v:
                    deps[s] = v
        for s, v in deps.items():
            if en == "pe" and s == "pe":
                continue
            if s in self.dsems:
                v = self.dsems[s].cnt
            if e["waited"].get(s, 0) < v:
                e["h"].wait_ge(self.sems[s], v)
                e["waited"][s] = v

    def emit(self, en, fn, reads=(), writes=()):
        if self.dry:
            return None
        e = self.eng[en]
        if any(r.excl for r in reads):
            writes = list(writes) + [r for r in reads if r.excl]
            reads = [r for r in reads if not r.excl]
        self._waits(en, reads, writes)
        ins = fn(e["h"])
        e["cnt"] += 1
        ins.then_inc(e["sem"], 1)
        c = e["cnt"]
        for r in reads:
            if r.rd.get(en, 0) < c:
                r.rd[en] = c
        for w in writes:
            w.lw = (en, c)
            w.rd = {}
        self.ninst += 1
        return ins

    def dma(self, qn, out, in_, reads, writes, ds):
        if self.dry:
            return None
        e = self.eng[qn]
        self._waits(qn, reads, writes)
        ins = e["h"].dma_start(out=out, in_=in_)
        ds.cnt += 16
        ins.then_inc(ds.sem, 16)
        for r in reads:
            if r.rd.get(ds.name, 0) < ds.cnt:
                r.rd[ds.name] = ds.cnt
        for w in writes:
            w.lw = (ds.name, ds.cnt)
            w.rd = {}
        self.ninst += 1
        return ins


GDN_PIPE = os.environ.get("GDN_PIPE", "1") == "1"
CONV_ACT = os.environ.get("CONV_ACT", "1") == "1"
KPARTS = os.environ.get("KPARTS", "att,att2,mix,out,ln,ffn").split(",")


def build(nlayers=DEPTH, npass=2):
    nc = bass.Bass("TRN2", target_bir_lowering=False)

    def din(name, shape):
        return nc.dram_tensor(name, list(shape), F32, kind="ExternalInput").ap()

    def dout(name, shape):
        return nc.dram_tensor(name, list(shape), F32, kind="ExternalOutput").ap()

    xpT = din("xpT", [D, SEQ])
    xsT = din("xsT", [D, NSC])
    memT = din("memT", [D, NMEM])
    ckT = din("ckT", [DEPTH, 256, NB, NMEM])
    cv = din("cv", [DEPTH, NB, NMEM, 256])
    scT = din("scT", [2, SC_DIM, 2, NB])
    gcT = din("gcT", [2, 2304, 3, NB])
    gs = din("gs", [2, NB, 128, GDN_H, 128])
    wbaT = din("wbaT", [2, 128, 8, 12])
    ffT = din("ffT", [DEPTH, 2 * D_FF, 2, NB])
    w_in_a = din("w_in_a", [2, D, 2560])
    w_out_a = din("w_out_a", [2, D, D])
    w_in_b = din("w_in_b", [2, D, 3340])
    w_out_b = din("w_out_b", [2, D, D])
    w_mem_kv = din("w_mem_kv", [DEPTH, D, 512])
    w_up = din("w_up", [DEPTH, D, 2 * D_FF])
    w_down = din("w_down", [DEPTH, D_FF, D])
    spd = din("spd", [128, NSP])
    cstd = din("cstd", [128, NCST])

    ypT = dout("ypT", [D, SEQ])
    ysT = dout("ysT", [D, NSC])
    mk = dout("mk", [DEPTH, NMEM, 256])
    mv = dout("mv", [DEPTH, NMEM, 256])
    scpT = dout("scpT", [2, 128, 6, 2])
    gcpT = dout("gcpT", [2, 128, 18, 3])
    gsp = dout("gsp", [2, 128, GDN_H, 128])
    ffpT = dout("ffpT", [DEPTH, 128, 44, 2])
    scsT = dout("scsT", [2, SC_DIM, 2, NB])
    gcsT = dout("gcsT", [2, 2304, 3, NB])
    gss = dout("gss", [2, NB, 128, GDN_H, 128])
    ffsT = dout("ffsT", [DEPTH, 2 * D_FF, 2, NB])

    es = contextlib.ExitStack()
    with es:
        S = Sched(nc, es)

        def sb(name, shape, dt=F32):
            return es.enter_context(nc.sbuf_tensor(name, list(shape), dt))

        xres = sb("xres", [128, 8, W]); r_xres = Res("xres")
        xbf = sb("xbf", [128, 8, W], BF16); r_xbf = Res("xbf")
        obuf = sb("obuf", [128, 8, W], BF16); r_obuf = Res("obuf")
        cst = sb("cst", [128, NCST]); r_cst = Res("cst")
        spt = sb("spt", [128, NSP]); r_sp = Res("sp")
        mask6 = sb("mask6", [64, 6, 64]); r_mask6 = Res("mask6")
        identb = sb("identb", [128, 128], BF16)
        onesb = sb("onesb", [128, 128], BF16)
        meanb = sb("meanb", [128, 128], BF16)
        r_cb = Res("constb")
        sc_carry = sb("sc_carry", [128, 2, 6, 2]); r_scc = Res("scc")
        gc_carry = sb("gc_carry", [128, 2, 18, 3]); r_gcc = Res("gcc")
        ff_carry = sb("ff_carry", [128, DEPTH, 44, 2]); r_ffc = Res("ffc")
        Sst = sb("Sst", [128, 2, GDN_H, 128]); r_Sst = [Res("Sst0"), Res("Sst1")]
        wba = sb("wba", [128, 8, 12], BF16); r_wba = Res("wba")
        negA = sb("negA", [128, 6]); r_negA = Res("negA")
        g_beta = sb("g_beta", [64, 96]); g_g = sb("g_g", [64, 96]); g_gc = sb("g_gc", [64, 96])
        g_egc = sb("g_egc", [64, 96]); g_negegc = sb("g_negegc", [64, 96]); g_kds = sb("g_kds", [64, 96])
        g_tmp = sb("g_tmp", [64, 96]); g_gam = sb("g_gam", [128, 96])
        r_gsc = Res("gsc")
        NSLOT = 3
        wslots = [(sb(f"wslot{i}", [128, 4096], BF16), Res(f"wslot{i}"), S.dsem()) for i in range(NSLOT)]
        BIGA = sb("BIGA", [128, 24 * W], BF16)
        SCRN = 10240
        SCR = sb("SCR", [128, SCRN])

        psum = [(es.enter_context(nc.psum_tensor(f"ps{i}", [128, 512], F32)), Res(f"ps{i}", True)) for i in range(7)]
        psbf = (es.enter_context(nc.psum_tensor("psbf", [128, 1024], BF16)), Res("psbf", True))
        psi = [0]

        psg = {"A": [0, (0, 1, 2)], "B": [0, (3, 4, 5, 6)]}

        def ps_next(which=None):
            if which is not None:
                st = psg[which]
                b = psum[st[1][st[0] % len(st[1])]]
                st[0] += 1
                return b
            b = psum[psi[0] % 7]
            psi[0] += 1
            return b

        d_in = S.dsem()
        d_x = S.dsem()
        d_h = [S.dsem(), S.dsem()]
        d_ss = [S.dsem(), S.dsem()]
        d_memb = S.dsem(); d_ckv = S.dsem(); d_wba = S.dsem(); d_kvn = S.dsem()
        d_oh = [S.dsem(), S.dsem()]; d_so = [S.dsem(), S.dsem()]; d_y = S.dsem(); d_fin = S.dsem()
        out_dsems = [d_kvn, d_oh[0], d_oh[1], d_so[0], d_so[1], d_y, d_fin]

        ident = cst[:, C_ID:C_ID + 128]
        ones = cst[:, C_ONE:C_ONE + 128]
        tri = cst[:, C_TRI:C_TRI + 64]
        maskT = cst[:, C_MASK:C_MASK + 64]
        strU = cst[:, C_STR:C_STR + 64]

        def spv(off, n=1):
            return spt[:, off:off + n]

        qkvz = BIGA[:, :].rearrange("p (c w) -> p c w", c=24)
        r_qkvz = Res("qkvz")
        actb = BIGA[:, 0:22 * W].rearrange("p (c w) -> p c w", c=22)
        r_actb = Res("actb")
        ln_zb = BIGA[:, 0:4096].rearrange("p (k n) -> p k n", k=8)
        ln_sqb = BIGA[:, 4096:8192].rearrange("p (k n) -> p k n", k=8)
        ln_f32 = BIGA[:, 8192:8192 + 2 * (4096 + 1024)].bitcast(F32)
        ln_t1 = ln_f32[:, 0:4096].rearrange("p (k n) -> p k n", k=8)
        ln_mean = ln_f32[:, 4096:4608]
        ln_rstd = ln_f32[:, 4608:5120]
        r_ln = Res("ln")
        ckb = BIGA[:, 0:8192].rearrange("p (c b m) -> p c b m", c=2, b=NB)
        cvb = BIGA[:, 8192:16384].rearrange("p (c b m) -> p c b m", c=2, b=NB)
        r_ckv = Res("ckv")
        biga_groups = {"qkvz": [r_qkvz], "act": [r_actb], "ln": [r_ln], "ckv": [r_ckv]}
        biga_cur = [None]

        def biga_phase(name):
            if biga_cur[0] is not None and biga_cur[0] != name:
                fence(biga_groups[name], biga_groups[biga_cur[0]])
            biga_cur[0] = name

        scr_res = {}

        def scrv(phase, name, off, n):
            key = (phase, name)
            if key not in scr_res:
                scr_res[key] = Res(f"scr_{phase}_{name}")
            assert off + n <= SCRN, (phase, name, off, n)
            return SCR[:, off:off + n], scr_res[key]

        scr_cur = [None]

        def scr_phase(name):
            if scr_cur[0] is not None and scr_cur[0] != name:
                new = [r for (ph, _), r in scr_res.items() if ph == name]
                old = [r for (ph, _), r in scr_res.items() if ph == scr_cur[0]]
                fence(new, old)
            scr_cur[0] = name

        RAWN = 3 + NCP + 1
        conv_raw = [scrv("conv", f"raw{i}", i * RAWN, RAWN) for i in range(2)]
        o = 2 * RAWN
        conv_raws = [scrv("conv", f"raws{i}", o + i * 112, 112) for i in range(2)]
        o += 224
        conv_acc = [scrv("conv", f"acc{i}", o + i * W, W) for i in range(2)]
        o += 2 * W
        conv_tmp = [scrv("conv", f"tmp{i}", o + i * 512, 512) for i in range(2)]
        o += 1024
        conv_bgb = scrv("conv", "bgb", o, W)
        o += W
        conv_sq = scrv("conv", "sq", o, W // 2)
        o += W // 2
        conv_rs = scrv("conv", "rs", o, 512)
        o += 512
        assert o <= SCRN, o
        o = 0
        ffn_raw = [[None, None], [None, None]]; ffn_raws = [[None, None], [None, None]]; ffn_acc = [[None, None], [None, None]]
        for hf in range(2):
            for pr_ in range(2):
                ffn_raw[hf][pr_] = scrv("ffn", f"raw{hf}{pr_}", o, RAWN); o += RAWN
                ffn_raws[hf][pr_] = scrv("ffn", f"raws{hf}{pr_}", o, 112); o += 112
                ffn_acc[hf][pr_] = scrv("ffn", f"acc{hf}{pr_}", o, W); o += W
        assert o <= SCRN, o
        d_hf = [[S.dsem(), S.dsem()], [S.dsem(), S.dsem()]]
        d_ohf = [[S.dsem(), S.dsem()], [S.dsem(), S.dsem()]]
        out_dsems += [d_ohf[0][0], d_ohf[0][1], d_ohf[1][0], d_ohf[1][1]]
        o = 0
        att_qbuf = scrv("att", "qbuf", o, W); o += W
        att_e = [scrv("att", f"e{i}", o + i * 256, 256) for i in range(2)]; o += 512
        att_rden = scrv("att", "rden", o, 512); o += 512
        att_kvn = scrv("att", "kvn", o, 512); o += 512
        att_KT = scrv("att", "KT", o, 256); o += 256
        att_V = scrv("att", "V", o, 256); o += 256
        att_memb = scrv("att", "memb", o, 1024); o += 1024
        att_es = scrv("att", "es", o, 256); o += 256
        o = 0
        gd = {}
        for nm, n in [("gtri", 384), ("dT", 384), ("U", 192), ("Pa", 192), ("Pb", 192), ("PTa", 192), ("PTb", 192),
                      ("Va", 192), ("Vb", 192),
                      ("Vf0", 192), ("Vf1", 192), ("qkT0", 192), ("qkT1", 192), ("kdec0", 384), ("kdec1", 384),
                      ("vtm0", 384), ("vtm1", 384),
                      ("Y", 384), ("vnew", 384), ("ot", 768), ("sq", 768), ("ssq", 16),
                      ("Ss0", 768), ("Ss1", 768), ("Ssb0", 384), ("Ssb1", 384), ("Spb", 384)]:
            gd[nm] = scrv("gdn", nm, o, n)
            o += n
        assert o <= SCRN, o

        class WS:
            def __init__(self):
                self.plan = []
                self.issued = 0
                self.popped = 0
                self.planning = True
                self.PREF = 2

            def _issue(self, s):
                tile_, res, ds = wslots[s % NSLOT]
                KC, parts = self.plan[s]
                nw = sum(n for _, n, _ in parts)
                v = tile_[:, 0:KC * nw].rearrange("p (k n) -> p k n", k=KC)
                for off, n, src in parts:
                    S.dma("pool", v[:, :, off:off + n], src.rearrange("(k p) n -> p k n", p=128), [], [res], ds)

            def stage(self, KC, parts):
                if self.planning:
                    self.plan.append((KC, parts))
                    return None, None
                while self.issued < min(len(self.plan), self.popped + self.PREF + 1):
                    self._issue(self.issued)
                    self.issued += 1
                s = self.popped
                self.popped += 1
                KCp, partsp = self.plan[s]
                assert KCp == KC and len(partsp) == len(parts)
                tile_, res, ds = wslots[s % NSLOT]
                nw = sum(n for _, n, _ in parts)
                return tile_[:, 0:KC * nw].rearrange("p (k n) -> p k n", k=KC), res

        ws = WS()

        def mm_group(out_ap, pres, pairs, reads):
            def fn(e):
                ins = None
                n = len(pairs)
                for i, (l, r) in enumerate(pairs):
                    ins = e.matmul(out_ap, lhsT=l, rhs=r, start=(i == 0), stop=(i == n - 1))
                return ins
            S.emit("pe", fn, reads, [pres])

        def mm_multi(items, pres_list, reads):
            def fn(e):
                ins = None
                for out_ap, pairs in items:
                    n = len(pairs)
                    for i, (l, r) in enumerate(pairs):
                        ins = e.matmul(out_ap, lhsT=l, rhs=r, start=(i == 0), stop=(i == n - 1))
                return ins
            S.emit("pe", fn, reads, pres_list)

        def tr_multi(items, pres_list, reads):
            def fn(e):
                ins = None
                for out_ap, in_ap, id_ap in items:
                    ins = e.transpose(out_ap, in_ap, id_ap)
                return ins
            S.emit("pe", fn, reads, pres_list)

        def coltiles(p):
            ct = [(0, 512), (512, 512)]
            if p == 1:
                ct.append((NCP, NSC))
            return ct

        def is_s(c0):
            return c0 >= NCP

        def program():
            psi[0] = 0
            biga_cur[0] = None
            scr_cur[0] = None
            S.dma("sp", cst[:, :], cstd[:, :], [], [r_cst], d_in)
            S.dma("sp", spt[:, :], spd[:, :], [], [r_sp], d_in)
            S.emit("dve", lambda e: e.tensor_copy(out=identb[:, :], in_=ident), [r_cst], [r_cb])
            S.emit("dve", lambda e: e.tensor_copy(out=onesb[:, :], in_=ones), [r_cst], [r_cb])
            S.emit("dve", lambda e: e.tensor_scalar(out=meanb[:, :], in0=ones, scalar1=1.0 / D, scalar2=None,
                                                    op0=ALU.mult), [r_cst], [r_cb])
            S.emit("dve", lambda e: e.tensor_copy(
                out=mask6[:, :, :], in_=maskT[0:64, :].unsqueeze(1).to_broadcast([64, 6, 64])), [r_cst], [r_mask6])
            S.emit("dve", lambda e: e.memset(obuf[:, :, :], 0.0), [], [r_obuf])
            S.emit("dve", lambda e: e.memset(sc_carry[:, :, :, :], 0.0), [], [r_scc])
            S.emit("dve", lambda e: e.memset(gc_carry[:, :, :, :], 0.0), [], [r_gcc])
            S.emit("dve", lambda e: e.memset(ff_carry[:, :, :, :], 0.0), [], [r_ffc])
            for j in range(2):
                S.emit("dve", lambda e, j=j: e.memset(Sst[:, j, :, :], 0.0), [], [r_Sst[j]])

            for p in range(npass):
                ct = coltiles(p)
                ncol = NCP + (NSC if p == 1 else 0)
                S.dma("sp", xres[:, :, 0:NCP], xpT[:, p * NCP:(p + 1) * NCP].rearrange("(k q) t -> q k t", q=128),
                      [], [r_xres], d_x)
                if p == 1:
                    S.dma("sp", xres[:, :, NCP:W], xsT[:, :].rearrange("(k q) t -> q k t", q=128), [], [r_xres], d_x)
                for kc in range(8):
                    en = ("act", "dve")[kc % 2]
                    if en == "act":
                        S.emit("act", lambda e, kc=kc: e.activation(out=xbf[:, kc, 0:ncol], in_=xres[:, kc, 0:ncol],
                                                                  func=AF.Copy), [r_xres], [r_xbf])
                    else:
                        S.emit(en, lambda e, kc=kc: e.tensor_copy(out=xbf[:, kc, 0:ncol], in_=xres[:, kc, 0:ncol]),
                               [r_xres], [r_xbf])
                for l in range(nlayers):
                    if l % 2 == 0:
                        mixer_a(l, p)
                    else:
                        mixer_b(l, p)
                    if "ln" in KPARTS:
                        layernorm(SP_LN1G + l * 8, SP_LN1B + l * 8, p)
                    if "ffn" in KPARTS:
                        ffn(l, p)
                    if "ln" in KPARTS:
                        layernorm(SP_LN2G + l * 8, SP_LN2B + l * 8, p)
                S.dma("sp", ypT[:, p * NCP:(p + 1) * NCP].rearrange("(k q) t -> q k t", q=128), xres[:, :, 0:NCP],
                      [r_xres], [], d_y)
                if p == 1:
                    S.dma("sp", ysT[:, :].rearrange("(k q) t -> q k t", q=128), xres[:, :, NCP:W], [r_xres], [], d_y)
            for j in range(2):
                S.dma("sp", scpT[j], sc_carry[:, j, :, :], [r_scc], [], d_fin)
                S.dma("sp", gcpT[j], gc_carry[:, j, :, :], [r_gcc], [], d_fin)
                S.dma("sp", gsp[j], Sst[:, j, :, :], [r_Sst[j]], [], d_fin)
            for l in range(DEPTH):
                S.dma("sp", ffpT[l], ff_carry[:, l, :, :], [r_ffc], [], d_fin)
            if not S.dry:
                for ds in out_dsems:
                    if ds.cnt > 0:
                        S.eng["sp"]["h"].wait_ge(ds.sem, ds.cnt)

        def conv_taps(p, accv, racc, rawv, rraw, rawsv, rraws, woff, Wd, only=None):
            Hh = Wd - 1
            jjs = [jj for jj in range(Wd) if only is None or (only == "first") == (jj == 0)]
            a_p = accv[:, 0:NCP]
            for jj in jjs:
                src = rawv[:, 3 - Hh + jj:3 - Hh + jj + NCP]
                wap = spv(woff + jj)
                if jj == 0 and CONV_ACT:
                    S.emit("act", lambda e, src=src, wap=wap: e.activation(
                        out=a_p, in_=src, func=AF.Copy, scale=wap), [rraw, r_sp], [racc])
                elif jj == 0:
                    S.emit("dve", lambda e, src=src, wap=wap: e.tensor_scalar(
                        out=a_p, in0=src, scalar1=wap, scalar2=None, op0=ALU.mult), [rraw, r_sp], [racc])
                else:
                    S.emit("dve", lambda e, src=src, wap=wap: e.scalar_tensor_tensor(
                        out=a_p, in0=src, scalar=wap, in1=a_p, op0=ALU.mult, op1=ALU.add), [rraw, r_sp, racc], [racc])
            if p == 1:
                a_s = accv[:, NCP:W].rearrange("p (b t) -> p b t", b=NB)
                rs3 = rawsv.rearrange("p (t b) -> p b t", b=NB)
                for jj in jjs:
                    src = rs3[:, :, 3 - Hh + jj:3 - Hh + jj + TS]
                    wap = spv(woff + jj)
                    if jj == 0 and CONV_ACT:
                        S.emit("act", lambda e, src=src, wap=wap: e.activation(
                            out=a_s, in_=src, func=AF.Copy, scale=wap), [rraws, r_sp], [racc])
                    elif jj == 0:
                        S.emit("dve", lambda e, src=src, wap=wap: e.tensor_scalar(
                            out=a_s, in0=src, scalar1=wap, scalar2=None, op0=ALU.mult), [rraws, r_sp], [racc])
                    else:
                        S.emit("dve", lambda e, src=src, wap=wap: e.scalar_tensor_tensor(
                            out=a_s, in0=src, scalar=wap, in1=a_s, op0=ALU.mult, op1=ALU.add),
                            [rraws, r_sp, racc], [racc])

        def raw_dst(c0, n, rawv, rawsv):
            if is_s(c0):
                return rawsv.rearrange("p (t b) -> p b t", b=NB)[:, :, 3:3 + TS]
            return rawv[:, 3 + c0:3 + c0 + n]

        def ps_src(psap, c0, n):
            if is_s(c0):
                return psap[:, 0:NSC].rearrange("p (b t) -> p b t", b=NB)
            return psap[:, 0:n]

        def attention(l, p, w_in, qcol0):
            scr_phase("att")
            qbuf, r_q = att_qbuf
            qb = qbuf.bitcast(BF16).rearrange("p (c w) -> p c w", c=2)
            KT = att_KT[0].bitcast(BF16).rearrange("p (c m) -> p c m", c=2); r_KT = att_KT[1]
            Vt = att_V[0].bitcast(BF16).rearrange("p (c m) -> p c m", c=2); r_V = att_V[1]
            memb = att_memb[0].bitcast(BF16).rearrange("p (k m) -> p k m", k=8); r_memb = att_memb[1]
            kvn, r_kvn = att_kvn
            S.dma("pool", memb, memT[:, :].rearrange("(k q) m -> q k m", q=128), [], [r_memb], d_memb)
            wt, rw = ws.stage(8, [(0, 512, w_mem_kv[l])])
            if not S.dry:
                for mc in range(2):
                    pt, pr = ps_next()
                    mm_group(pt[:, :], pr, [(memb[:, kc, mc * 128:(mc + 1) * 128], wt[:, kc, :]) for kc in range(8)],
                             [r_memb, rw])
                    S.emit("dve", lambda e, pt=pt, mc=mc: e.tensor_copy(out=Vt[:, mc, :], in_=pt[:, 256:512]), [pr], [r_V])
                    if p == 0:
                        S.emit("act", lambda e, pt=pt: e.activation(out=kvn, in_=pt[:, :], func=AF.Copy), [pr], [r_kvn])
                        S.dma("sp", mk[l, mc * 128:(mc + 1) * 128, :], kvn[:, 0:256], [r_kvn], [], d_kvn)
                        S.dma("sp", mv[l, mc * 128:(mc + 1) * 128, :], kvn[:, 256:512], [r_kvn], [], d_kvn)
                for c in range(2):
                    pt, pr = ps_next()
                    mm_group(pt[:, 0:256], pr, [(wt[:, kc, c * 128:(c + 1) * 128], memb[:, kc, :]) for kc in range(8)],
                             [r_memb, rw])
                    S.emit("act", lambda e, pt=pt, c=c: e.activation(out=KT[:, c, :], in_=pt[:, 0:256], func=AF.Copy),
                           [pr], [r_KT])
            if p == 1:
                biga_phase("ckv")
                for c_ in range(2):
                    S.dma("pool", ckb[:, c_, :, :], ckT[l, c_ * 128:(c_ + 1) * 128, :, :], [], [r_ckv], d_ckv)
                    S.dma("pool", cvb[:, c_, :, :], cv[l, :, c_ * 128:(c_ + 1) * 128, :].rearrange("b q e -> q b e"), [], [r_ckv], d_ckv)
            wt, rw = ws.stage(8, [(0, 256, w_in[:, qcol0:qcol0 + 256])])
            if S.dry:
                return
            for c in range(2):
                for (c0, n) in coltiles(p):
                    pt, pr = ps_next()
                    mm_group(pt[:, 0:n], pr, [(wt[:, kc, c * 128:(c + 1) * 128], xbf[:, kc, c0:c0 + n]) for kc in range(8)],
                             [r_xbf, rw])
                    S.emit("act", lambda e, pt=pt, c=c, c0=c0, n=n: e.activation(
                        out=qb[:, c, c0:c0 + n], in_=pt[:, 0:n], func=AF.Copy, scale=0.125), [pr], [r_q])
            rden, r_rden = att_rden
            for h in range(4 if "att2" in KPARTS else 0):
                c = h // 2
                pb = (h % 2) * 64
                for (c0, n) in ((0, 512), (512, 512)):
                    evs = []
                    for mc in range(2):
                        pt, pr = ps_next()
                        mm_group(pt[:, 0:n], pr, [(KT[pb:pb + 64, c, mc * 128:(mc + 1) * 128], qb[pb:pb + 64, c, c0:c0 + n])],
                                 [r_KT, r_q])
                        ev, r_ev = att_e[mc]
                        evb = ev.bitcast(BF16)
                        S.emit("act", lambda e, pt=pt, evb=evb, n=n: e.activation(out=evb[:, 0:n], in_=pt[:, 0:n], func=AF.Exp),
                               [pr], [r_ev])
                        evs.append((evb, r_ev))
                    pso, pro = ps_next()
                    mm_group(pso[pb:pb + 64, 0:n], pro,
                             [(Vt[:, mc, h * 64:(h + 1) * 64], evs[mc][0][:, 0:n]) for mc in range(2)],
                             [r_V, evs[0][1], evs[1][1]])
                    psd, prd = ps_next()
                    mm_group(psd[pb:pb + 64, 0:n], prd, [(onesb[:, 0:64], evs[mc][0][:, 0:n]) for mc in range(2)],
                             [r_cb, evs[0][1], evs[1][1]])
                    S.emit("dve", lambda e, psd=psd, pb=pb, n=n: e.reciprocal(out=rden[pb:pb + 64, 0:n], in_=psd[pb:pb + 64, 0:n]),
                           [prd], [r_rden])
                    S.emit("dve", lambda e, pso=pso, pb=pb, n=n, c=c, c0=c0: e.tensor_tensor(
                        out=obuf[pb:pb + 64, 6 + c, c0:c0 + n], in0=pso[pb:pb + 64, 0:n], in1=rden[pb:pb + 64, 0:n],
                        op=ALU.mult), [pro, r_rden], [r_obuf])
            if p == 1:
                esb = att_es[0].bitcast(BF16); r_es = att_es[1]
                pt, pr = ps_next()
                items = []
                for b in range(NB):
                    for h in range(4):
                        c = h // 2
                        pb = (h % 2) * 64
                        for mc in range(2):
                            idx = ((b * 4 + h) * 2 + mc) * TS
                            items.append((pt[:, idx:idx + TS],
                                          [(ckb[pb:pb + 64, c, b, mc * 128:(mc + 1) * 128],
                                            qb[pb:pb + 64, c, NCP + b * TS:NCP + (b + 1) * TS])]))
                mm_multi(items, [pr], [r_ckv, r_q])
                S.emit("act", lambda e: e.activation(out=esb[:, :], in_=pt[:, :], func=AF.Exp), [pr], [r_es])
                pso, pro = ps_next()
                psd, prd = ps_next()
                items = []
                for b in range(NB):
                    for h in range(4):
                        c = h // 2
                        pb = (h % 2) * 64
                        oc0 = c * NSC + b * TS
                        prs_o = []
                        prs_d = []
                        for mc in range(2):
                            idx = ((b * 4 + h) * 2 + mc) * TS
                            prs_o.append((cvb[:, mc, b, h * 64:(h + 1) * 64], esb[:, idx:idx + TS]))
                            prs_d.append((onesb[:, 0:64], esb[:, idx:idx + TS]))
                        items.append((pso[pb:pb + 64, oc0:oc0 + TS], prs_o))
                        items.append((psd[pb:pb + 64, oc0:oc0 + TS], prs_d))
                mm_multi(items, [pro, prd], [r_ckv, r_es, r_cb])
                S.emit("dve", lambda e: e.reciprocal(out=rden[:, 0:128], in_=psd[:, 0:128]), [prd], [r_rden])
                S.emit("dve", lambda e: e.tensor_tensor(
                    out=obuf[:, 6:8, NCP:W], in0=pso[:, 0:128].rearrange("p (c t) -> p c t", c=2),
                    in1=rden[:, 0:128].rearrange("p (c t) -> p c t", c=2), op=ALU.mult), [pro, r_rden], [r_obuf])

        def out_proj(w_out, p):
            for st in range(2):
                wt, rw = ws.stage(8, [(0, 512, w_out[:, st * 512:(st + 1) * 512])])
                if S.dry:
                    continue
                for cc in range(4):
                    oc = st * 4 + cc
                    for (c0, n) in coltiles(p):
                        pt, pr = ps_next()
                        mm_group(pt[:, 0:n], pr, [(wt[:, kc, cc * 128:(cc + 1) * 128], obuf[:, kc, c0:c0 + n]) for kc in range(8)],
                                 [r_obuf, rw])
                        S.emit("dve", lambda e, pt=pt, oc=oc, c0=c0, n=n: e.scalar_tensor_tensor(
                            out=xres[:, oc, c0:c0 + n], in0=xres[:, oc, c0:c0 + n], scalar=ALPHA, in1=pt[:, 0:n],
                            op0=ALU.mult, op1=ALU.add), [pr, r_xres], [r_xres])

        def mixer_a(l, p):
            j = l // 2
            Wm = w_in_a[j]
            if "att" in KPARTS:
                attention(l, p, Wm, 3 * SC_DIM)
            scr_phase("conv")
            for c in range(6 if "mix" in KPARTS else 0):
                wt, rw = ws.stage(8, [(0, 128, Wm[:, c * 128:(c + 1) * 128]),
                                      (128, 128, Wm[:, SC_DIM + c * 128:SC_DIM + (c + 1) * 128]),
                                      (256, 128, Wm[:, 2 * SC_DIM + c * 128:2 * SC_DIM + (c + 1) * 128])])
                if S.dry:
                    continue
                rawv, rraw = conv_raw[c % 2]
                rawsv, rraws = conv_raws[c % 2]
                accv, racc = conv_acc[c % 2]
                bgb, rbg = conv_bgb
                S.emit("dve", lambda e, c=c: e.tensor_copy(out=rawv[:, 1:3], in_=sc_carry[:, j, c, :]), [r_scc], [rraw])
                if p == 1:
                    S.dma("sp", rawsv[:, NB:3 * NB].rearrange("p (r b) -> p r b", b=NB), scT[j, c * 128:(c + 1) * 128, :, :],
                          [], [rraws], d_h[c % 2])
                for ti, (c0, n) in enumerate(coltiles(p)):
                    pss = []
                    for part in range(3):
                        pt, pr = ps_next()
                        mm_group(pt[:, 0:n], pr, [(wt[:, kc, part * 128:(part + 1) * 128], xbf[:, kc, c0:c0 + n])
                                                 for kc in range(8)], [r_xbf, rw])
                        pss.append((pt, pr))
                    tmpv, rtmp = conv_tmp[ti % 2]
                    S.emit("act", lambda e, pt=pss[0][0], n=n: e.activation(out=tmpv[:, 0:n], in_=pt[:, 0:n], func=AF.Copy),
                           [pss[0][1]], [rtmp])
                    dst = raw_dst(c0, n, rawv, rawsv)
                    in1 = tmpv[:, 0:NSC].rearrange("p (b t) -> p b t", b=NB) if is_s(c0) else tmpv[:, 0:n]
                    S.emit("dve", lambda e, pt=pss[2][0], dst=dst, in1=in1, c0=c0, n=n: e.tensor_tensor(
                        out=dst, in0=ps_src(pt, c0, n), in1=in1, op=ALU.mult),
                        [pss[2][1], rtmp], [rraws if is_s(c0) else rraw])
                    S.emit("act", lambda e, pt=pss[1][0], c0=c0, n=n: e.activation(out=bgb[:, c0:c0 + n], in_=pt[:, 0:n],
                                                                                 func=AF.Copy), [pss[1][1]], [rbg])
                conv_taps(p, accv, racc, rawv, rraw, rawsv, rraws, SP_CONVA + (j * 6 + c) * 3, 3)
                S.emit("dve", lambda e, c=c: e.tensor_copy(out=sc_carry[:, j, c, :], in_=rawv[:, 3 + NCP - 2:3 + NCP]),
                       [rraw], [r_scc])
                if p == 1:
                    S.dma("sp", scsT[j, c * 128:(c + 1) * 128, :, :],
                          rawsv[:, 5 * NB:7 * NB].rearrange("p (r b) -> p r b", b=NB), [rraws], [], d_oh[c % 2])
                ncol = NCP + (NSC if p == 1 else 0)
                S.emit("dve", lambda e, c=c, ncol=ncol: e.tensor_tensor(out=obuf[:, c, 0:ncol], in0=bgb[:, 0:ncol],
                                                                         in1=accv[:, 0:ncol], op=ALU.mult),
                       [rbg, racc], [r_obuf])
            if "out" in KPARTS:
                out_proj(w_out_a[j], p)

        def mixer_b(l, p):
            j = l // 2
            Wm = w_in_b[j]
            attention(l, p, Wm, 4 * SC_DIM + 12)
            scr_phase("conv")
            biga_phase("qkvz")
            ncol = NCP + (NSC if p == 1 else 0)
            S.dma("pool", wba[:, :, :], wbaT[j], [], [r_wba], d_wba)
            S.emit("act", lambda e: e.activation(out=negA[:, :], in_=spv(SP_ALOG + j * 6, 6), func=AF.Exp), [r_sp], [r_negA])
            S.emit("dve", lambda e: e.tensor_scalar(out=negA[:, :], in0=negA[:, :], scalar1=-1.0, scalar2=None, op0=ALU.mult),
                   [r_negA], [r_negA])
            sqv = conv_sq[0].bitcast(BF16); rsq = conv_sq[1]
            rsv, rrs = conv_rs
            for st in range(6):
                wt, rw = ws.stage(8, [(0, 512, Wm[:, st * 512:(st + 1) * 512])])
                if S.dry:
                    continue
                for cc in range(4):
                    c = st * 4 + cc
                    if c >= 18:
                        for (c0, n) in coltiles(p):
                            pt, pr = ps_next()
                            mm_group(pt[:, 0:n], pr, [(wt[:, kc, cc * 128:(cc + 1) * 128], xbf[:, kc, c0:c0 + n])
                                                     for kc in range(8)], [r_xbf, rw])
                            S.emit("act", lambda e, pt=pt, c=c, c0=c0, n=n: e.activation(
                                out=qkvz[:, c, c0:c0 + n], in_=pt[:, 0:n], func=AF.Silu), [pr], [r_qkvz])
                        continue
                    rawv, rraw = conv_raw[c % 2]
                    rawsv, rraws = conv_raws[c % 2]
                    accv, racc = conv_acc[c % 2]
                    S.emit("dve", lambda e, c=c: e.tensor_copy(out=rawv[:, 0:3], in_=gc_carry[:, j, c, :]), [r_gcc], [rraw])
                    if p == 1:
                        S.dma("sp", rawsv[:, 0:3 * NB].rearrange("p (r b) -> p r b", b=NB),
                              gcT[j, c * 128:(c + 1) * 128, :, :], [], [rraws], d_h[c % 2])
                    for (c0, n) in coltiles(p):
                        pt, pr = ps_next()
                        mm_group(pt[:, 0:n], pr, [(wt[:, kc, cc * 128:(cc + 1) * 128], xbf[:, kc, c0:c0 + n])
                                                 for kc in range(8)], [r_xbf, rw])
                        dst = raw_dst(c0, n, rawv, rawsv)
                        S.emit("act", lambda e, pt=pt, dst=dst, c0=c0, n=n: e.activation(
                            out=dst, in_=ps_src(pt, c0, n), func=AF.Copy), [pr], [rraws if is_s(c0) else rraw])
                    conv_taps(p, accv, racc, rawv, rraw, rawsv, rraws, SP_CONVB + (j * 18 + c) * 4, 4)
                    S.emit("dve", lambda e, c=c: e.tensor_copy(out=gc_carry[:, j, c, :], in_=rawv[:, 3 + NCP - 3:3 + NCP]),
                           [rraw], [r_gcc])
                    if p == 1:
                        S.dma("sp", gcsT[j, c * 128:(c + 1) * 128, :, :],
                              rawsv[:, 4 * NB:7 * NB].rearrange("p (r b) -> p r b", b=NB), [rraws], [], d_oh[c % 2])
                    if c >= 12:
                        S.emit("act", lambda e, c=c: e.activation(out=qkvz[:, c, 0:ncol], in_=accv[:, 0:ncol], func=AF.Silu),
                               [racc], [r_qkvz])
                        continue
                    S.emit("act", lambda e: e.activation(out=accv[:, 0:ncol], in_=accv[:, 0:ncol], func=AF.Silu),
                           [racc], [racc])
                    S.emit("act", lambda e: e.activation(out=sqv[:, 0:ncol], in_=accv[:, 0:ncol], func=AF.Square), [racc], [rsq])
                    ebias = float(np.log(128.0 ** -0.5)) if c < 6 else 0.0
                    for (c0, n) in coltiles(p):
                        pt, pr = ps_next()
                        mm_group(pt[:, 0:n], pr, [(onesb[:, :], sqv[:, c0:c0 + n])], [r_cb, rsq])
                        S.emit("dve", lambda e, pt=pt, n=n: e.tensor_scalar(out=rsv[:, 0:n], in0=pt[:, 0:n], scalar1=RMS_EPS,
                                                                           scalar2=None, op0=ALU.add), [pr], [rrs])
                        S.emit("act", lambda e, n=n: e.activation(out=rsv[:, 0:n], in_=rsv[:, 0:n], func=AF.Ln), [rrs], [rrs])
                        if ebias != 0.0:
                            S.emit("dve", lambda e, n=n: e.tensor_scalar(out=rsv[:, 0:n], in0=rsv[:, 0:n], scalar1=-0.5,
                                                                        scalar2=ebias, op0=ALU.mult, op1=ALU.add),
                                   [rrs], [rrs])
                            S.emit("act", lambda e, n=n: e.activation(out=rsv[:, 0:n], in_=rsv[:, 0:n], func=AF.Exp),
                                   [rrs], [rrs])
                        else:
                            S.emit("act", lambda e, n=n: e.activation(out=rsv[:, 0:n], in_=rsv[:, 0:n], func=AF.Exp,
                                                                      scale=-0.5), [rrs], [rrs])
                        S.emit("dve", lambda e, c=c, c0=c0, n=n: e.tensor_tensor(
                            out=qkvz[:, c, c0:c0 + n], in0=accv[:, c0:c0 + n], in1=rsv[:, 0:n], op=ALU.mult),
                            [racc, rrs], [r_qkvz])
            if not S.dry:
                scr_phase("gdn")
                gdn_scalars(p, 64, lambda n: n * 64)
                Spb_ap, r_Spb = gd["Spb"]
                Spb = Spb_ap.bitcast(BF16).rearrange("p (h e) -> p h e", h=GDN_H)
                S.emit("act", lambda e: e.activation(out=Spb, in_=Sst[:, j, :, :], func=AF.Copy), [r_Sst[j]], [r_Spb])
                nxt = run_gens(gdn_A(0, 0, 64), None)
                for n in range(16):
                    cur = nxt
                    gA = gdn_A(n + 1, (n + 1) * 64, 64) if n + 1 < 16 else None
                    gB = gdn_B(j, n, n * 64, 64, Sst[:, j, :, :], r_Sst[j], Spb, r_Spb, cur)
                    nxt = run_gens(gA, gB)
                if p == 1:
                    gdn_scalars(p, TS, lambda n: NCP + n * TS)

                    def load_S(b):
                        St, rS = gd[f"Ss{b % 2}"]
                        Sv = St.rearrange("p (h e) -> p h e", h=GDN_H)
                        Sb_ap, rSb = gd[f"Ssb{b % 2}"]
                        Sb = Sb_ap.bitcast(BF16).rearrange("p (h e) -> p h e", h=GDN_H)
                        S.dma("sp", Sv, gs[j, b], [], [rS], d_ss[b % 2])
                        S.emit("act", lambda e: e.activation(out=Sb, in_=Sv, func=AF.Copy), [rS], [rSb])
                        return Sv, rS, Sb, rSb

                    nxt = run_gens(gdn_A(0, NCP, TS), None)
                    nS = load_S(0)
                    for b in range(NB):
                        cur, cS = nxt, nS
                        gA = None
                        if b + 1 < NB:
                            nS = load_S(b + 1)
                            gA = gdn_A(b + 1, NCP + (b + 1) * TS, TS)
                        gB = gdn_B(j, b, NCP + b * TS, TS, cS[0], cS[1], cS[2], cS[3], cur)
                        nxt = run_gens(gA, gB)
                        S.dma("sp", gss[j, b], cS[0], [cS[1]], [], d_so[b % 2])
            out_proj(w_out_b[j], p)

        def gdn_scalars(p, C, colfn):
            pt, pr = ps_next()
            items = []
            for n in range(16):
                col = colfn(n)
                items.append((pt[0:C, n * 12:(n + 1) * 12], [(xbf[:, kc, col:col + C], wba[:, kc, :]) for kc in range(8)]))
            mm_multi(items, [pr], [r_xbf, r_wba])
            pv = pt[0:C, 0:192].rearrange("p (n k) -> p n k", n=16)
            b3 = g_beta[0:C, :].rearrange("p (n h) -> p n h", n=16)
            t3 = g_tmp[0:C, :].rearrange("p (n h) -> p n h", n=16)
            S.emit("act", lambda e: e.activation(out=b3, in_=pv[:, :, 0:6], func=AF.Exp, scale=-1.0), [pr], [r_gsc])
            S.emit("dve", lambda e: e.tensor_scalar(out=g_beta[0:C, :], in0=g_beta[0:C, :], scalar1=1.0, scalar2=None,
                                                    op0=ALU.add), [r_gsc], [r_gsc])
            S.emit("dve", lambda e: e.reciprocal(out=g_beta[0:C, :], in_=g_beta[0:C, :]), [r_gsc], [r_gsc])
            jdt = spv(SP_DTB + cur_j[0] * 6, 6)
            S.emit("dve", lambda e: e.tensor_tensor(out=t3, in0=pv[:, :, 6:12],
                                                    in1=jdt[0:C, :].unsqueeze(1).to_broadcast([C, 16, 6]), op=ALU.add),
                   [pr, r_sp], [r_gsc])
            S.emit("act", lambda e: e.activation(out=g_tmp[0:C, :], in_=g_tmp[0:C, :], func=AF.Exp), [r_gsc], [r_gsc])
            S.emit("act", lambda e: e.activation(out=g_tmp[0:C, :], in_=g_tmp[0:C, :], func=AF.Ln, bias=1.0), [r_gsc], [r_gsc])
            g3 = g_g[0:C, :].rearrange("p (n h) -> p n h", n=16)
            S.emit("dve", lambda e: e.tensor_tensor(out=g3, in0=t3, in1=negA[0:C, :].unsqueeze(1).to_broadcast([C, 16, 6]),
                                                    op=ALU.mult), [r_gsc, r_negA], [r_gsc])
            pg, prg = ps_next()
            mm_multi([(pg[0:C, 0:96], [(tri[0:C, 0:C], g_g[0:C, :])]),
                      (pg[0:C, 96:192], [(ones[0:C, 0:C], g_g[0:C, :])]),
                      (pg[:, 192:288], [(ones[0:C, 0:128], g_g[0:C, :])])], [prg], [r_cst, r_gsc])
            S.emit("act", lambda e: e.activation(out=g_gc[0:C, :], in_=pg[0:C, 0:96], func=AF.Copy), [prg], [r_gsc])
            S.emit("act", lambda e: e.activation(out=g_egc[0:C, :], in_=pg[0:C, 0:96], func=AF.Exp), [prg], [r_gsc])
            S.emit("dve", lambda e: e.tensor_scalar(out=g_negegc[0:C, :], in0=g_egc[0:C, :], scalar1=-1.0, scalar2=None,
                                                    op0=ALU.mult), [r_gsc], [r_gsc])
            S.emit("dve", lambda e: e.tensor_tensor(out=g_kds[0:C, :], in0=pg[0:C, 96:192], in1=g_gc[0:C, :], op=ALU.subtract),
                   [prg, r_gsc], [r_gsc])
            S.emit("act", lambda e: e.activation(out=g_kds[0:C, :], in_=g_kds[0:C, :], func=AF.Exp), [r_gsc], [r_gsc])
            S.emit("act", lambda e: e.activation(out=g_gam[:, :], in_=pg[:, 192:288], func=AF.Exp), [prg], [r_gsc])

        cur_j = [0]

        def run_gens(gA, gB):
            res = None
            live = [g for g in (gA, gB) if g is not None]
            while live:
                for g in list(live):
                    try:
                        next(g)
                    except StopIteration as e:
                        if g is gA:
                            res = e.value
                        live.remove(g)
            return res

        def gdn_A(n, col0, C):
            H = GDN_H
            HC = H * C
            par = n % 2

            def t3(nm, inner, dt=F32):
                ap, r = gd[nm]
                if dt == BF16:
                    ap = ap.bitcast(BF16)
                return ap[0:C, 0:H * inner].rearrange("p (h x) -> p h x", h=H), r

            def kT(h):
                return qkvz[:, 6 + h, col0:col0 + C]

            def qT(h):
                return qkvz[:, h, col0:col0 + C]

            kdec, r_kdec = t3(f"kdec{par}", 128, BF16)
            vtm, r_vtm = t3(f"vtm{par}", 128, BF16)
            pb_, prb = psbf
            pbv = pb_[0:C, 0:768].rearrange("p (h x) -> p h x", h=H)
            tr_multi([(pb_[0:C, h * 128:(h + 1) * 128], kT(h), identb[:, :]) for h in range(H)], [prb], [r_qkvz, r_cb])
            yield
            S.emit("dve", lambda e: e.tensor_tensor(
                out=kdec, in0=pbv, in1=g_kds[0:C, n * 6:n * 6 + 6].unsqueeze(2).to_broadcast([C, H, 128]), op=ALU.mult),
                [prb, r_gsc], [r_kdec])
            yield
            tr_multi([(pb_[0:C, h * 128:(h + 1) * 128], qkvz[:, 12 + h, col0:col0 + C], identb[:, :]) for h in range(H)],
                     [prb], [r_qkvz, r_cb])
            yield
            S.emit("act", lambda e: e.activation(out=vtm, in_=pbv, func=AF.Copy), [prb], [r_vtm])
            yield
            pK, prK = ps_next("A")
            mm_multi([(pK[0:C, h * C:(h + 1) * C], [(kT(h), kT(h))]) for h in range(H)], [prK], [r_qkvz])
            yield
            pQ, prQ = ps_next("A")
            mm_multi([(pQ[0:C, h * C:(h + 1) * C], [(kT(h), qT(h))]) for h in range(H)], [prQ], [r_qkvz])
            yield
            gtri, r_gtri = t3("gtri", C)
            S.emit("dve", lambda e: e.tensor_tensor(
                out=gtri, in0=tri[0:C, 0:C].unsqueeze(1).to_broadcast([C, H, C]),
                in1=g_g[0:C, n * 6:n * 6 + 6].unsqueeze(2).to_broadcast([C, H, C]), op=ALU.mult), [r_cst, r_gsc], [r_gtri])
            yield
            pD, prD = ps_next("A")
            m6 = mask6[0:C, :, 0:C]
            if C == 64:
                m6f = mask6[0:C, :, :].rearrange("p h x -> p (h x)")
                mm_group(pD[0:C, 0:HC], prD, [(ones[0:C, 0:C], gd["gtri"][0][0:C, 0:HC]), (ident[0:C, 0:C], m6f)],
                         [r_cst, r_gtri, r_mask6])
                yield
            else:
                mm_group(pD[0:C, 0:HC], prD, [(ones[0:C, 0:C], gd["gtri"][0][0:C, 0:HC])], [r_cst, r_gtri])
                yield
            dT, r_dT = t3("dT", C)
            pD3 = pD[0:C, 0:HC].rearrange("p (h x) -> p h x", h=H)
            S.emit("dve", lambda e: e.tensor_tensor(
                out=dT, in0=pD3, in1=g_gc[0:C, n * 6:n * 6 + 6].unsqueeze(2).to_broadcast([C, H, C]), op=ALU.subtract),
                [prD, r_gsc], [r_dT])
            yield
            if C != 64:
                S.emit("dve", lambda e: e.tensor_tensor(out=dT, in0=dT, in1=m6, op=ALU.add), [r_dT, r_mask6], [r_dT])
                yield
            S.emit("act", lambda e: e.activation(out=dT, in_=dT, func=AF.Exp), [r_dT], [r_dT])
            yield
            bsm = gtri
            S.emit("dve", lambda e: e.tensor_tensor(
                out=bsm, in0=strU[0:C, 0:C].unsqueeze(1).to_broadcast([C, H, C]),
                in1=g_beta[0:C, n * 6:n * 6 + 6].unsqueeze(2).to_broadcast([C, H, C]), op=ALU.mult),
                [r_cst, r_gsc, r_gtri], [r_gtri])
            yield
            U, r_U = t3("U", C, BF16)
            pK3 = pK[0:C, 0:HC].rearrange("p (h x) -> p h x", h=H)
            pQ3 = pQ[0:C, 0:HC].rearrange("p (h x) -> p h x", h=H)
            S.emit("dve", lambda e: e.tensor_tensor(out=bsm, in0=bsm, in1=dT, op=ALU.mult), [r_dT, r_gtri], [r_gtri])
            yield
            S.emit("dve", lambda e: e.tensor_tensor(out=U, in0=pK3, in1=bsm, op=ALU.mult), [prK, r_gtri], [r_U])
            yield
            qkT, r_qkT = t3(f"qkT{par}", C, BF16)
            S.emit("dve", lambda e: e.tensor_tensor(out=qkT, in0=pQ3, in1=dT, op=ALU.mult), [prQ, r_dT], [r_qkT])
            yield
            idC = ident[0:C, 0:C]
            idCb = identb[0:C, 0:C]
            PT, r_PT = t3("PTa", C, BF16)
            pT_, prT = ps_next("A")
            mm_multi([(pT_[0:C, h * C:(h + 1) * C], [(U[:, h, :], idCb)]) for h in range(H)], [prT], [r_U, r_cb])
            yield
            S.emit("act", lambda e: e.activation(out=PT, in_=pT_[0:C, 0:HC].rearrange("p (h x) -> p h x", h=H), func=AF.Copy),
                   [prT], [r_PT])
            yield
            Vc, r_Vc = t3("Va", C, BF16)
            S.emit("dve", lambda e: e.tensor_tensor(out=Vc, in0=idC.unsqueeze(1).to_broadcast([C, H, C]), in1=U,
                                                     op=ALU.subtract), [r_cst, r_U], [r_Vc])
            yield
            P, r_P = U, r_U
            nlev = {64: 5, 4: 1}[C]
            pnames = [("Pa", "PTb"), ("Pb", "PTa")]
            vnames = ["Vb", "Va"]
            for lev in range(nlev):
                last = lev == nlev - 1
                nP, nPT = pnames[lev % 2]
                PTn, r_PTn = t3(nPT, C, BF16)
                pA, prA = ps_next("A")
                mm_multi([(pA[0:C, h * C:(h + 1) * C], [(P[:, h, :], PT[:, h, :])]) for h in range(H)], [prA], [r_P, r_PT])
                yield
                S.emit("act", lambda e, pA=pA, PTn=PTn: e.activation(
                    out=PTn, in_=pA[0:C, 0:HC].rearrange("p (h x) -> p h x", h=H), func=AF.Copy), [prA], [r_PTn])
                yield
                if not last:
                    Pn, r_Pn = t3(nP, C, BF16)
                    pB, prB = ps_next("A")
                    mm_multi([(pB[0:C, h * C:(h + 1) * C], [(PT[:, h, :], P[:, h, :])]) for h in range(H)], [prB], [r_P, r_PT])
                    yield
                    S.emit("act", lambda e, pB=pB, Pn=Pn: e.activation(
                        out=Pn, in_=pB[0:C, 0:HC].rearrange("p (h x) -> p h x", h=H), func=AF.Copy), [prB], [r_Pn])
                    yield
                Vn, r_Vn = t3(f"Vf{par}" if last else vnames[lev % 2], C, BF16)
                pV, prV = ps_next("A")
                mm_multi([(pV[0:C, h * C:(h + 1) * C], [(PTn[:, h, :], Vc[:, h, :])]) for h in range(H)], [prV], [r_PTn, r_Vc])
                yield
                S.emit("dve", lambda e, pV=pV, Vn=Vn, Vc=Vc: e.tensor_tensor(
                    out=Vn, in0=pV[0:C, 0:HC].rearrange("p (h x) -> p h x", h=H), in1=Vc, op=ALU.add), [prV, r_Vc], [r_Vn])
                yield
                Vc, r_Vc = Vn, r_Vn
                PT, r_PT = PTn, r_PTn
                if not last:
                    P, r_P = Pn, r_Pn
            return dict(V=Vc, r_V=r_Vc, qkT=qkT, r_qkT=r_qkT, kdec=kdec, r_kdec=r_kdec, vtm=vtm, r_vtm=r_vtm)

        def gdn_B(j, n, col0, C, Sv, rS, Sb, rSb, A):
            H = GDN_H
            HC = H * C

            def t3(nm, inner, dt=F32):
                ap, r = gd[nm]
                if dt == BF16:
                    ap = ap.bitcast(BF16)
                return ap[0:C, 0:H * inner].rearrange("p (h x) -> p h x", h=H), r

            def bc(t, hs):
                return t[0:C, n * 6 + hs:n * 6 + hs + 3].unsqueeze(2).to_broadcast([C, 3, 128])

            def kT(h):
                return qkvz[:, 6 + h, col0:col0 + C]

            def qT(h):
                return qkvz[:, h, col0:col0 + C]

            Vc, r_Vc, qkT, r_qkT = A["V"], A["r_V"], A["qkT"], A["r_qkT"]
            kdec, r_kdec, vtm, r_vtm = A["kdec"], A["r_kdec"], A["vtm"], A["r_vtm"]
            Y, r_Y = t3("Y", 128, BF16)
            vnew, r_vn = t3("vnew", 128, BF16)
            ot, r_ot = t3("ot", 128)
            sq, r_sq = t3("sq", 128)
            idC = ident[0:C, 0:C]
            for hg in range(2):
                hs = 3 * hg
                pk, prk = ps_next("B")
                mm_multi([(pk[0:C, hh * 128:(hh + 1) * 128], [(kT(hs + hh), Sb[:, hs + hh, :])]) for hh in range(3)],
                         [prk], [r_qkvz, rSb])
                yield
                pq, prq = ps_next("B")
                mm_multi([(pq[0:C, hh * 128:(hh + 1) * 128], [(qT(hs + hh), Sb[:, hs + hh, :])]) for hh in range(3)],
                         [prq], [r_qkvz, rSb])
                yield
                pk3 = pk[0:C, 0:384].rearrange("p (h x) -> p h x", h=3)
                pq3 = pq[0:C, 0:384].rearrange("p (h x) -> p h x", h=3)
                S.emit("dve", lambda e, pk3=pk3, hs=hs: e.tensor_tensor(out=sq[:, hs:hs + 3, :], in0=pk3,
                                                                         in1=bc(g_negegc, hs), op=ALU.mult),
                       [prk, r_gsc], [r_sq])
                yield
                S.emit("dve", lambda e, hs=hs: e.tensor_tensor(out=Y[:, hs:hs + 3, :], in0=sq[:, hs:hs + 3, :],
                                                                in1=vtm[:, hs:hs + 3, :], op=ALU.add), [r_sq, r_vtm], [r_Y])
                yield
                px, prx = ps_next("B")
                mm_multi([(px[0:C, hh * 128:(hh + 1) * 128], [(Vc[:, hs + hh, :], Y[:, hs + hh, :])]) for hh in range(3)],
                         [prx], [r_Vc, r_Y])
                yield
                px3 = px[0:C, 0:384].rearrange("p (h x) -> p h x", h=3)
                S.emit("dve", lambda e, px3=px3, hs=hs: e.tensor_tensor(out=vnew[:, hs:hs + 3, :], in0=px3,
                                                                         in1=bc(g_beta, hs), op=ALU.mult),
                       [prx, r_gsc], [r_vn])
                yield
                po, pro = ps_next("B")
                mm_multi([(po[0:C, hh * 128:(hh + 1) * 128], [(qkT[:, hs + hh, :], vnew[:, hs + hh, :])]) for hh in range(3)],
                         [pro], [r_qkT, r_vn])
                yield
                psn, prs = ps_next("B")
                mm_multi([(psn[:, hh * 128:(hh + 1) * 128], [(kdec[:, hs + hh, :], vnew[:, hs + hh, :])]) for hh in range(3)],
                         [prs], [r_kdec, r_vn])
                yield
                ps3 = psn[:, 0:384].rearrange("p (h x) -> p h x", h=3)
                gb = g_gam[:, n * 6 + hs:n * 6 + hs + 3].unsqueeze(2).to_broadcast([128, 3, 128])
                S.emit("dve", lambda e, hs=hs, gb=gb: e.tensor_tensor(out=Sv[:, hs:hs + 3, :], in0=Sv[:, hs:hs + 3, :], in1=gb,
                                                                       op=ALU.mult), [rS, r_gsc], [rS])
                yield
                S.emit("dve", lambda e, hs=hs, ps3=ps3: e.tensor_tensor(out=Sv[:, hs:hs + 3, :], in0=Sv[:, hs:hs + 3, :],
                                                                         in1=ps3, op=ALU.add), [rS, prs], [rS])
                yield
                S.emit("act", lambda e, hs=hs: e.activation(out=Sb[:, hs:hs + 3, :], in_=Sv[:, hs:hs + 3, :], func=AF.Copy),
                       [rS], [rSb])
                yield
                po3 = po[0:C, 0:384].rearrange("p (h x) -> p h x", h=3)
                S.emit("dve", lambda e, pq3=pq3, hs=hs: e.tensor_tensor(out=ot[:, hs:hs + 3, :], in0=pq3,
                                                                         in1=bc(g_egc, hs), op=ALU.mult),
                       [prq, r_gsc], [r_ot])
                yield
                S.emit("dve", lambda e, po3=po3, hs=hs: e.tensor_tensor(out=ot[:, hs:hs + 3, :], in0=ot[:, hs:hs + 3, :],
                                                                         in1=po3, op=ALU.add), [pro, r_ot], [r_ot])
                yield
            ssq_ap, r_ssq = gd["ssq"]
            ssq = ssq_ap[0:C, 0:H]
            S.emit("dve", lambda e: e.tensor_tensor(out=sq, in0=ot, in1=ot, op=ALU.mult), [r_ot, r_sq], [r_sq])
            yield
            S.emit("dve", lambda e: e.tensor_reduce(out=ssq, in_=sq, axis=AX.X, op=ALU.add), [r_sq], [r_ssq])
            yield
            S.emit("dve", lambda e: e.tensor_scalar(out=ssq, in0=ssq, scalar1=1.0 / 128.0, scalar2=RMS_EPS, op0=ALU.mult,
                                                    op1=ALU.add), [r_ssq], [r_ssq])
            yield
            S.emit("act", lambda e: e.activation(out=ssq, in_=ssq, func=AF.Ln), [r_ssq], [r_ssq])
            yield
            S.emit("act", lambda e: e.activation(out=ssq, in_=ssq, func=AF.Exp, scale=-0.5), [r_ssq], [r_ssq])
            yield
            S.emit("dve", lambda e: e.tensor_tensor(out=ot, in0=ot, in1=ssq.unsqueeze(2).to_broadcast([C, H, 128]),
                                                     op=ALU.mult), [r_ot, r_ssq], [r_ot])
            yield
            pz, prz = ps_next("B")
            tr_multi([(pz[:, h * C:(h + 1) * C], ot[:, h, :], idC) for h in range(H)], [prz], [r_ot, r_cst])
            yield
            pz3 = pz[:, 0:HC].rearrange("p (h x) -> p h x", h=H)
            S.emit("dve", lambda e: e.scalar_tensor_tensor(
                out=obuf[:, 0:6, col0:col0 + C], in0=pz3, scalar=spv(SP_NORMW + j), in1=qkvz[:, 18:24, col0:col0 + C],
                op0=ALU.mult, op1=ALU.mult), [prz, r_sp, r_qkvz], [r_obuf])
            yield

        def layernorm(goff, boff, p):
            if S.dry:
                return
            biga_phase("ln")
            for (c0, n) in coltiles(p):
                S.emit("act", lambda e, c0=c0, n=n: e.activation(out=ln_zb[:, :, 0:n], in_=xres[:, :, c0:c0 + n], func=AF.Copy),
                       [r_xres], [r_ln])
                S.emit("act", lambda e, c0=c0, n=n: e.activation(out=ln_sqb[:, :, 0:n], in_=xres[:, :, c0:c0 + n],
                                                                func=AF.Square), [r_xres], [r_ln])
                pm, prm = ps_next()
                mm_group(pm[:, 0:n], prm, [(meanb[:, :], ln_zb[:, kc, 0:n]) for kc in range(8)], [r_cb, r_ln])
                pq, prq = ps_next()
                mm_group(pq[:, 0:n], prq, [(meanb[:, :], ln_sqb[:, kc, 0:n]) for kc in range(8)], [r_cb, r_ln])
                S.emit("act", lambda e, pm=pm, n=n: e.activation(out=ln_mean[:, 0:n], in_=pm[:, 0:n], func=AF.Copy), [prm], [r_ln])
                S.emit("dve", lambda e, n=n: e.tensor_tensor(out=ln_rstd[:, 0:n], in0=ln_mean[:, 0:n], in1=ln_mean[:, 0:n],
                                                             op=ALU.mult), [r_ln], [r_ln])
                S.emit("dve", lambda e, pq=pq, n=n: e.tensor_tensor(out=ln_rstd[:, 0:n], in0=pq[:, 0:n], in1=ln_rstd[:, 0:n],
                                                                    op=ALU.subtract), [prq, r_ln], [r_ln])
                S.emit("dve", lambda e, n=n: e.tensor_scalar(out=ln_rstd[:, 0:n], in0=ln_rstd[:, 0:n], scalar1=LN_EPS,
                                                             scalar2=None, op0=ALU.add), [r_ln], [r_ln])
                S.emit("act", lambda e, n=n: e.activation(out=ln_rstd[:, 0:n], in_=ln_rstd[:, 0:n], func=AF.Ln), [r_ln], [r_ln])
                S.emit("act", lambda e, n=n: e.activation(out=ln_rstd[:, 0:n], in_=ln_rstd[:, 0:n], func=AF.Exp, scale=-0.5),
                       [r_ln], [r_ln])
                S.emit("dve", lambda e, c0=c0, n=n: e.tensor_tensor(
                    out=ln_t1[:, :, 0:n], in0=xres[:, :, c0:c0 + n],
                    in1=ln_mean[:, 0:n].unsqueeze(1).to_broadcast([128, 8, n]), op=ALU.subtract), [r_xres, r_ln], [r_ln])
                S.emit("dve", lambda e, n=n: e.tensor_tensor(
                    out=ln_t1[:, :, 0:n], in0=ln_t1[:, :, 0:n],
                    in1=ln_rstd[:, 0:n].unsqueeze(1).to_broadcast([128, 8, n]), op=ALU.mult), [r_ln], [r_ln])
                for kc in range(8):
                    S.emit("act", lambda e, kc=kc, c0=c0, n=n: e.activation(
                        out=xres[:, kc, c0:c0 + n], in_=ln_t1[:, kc, 0:n], func=AF.Identity,
                        scale=spv(goff + kc), bias=spv(boff + kc)), [r_ln, r_sp], [r_xres])
                S.emit("act", lambda e, c0=c0, n=n: e.activation(out=xbf[:, :, c0:c0 + n], in_=xres[:, :, c0:c0 + n], func=AF.Copy),
                       [r_xres], [r_xbf])

        def ffn(l, p):
            scr_phase("ffn")
            biga_phase("act")
            Wu = w_up[l]
            ncol = NCP + (NSC if p == 1 else 0)

            def bufs(i, half):
                return ffn_raw[half][i % 2], ffn_raws[half][i % 2], ffn_acc[half][i % 2]

            def stage1(i):
                wt, rw = ws.stage(8, [(0, 128, Wu[:, i * 128:(i + 1) * 128]),
                                      (128, 128, Wu[:, D_FF + i * 128:D_FF + (i + 1) * 128])])
                if S.dry:
                    return
                for half in range(2):
                    cidx = half * 22 + i
                    (rawv, rraw), (rawsv, rraws), (accv, racc) = bufs(i, half)
                    S.emit("dve", lambda e, cidx=cidx, rawv=rawv: e.tensor_copy(out=rawv[:, 1:3], in_=ff_carry[:, l, cidx, :]),
                           [r_ffc], [rraw])
                    if p == 1:
                        S.dma("sp", rawsv[:, NB:3 * NB].rearrange("p (r b) -> p r b", b=NB),
                              ffT[l, cidx * 128:(cidx + 1) * 128, :, :], [], [rraws], d_hf[half][i % 2])
                    for (c0, n) in coltiles(p):
                        pt, pr = ps_next()
                        mm_group(pt[:, 0:n], pr, [(wt[:, kc, half * 128:(half + 1) * 128], xbf[:, kc, c0:c0 + n])
                                                 for kc in range(8)], [r_xbf, rw])
                        dst = raw_dst(c0, n, rawv, rawsv)
                        S.emit("act", lambda e, pt=pt, dst=dst, c0=c0, n=n: e.activation(
                            out=dst, in_=ps_src(pt, c0, n), func=AF.Copy), [pr], [rraws if is_s(c0) else rraw])
                    conv_taps(p, accv, racc, rawv, rraw, rawsv, rraws, SP_CONVF + (l * 44 + cidx) * 3, 3, only="first")

            def stage2(i):
                for half in range(2):
                    cidx = half * 22 + i
                    (rawv, rraw), (rawsv, rraws), (accv, racc) = bufs(i, half)
                    conv_taps(p, accv, racc, rawv, rraw, rawsv, rraws, SP_CONVF + (l * 44 + cidx) * 3, 3, only="rest")
                    S.emit("dve", lambda e, cidx=cidx, rawv=rawv: e.tensor_copy(out=ff_carry[:, l, cidx, :],
                                                                                in_=rawv[:, 3 + NCP - 2:3 + NCP]), [rraw], [r_ffc])
                    if p == 1:
                        S.dma("sp", ffsT[l, cidx * 128:(cidx + 1) * 128, :, :],
                              rawsv[:, 5 * NB:7 * NB].rearrange("p (r b) -> p r b", b=NB), [rraws], [], d_ohf[half][i % 2])

            def stage3(i):
                (ag, rag) = ffn_acc[0][i % 2]
                (au, rau) = ffn_acc[1][i % 2]
                S.emit("act", lambda e: e.activation(out=ag[:, 0:ncol], in_=ag[:, 0:ncol], func=AF.Silu), [rag], [rag])
                S.emit("dve", lambda e: e.tensor_tensor(out=actb[:, i, 0:ncol], in0=ag[:, 0:ncol], in1=au[:, 0:ncol],
                                                        op=ALU.mult), [rag, rau], [r_actb])

            stage1(0)
            for i in range(22):
                if i + 1 < 22:
                    stage1(i + 1)
                if not S.dry:
                    stage2(i)
                    stage3(i)
            Wd = w_down[l]
            for oc in range(8):
                wt, rw = ws.stage(22, [(0, 128, Wd[:, oc * 128:(oc + 1) * 128])])
                if S.dry:
                    continue
                for (c0, n) in coltiles(p):
                    pt, pr = ps_next()
                    mm_group(pt[:, 0:n], pr, [(wt[:, kc, :], actb[:, kc, c0:c0 + n]) for kc in range(22)], [r_actb, rw])
                    S.emit("dve", lambda e, pt=pt, oc=oc, c0=c0, n=n: e.scalar_tensor_tensor(
                        out=xres[:, oc, c0:c0 + n], in0=xres[:, oc, c0:c0 + n], scalar=ALPHA, in1=pt[:, 0:n],
                        op0=ALU.mult, op1=ALU.add), [pr, r_xres], [r_xres])

        _mixer_b = mixer_b

        def mixer_b(l, p):
            cur_j[0] = l // 2
            _mixer_b(l, p)

        S.dry = True
        ws.planning = True
        program()
        S.dry = False
        ws.planning = False
        program()
        build.stats = dict(ninst=S.ninst, nstages=len(ws.plan), counts={k: v["cnt"] for k, v in S.eng.items()})
    return nc


def _consts():
    c = np.zeros((128, NCST), np.float32)
    c[:, C_ID:C_ID + 128] = np.eye(128, dtype=np.float32)
    c[:, C_ONE:C_ONE + 128] = 1.0
    k = np.arange(64)
    c[0:64, C_TRI:C_TRI + 64] = (k[:, None] <= k[None, :]).astype(np.float32)
    c[0:64, C_MASK:C_MASK + 64] = np.where(k[:, None] <= k[None, :], 0.0, NEG).astype(np.float32)
    c[0:64, C_STR:C_STR + 64] = (k[:, None] < k[None, :]).astype(np.float32)
    return c


def _small_params(conv_a, conv_b, w_conv_ffn, ln1_g, ln1_b, ln2_g, ln2_b, gdn_norm_w, a_log, dt_bias):
    sp = np.zeros((128, NSP), np.float32)

    def fm(w, nchunk):
        L, J, F = w.shape
        return np.ascontiguousarray(w.reshape(L, J, nchunk, 128).transpose(3, 0, 2, 1)).reshape(128, -1)

    sp[:, SP_CONVA:SP_CONVA + 36] = fm(conv_a, 6)
    sp[:, SP_CONVB:SP_CONVB + 144] = fm(conv_b, 18)
    sp[:, SP_CONVF:SP_CONVF + 528] = fm(w_conv_ffn, 44)
    for off, a in ((SP_LN1G, ln1_g), (SP_LN1B, ln1_b), (SP_LN2G, ln2_g), (SP_LN2B, ln2_b)):
        sp[:, off:off + 32] = a.reshape(DEPTH, 8, 128).transpose(2, 0, 1).reshape(128, 32)
    sp[:, SP_NORMW:SP_NORMW + 2] = gdn_norm_w.T
    sp[:, SP_ALOG:SP_ALOG + 12] = a_log.reshape(1, 12)
    sp[:, SP_DTB:SP_DTB + 12] = dt_bias.reshape(1, 12)
    return sp


def make_in_maps(inp, cores):
    f = lambda a: np.ascontiguousarray(np.asarray(a, dtype=np.float32))
    sp = _small_params(f(inp["conv_a"]), f(inp["conv_b"]), f(inp["w_conv_ffn"]), f(inp["ln1_g"]), f(inp["ln1_b"]),
                       f(inp["ln2_g"]), f(inp["ln2_b"]), f(inp["gdn_norm_w"]), f(inp["a_log"]), f(inp["dt_bias"]))
    cst = _consts()
    shared = {k: f(inp[k]) for k in ("w_in_a", "w_out_a", "w_in_b", "w_out_b", "w_mem_kv", "w_up", "w_down")}
    wbaT = f(np.asarray(inp["w_in_b"])[:, :, 3072:3084].reshape(2, 8, 128, 12).transpose(0, 2, 1, 3))
    maps = []
    for c in cores:
        b0, b1 = c * NB, (c + 1) * NB
        m = dict(shared)
        m["xpT"] = f(np.asarray(inp["x_prompt"][c]).T)
        m["xsT"] = f(np.asarray(inp["x_sample"][b0:b1]).reshape(NSC, D).T)
        m["memT"] = f(np.asarray(inp["mem_prompt"][c]).T)
        ck = np.asarray(inp["cache_mem_k"][:, b0:b1]).reshape(DEPTH, NB, NMEM, 256)
        m["ckT"] = f(ck.transpose(0, 3, 1, 2))
        m["cv"] = f(np.asarray(inp["cache_mem_v"][:, b0:b1]).reshape(DEPTH, NB, NMEM, 256))
        m["scT"] = f(np.asarray(inp["state_shortconv"][:, b0:b1]).transpose(0, 3, 2, 1))
        m["gcT"] = f(np.asarray(inp["state_gdn_conv"][:, b0:b1]).transpose(0, 3, 2, 1))
        m["gs"] = f(np.asarray(inp["state_gdn"][:, b0:b1]).transpose(0, 1, 3, 2, 4))
        m["wbaT"] = wbaT
        m["ffT"] = f(np.asarray(inp["state_ffn_conv"][:, b0:b1]).transpose(0, 3, 2, 1))
        m["spd"] = sp
        m["cstd"] = cst
        maps.append(m)
    return maps


def assemble(results, ncores):
    B = ncores
    y_p = np.stack([r["ypT"].T for r in results])
    y_s = np.concatenate([r["ysT"].T.reshape(NB, TS, D) for r in results])
    mk_ = np.stack([r["mk"] for r in results], axis=1).reshape(DEPTH, B, NMEM, 4, 64)
    mv_ = np.stack([r["mv"] for r in results], axis=1).reshape(DEPTH, B, NMEM, 4, 64)
    sc_p = np.stack([r["scpT"].transpose(0, 3, 2, 1).reshape(2, 2, SC_DIM) for r in results], axis=1)
    gc_p = np.stack([r["gcpT"].transpose(0, 3, 2, 1).reshape(2, 3, 2304) for r in results], axis=1)
    gs_p = np.stack([r["gsp"].transpose(0, 2, 1, 3) for r in results], axis=1)
    ff_p = np.stack([r["ffpT"].transpose(0, 3, 2, 1).reshape(DEPTH, 2, 2 * D_FF) for r in results], axis=1)
    sc_s = np.concatenate([r["scsT"].transpose(0, 3, 2, 1) for r in results], axis=1)
    gc_s = np.concatenate([r["gcsT"].transpose(0, 3, 2, 1) for r in results], axis=1)
    gs_s = np.concatenate([r["gss"].transpose(0, 1, 3, 2, 4) for r in results], axis=1)
    ff_s = np.concatenate([r["ffsT"].transpose(0, 3, 2, 1) for r in results], axis=1)
    outs = (y_p, y_s, mk_, mv_, sc_p, gc_p, gs_p, ff_p, sc_s, gc_s, gs_s, ff_s)
    return tuple(np.ascontiguousarray(o, dtype=np.float32) for o in outs)


def kernel(**inputs):
    nc = build()
    maps = make_in_maps(inputs, list(range(8)))
    res = run_bass_kernel_spmd(nc, maps, core_ids=list(range(8)))
    return assemble(res.results, 8)
```

```python
import contextlib
import os
import numpy as np
import concourse.bass as bass
import concourse.mybir as mybir
from concourse.bass_utils import run_bass_kernel_spmd

F32 = mybir.dt.float32
BF16 = mybir.dt.bfloat16
AF = mybir.ActivationFunctionType
ALU = mybir.AluOpType
AX = mybir.AxisListType

D = 1024
SEQ = 2048
NCP = 1024
NSC = 64
W = NCP + NSC
NB = 16
TS = 4
DEPTH = 4
SC_DIM = 768
GDN_H = 6
D_FF = 2816
NMEM = 256
ALPHA = (2.0 * DEPTH) ** 0.25
LN_EPS = 1e-5
RMS_EPS = 1e-6
NEG = -30000.0

SP_CONVA = 0
SP_CONVB = SP_CONVA + 36
SP_CONVF = SP_CONVB + 144
SP_LN1G = SP_CONVF + 528
SP_LN1B = SP_LN1G + 32
SP_LN2G = SP_LN1B + 32
SP_LN2B = SP_LN2G + 32
SP_NORMW = SP_LN2B + 32
SP_ALOG = SP_NORMW + 2
SP_DTB = SP_ALOG + 12
NSP = SP_DTB + 12
C_ID = 0
C_ONE = 128
C_TRI = 256
C_MASK = 320
C_STR = 384
NCST = 448


class Res:
    __slots__ = ("name", "lw", "rd", "excl")

    def __init__(self, name, excl=False):
        self.name = name
        self.lw = None
        self.rd = {}
        self.excl = excl


class DSem:
    def __init__(self, name, sem):
        self.name = name
        self.sem = sem
        self.cnt = 0


def fence(new, olds):
    for n in new:
        for o in olds:
            if o.lw is not None and n.rd.get(o.lw[0], 0) < o.lw[1]:
                n.rd[o.lw[0]] = o.lw[1]
            for s, v in o.rd.items():
                if n.rd.get(s, 0) < v:
                    n.rd[s] = v


class Sched:
    def __init__(self, nc, es):
        self.nc = nc
        self.es = es
        self.eng = {}
        self.sems = {}
        self.dry = False
        for name, h in [("pe", nc.tensor), ("act", nc.scalar), ("dve", nc.vector),
                        ("pool", nc.gpsimd), ("sp", nc.sync)]:
            sem = es.enter_context(nc.semaphore("sem_" + name))
            self.eng[name] = dict(h=h, sem=sem, cnt=0, waited={})
            self.sems[name] = sem
        self.ndsem = 0
        self.dsems = {}
        self.ninst = 0

    def dsem(self):
        name = f"dsem{self.ndsem}"
        self.ndsem += 1
        sem = self.es.enter_context(self.nc.semaphore(name))
        self.sems[name] = sem
        d = DSem(name, sem)
        self.dsems[name] = d
        return d

    def _waits(self, en, reads, writes):
        e = self.eng[en]
        deps = {}
        for r in reads:
            if r.lw is not None and deps.get(r.lw[0], 0) < r.lw[1]:
                deps[r.lw[0]] = r.lw[1]
        for w in writes:
            if w.lw is not None and deps.get(w.lw[0], 0) < w.lw[1]:
                deps[w.lw[0]] = w.lw[1]
            for s, v in w.rd.items():
                if deps.get(s, 0) < v:
                    deps[s] = v
        for s, v in deps.items():
            if en == "pe" and s == "pe":
                continue
            if s in self.dsems:
                v = self.dsems[s].cnt
            if e["waited"].get(s, 0) < v:
                e["h"].wait_ge(self.sems[s], v)
                e["waited"][s] = v

    def emit(self, en, fn, reads=(), writes=()):
        if self.dry:
            return None
        e = self.eng[en]
        if any(r.excl for r in reads):
            writes = list(writes) + [r for r in reads if r.excl]
            reads = [r for r in reads if not r.excl]
        self._waits(en, reads, writes)
        ins = fn(e["h"])
        e["cnt"] += 1
        ins.then_inc(e["sem"], 1)
        c = e["cnt"]
        for r in reads:
            if r.rd.get(en, 0) < c:
                r.rd[en] = c
        for w in writes:
            w.lw = (en, c)
            w.rd = {}
        self.ninst += 1
        return ins

    def dma(self, qn, out, in_, reads, writes, ds):
        if self.dry:
            return None
        e = self.eng[qn]
        self._waits(qn, reads, writes)
        ins = e["h"].dma_start(out=out, in_=in_)
        ds.cnt += 16
        ins.then_inc(ds.sem, 16)
        for r in reads:
            if r.rd.get(ds.name, 0) < ds.cnt:
                r.rd[ds.name] = ds.cnt
        for w in writes:
            w.lw = (ds.name, ds.cnt)
            w.rd = {}
        self.ninst += 1
        return ins


GDN_PIPE = os.environ.get("GDN_PIPE", "1") == "1"
CONV_ACT = os.environ.get("CONV_ACT", "1") == "1"
KPARTS = os.environ.get("KPARTS", "att,att2,mix,out,ln,ffn").split(",")


def build(nlayers=DEPTH, npass=2):
    nc = bass.Bass("TRN2", target_bir_lowering=False)

    def din(name, shape):
        return nc.dram_tensor(name, list(shape), F32, kind="ExternalInput").ap()

    def dout(name, shape):
        return nc.dram_tensor(name, list(shape), F32, kind="ExternalOutput").ap()

    xpT = din("xpT", [D, SEQ])
    xsT = din("xsT", [D, NSC])
    memT = din("memT", [D, NMEM])
    ckT = din("ckT", [DEPTH, 256, NB, NMEM])
    cv = din("cv", [DEPTH, NB, NMEM, 256])
    scT = din("scT", [2, SC_DIM, 2, NB])
    gcT = din("gcT", [2, 2304, 3, NB])
    gs = din("gs", [2, NB, 128, GDN_H, 128])
    wbaT = din("wbaT", [2, 128, 8, 12])
    ffT = din("ffT", [DEPTH, 2 * D_FF, 2, NB])
    w_in_a = din("w_in_a", [2, D, 2560])
    w_out_a = din("w_out_a", [2, D, D])
    w_in_b = din("w_in_b", [2, D, 3340])
    w_out_b = din("w_out_b", [2, D, D])
    w_mem_kv = din("w_mem_kv", [DEPTH, D, 512])
    w_up = din("w_up", [DEPTH, D, 2 * D_FF])
    w_down = din("w_down", [DEPTH, D_FF, D])
    spd = din("spd", [128, NSP])
    cstd = din("cstd", [128, NCST])

    ypT = dout("ypT", [D, SEQ])
    ysT = dout("ysT", [D, NSC])
    mk = dout("mk", [DEPTH, NMEM, 256])
    mv = dout("mv", [DEPTH, NMEM, 256])
    scpT = dout("scpT", [2, 128, 6, 2])
    gcpT = dout("gcpT", [2, 128, 18, 3])
    gsp = dout("gsp", [2, 128, GDN_H, 128])
    ffpT = dout("ffpT", [DEPTH, 128, 44, 2])
    scsT = dout("scsT", [2, SC_DIM, 2, NB])
    gcsT = dout("gcsT", [2, 2304, 3, NB])
    gss = dout("gss", [2, NB, 128, GDN_H, 128])
    ffsT = dout("ffsT", [DEPTH, 2 * D_FF, 2, NB])

    es = contextlib.ExitStack()
    with es:
        S = Sched(nc, es)

        def sb(name, shape, dt=F32):
            return es.enter_context(nc.sbuf_tensor(name, list(shape), dt))

        xres = sb("xres", [128, 8, W]); r_xres = Res("xres")
        xbf = sb("xbf", [128, 8, W], BF16); r_xbf = Res("xbf")
        obuf = sb("obuf", [128, 8, W], BF16); r_obuf = Res("obuf")
        cst = sb("cst", [128, NCST]); r_cst = Res("cst")
        spt = sb("spt", [128, NSP]); r_sp = Res("sp")
        mask6 = sb("mask6", [64, 6, 64]); r_mask6 = Res("mask6")
        identb = sb("identb", [128, 128], BF16)
        onesb = sb("onesb", [128, 128], BF16)
        meanb = sb("meanb", [128, 128], BF16)
        r_cb = Res("constb")
        sc_carry = sb("sc_carry", [128, 2, 6, 2]); r_scc = Res("scc")
        gc_carry = sb("gc_carry", [128, 2, 18, 3]); r_gcc = Res("gcc")
        ff_carry = sb("ff_carry", [128, DEPTH, 44, 2]); r_ffc = Res("ffc")
        Sst = sb("Sst", [128, 2, GDN_H, 128]); r_Sst = [Res("Sst0"), Res("Sst1")]
        wba = sb("wba", [128, 8, 12], BF16); r_wba = Res("wba")
        negA = sb("negA", [128, 6]); r_negA = Res("negA")
        g_beta = sb("g_beta", [64, 96]); g_g = sb("g_g", [64, 96]); g_gc = sb("g_gc", [64, 96])
        g_egc = sb("g_egc", [64, 96]); g_negegc = sb("g_negegc", [64, 96]); g_kds = sb("g_kds", [64, 96])
        g_tmp = sb("g_tmp", [64, 96]); g_gam = sb("g_gam", [128, 96])
        r_gsc = Res("gsc")
        NSLOT = 3
        wslots = [(sb(f"wslot{i}", [128, 4096], BF16), Res(f"wslot{i}"), S.dsem()) for i in range(NSLOT)]
        BIGA = sb("BIGA", [128, 24 * W], BF16)
        SCRN = 10240
        SCR = sb("SCR", [128, SCRN])

        psum = [(es.enter_context(nc.psum_tensor(f"ps{i}", [128, 512], F32)), Res(f"ps{i}", True)) for i in range(7)]
        psbf = (es.enter_context(nc.psum_tensor("psbf", [128, 1024], BF16)), Res("psbf", True))
        psi = [0]

        psg = {"A": [0, (0, 1, 2)], "B": [0, (3, 4, 5, 6)]}

        def ps_next(which=None):
            if which is not None:
                st = psg[which]
                b = psum[st[1][st[0] % len(st[1])]]
                st[0] += 1
                return b
            b = psum[psi[0] % 7]
            psi[0] += 1
            return b

        d_in = S.dsem()
        d_x = S.dsem()
        d_h = [S.dsem(), S.dsem()]
        d_ss = [S.dsem(), S.dsem()]
        d_memb = S.dsem(); d_ckv = S.dsem(); d_wba = S.dsem(); d_kvn = S.dsem()
        d_oh = [S.dsem(), S.dsem()]; d_so = [S.dsem(), S.dsem()]; d_y = S.dsem(); d_fin = S.dsem()
        out_dsems = [d_kvn, d_oh[0], d_oh[1], d_so[0], d_so[1], d_y, d_fin]

        ident = cst[:, C_ID:C_ID + 128]
        ones = cst[:, C_ONE:C_ONE + 128]
        tri = cst[:, C_TRI:C_TRI + 64]
        maskT = cst[:, C_MASK:C_MASK + 64]
        strU = cst[:, C_STR:C_STR + 64]

        def spv(off, n=1):
            return spt[:, off:off + n]

        qkvz = BIGA[:, :].rearrange("p (c w) -> p c w", c=24)
        r_qkvz = Res("qkvz")
        actb = BIGA[:, 0:22 * W].rearrange("p (c w) -> p c w", c=22)
        r_actb = Res("actb")
        ln_zb = BIGA[:, 0:4096].rearrange("p (k n) -> p k n", k=8)
        ln_sqb = BIGA[:, 4096:8192].rearrange("p (k n) -> p k n", k=8)
        ln_f32 = BIGA[:, 8192:8192 + 2 * (4096 + 1024)].bitcast(F32)
        ln_t1 = ln_f32[:, 0:4096].rearrange("p (k n) -> p k n", k=8)
        ln_mean = ln_f32[:, 4096:4608]
        ln_rstd = ln_f32[:, 4608:5120]
        r_ln = Res("ln")
        ckb = BIGA[:, 0:8192].rearrange("p (c b m) -> p c b m", c=2, b=NB)
        cvb = BIGA[:, 8192:16384].rearrange("p (c b m) -> p c b m", c=2, b=NB)
        r_ckv = Res("ckv")
        biga_groups = {"qkvz": [r_qkvz], "act": [r_actb], "ln": [r_ln], "ckv": [r_ckv]}
        biga_cur = [None]

        def biga_phase(name):
            if biga_cur[0] is not None and biga_cur[0] != name:
                fence(biga_groups[name], biga_groups[biga_cur[0]])
            biga_cur[0] = name

        scr_res = {}

        def scrv(phase, name, off, n):
            key = (phase, name)
            if key not in scr_res:
                scr_res[key] = Res(f"scr_{phase}_{name}")
            assert off + n <= SCRN, (phase, name, off, n)
            return SCR[:, off:off + n], scr_res[key]

        scr_cur = [None]

        def scr_phase(name):
            if scr_cur[0] is not None and scr_cur[0] != name:
                new = [r for (ph, _), r in scr_res.items() if ph == name]
                old = [r for (ph, _), r in scr_res.items() if ph == scr_cur[0]]
                fence(new, old)
            scr_cur[0] = name

        RAWN = 3 + NCP + 1
        conv_raw = [scrv("conv", f"raw{i}", i * RAWN, RAWN) for i in range(2)]
        o = 2 * RAWN
        conv_raws = [scrv("conv", f"raws{i}", o + i * 112, 112) for i in range(2)]
        o += 224
        conv_acc = [scrv("conv", f"acc{i}", o + i * W, W) for i in range(2)]
        o += 2 * W
        conv_tmp = [scrv("conv", f"tmp{i}", o + i * 512, 512) for i in range(2)]
        o += 1024
        conv_bgb = [scrv("conv", f"bgb{i}", o + i * W, W) for i in range(2)]
        o += 2 * W
        conv_sq = scrv("conv", "sq", o, W // 2)
        o += W // 2
        conv_rs = scrv("conv", "rs", o, 512)
        o += 512
        assert o <= SCRN, o
        o = 0
        ffn_raw = [[None, None], [None, None]]; ffn_raws = [[None, None], [None, None]]; ffn_acc = [[None, None], [None, None]]
        for hf in range(2):
            for pr_ in range(2):
                ffn_raw[hf][pr_] = scrv("ffn", f"raw{hf}{pr_}", o, RAWN); o += RAWN
                ffn_raws[hf][pr_] = scrv("ffn", f"raws{hf}{pr_}", o, 112); o += 112
                ffn_acc[hf][pr_] = scrv("ffn", f"acc{hf}{pr_}", o, W); o += W
        assert o <= SCRN, o
        d_hf = [[S.dsem(), S.dsem()], [S.dsem(), S.dsem()]]
        d_ohf = [[S.dsem(), S.dsem()], [S.dsem(), S.dsem()]]
        out_dsems += [d_ohf[0][0], d_ohf[0][1], d_ohf[1][0], d_ohf[1][1]]
        o = 0
        att_qbuf = scrv("att", "qbuf", o, W); o += W
        att_e = [scrv("att", f"e{i}", o + i * 256, 256) for i in range(2)]; o += 512
        att_rden = scrv("att", "rden", o, 512); o += 512
        att_kvn = scrv("att", "kvn", o, 512); o += 512
        att_KT = scrv("att", "KT", o, 256); o += 256
        att_V = scrv("att", "V", o, 256); o += 256
        att_memb = scrv("att", "memb", o, 1024); o += 1024
        att_es = scrv("att", "es", o, 256); o += 256
        o = 0
        gd = {}
        for nm, n in [("gtri", 384), ("dT", 384), ("U", 192), ("Pa", 192), ("Pb", 192), ("PTa", 192), ("PTb", 192),
                      ("Va", 192), ("Vb", 192),
                      ("Vf0", 192), ("Vf1", 192), ("qkT0", 192), ("qkT1", 192), ("kdec0", 384), ("kdec1", 384),
                      ("vtm0", 384), ("vtm1", 384),
                      ("Y", 384), ("vnew", 384), ("ot", 768), ("sq", 768), ("ssq", 16),
                      ("Ss0", 768), ("Ss1", 768), ("Ssb0", 384), ("Ssb1", 384), ("Spb", 384)]:
            gd[nm] = scrv("gdn", nm, o, n)
            o += n
        assert o <= SCRN, o

        class WS:
            def __init__(self):
                self.plan = []
                self.issued = 0
                self.popped = 0
                self.planning = True
                self.PREF = 2

            def _issue(self, s):
                tile_, res, ds = wslots[s % NSLOT]
                KC, parts = self.plan[s]
                nw = sum(n for _, n, _ in parts)
                v = tile_[:, 0:KC * nw].rearrange("p (k n) -> p k n", k=KC)
                for off, n, src in parts:
                    S.dma("pool", v[:, :, off:off + n], src.rearrange("(k p) n -> p k n", p=128), [], [res], ds)

            def stage(self, KC, parts):
                if self.planning:
                    self.plan.append((KC, parts))
                    return None, None
                while self.issued < min(len(self.plan), self.popped + self.PREF + 1):
                    self._issue(self.issued)
                    self.issued += 1
                s = self.popped
                self.popped += 1
                KCp, partsp = self.plan[s]
                assert KCp == KC and len(partsp) == len(parts)
                tile_, res, ds = wslots[s % NSLOT]
                nw = sum(n for _, n, _ in parts)
                return tile_[:, 0:KC * nw].rearrange("p (k n) -> p k n", k=KC), res

        ws = WS()

        def mm_group(out_ap, pres, pairs, reads):
            def fn(e):
                ins = None
                n = len(pairs)
                for i, (l, r) in enumerate(pairs):
                    ins = e.matmul(out_ap, lhsT=l, rhs=r, start=(i == 0), stop=(i == n - 1))
                return ins
            S.emit("pe", fn, reads, [pres])

        def mm_multi(items, pres_list, reads):
            def fn(e):
                ins = None
                for out_ap, pairs in items:
                    n = len(pairs)
                    for i, (l, r) in enumerate(pairs):
                        ins = e.matmul(out_ap, lhsT=l, rhs=r, start=(i == 0), stop=(i == n - 1))
                return ins
            S.emit("pe", fn, reads, pres_list)

        def tr_multi(items, pres_list, reads):
            def fn(e):
                ins = None
                for out_ap, in_ap, id_ap in items:
                    ins = e.transpose(out_ap, in_ap, id_ap)
                return ins
            S.emit("pe", fn, reads, pres_list)

        def coltiles(p):
            ct = [(0, 512), (512, 512)]
            if p == 1:
                ct.append((NCP, NSC))
            return ct

        def is_s(c0):
            return c0 >= NCP

        def program():
            psi[0] = 0
            biga_cur[0] = None
            scr_cur[0] = None
            S.dma("sp", cst[:, :], cstd[:, :], [], [r_cst], d_in)
            S.dma("sp", spt[:, :], spd[:, :], [], [r_sp], d_in)
            S.emit("dve", lambda e: e.tensor_copy(out=identb[:, :], in_=ident), [r_cst], [r_cb])
            S.emit("dve", lambda e: e.tensor_copy(out=onesb[:, :], in_=ones), [r_cst], [r_cb])
            S.emit("dve", lambda e: e.tensor_scalar(out=meanb[:, :], in0=ones, scalar1=1.0 / D, scalar2=None,
                                                    op0=ALU.mult), [r_cst], [r_cb])
            S.emit("dve", lambda e: e.tensor_copy(
                out=mask6[:, :, :], in_=maskT[0:64, :].unsqueeze(1).to_broadcast([64, 6, 64])), [r_cst], [r_mask6])
            S.emit("dve", lambda e: e.memset(obuf[:, :, :], 0.0), [], [r_obuf])
            S.emit("dve", lambda e: e.memset(sc_carry[:, :, :, :], 0.0), [], [r_scc])
            S.emit("dve", lambda e: e.memset(gc_carry[:, :, :, :], 0.0), [], [r_gcc])
            S.emit("dve", lambda e: e.memset(ff_carry[:, :, :, :], 0.0), [], [r_ffc])
            for j in range(2):
                S.emit("dve", lambda e, j=j: e.memset(Sst[:, j, :, :], 0.0), [], [r_Sst[j]])

            for p in range(npass):
                ct = coltiles(p)
                ncol = NCP + (NSC if p == 1 else 0)
                S.dma("sp", xres[:, :, 0:NCP], xpT[:, p * NCP:(p + 1) * NCP].rearrange("(k q) t -> q k t", q=128),
                      [], [r_xres], d_x)
                if p == 1:
                    S.dma("sp", xres[:, :, NCP:W], xsT[:, :].rearrange("(k q) t -> q k t", q=128), [], [r_xres], d_x)
                for kc in range(8):
                    en = ("act", "dve")[kc % 2]
                    if en == "act":
                        S.emit("act", lambda e, kc=kc: e.activation(out=xbf[:, kc, 0:ncol], in_=xres[:, kc, 0:ncol],
                                                                  func=AF.Copy), [r_xres], [r_xbf])
                    else:
                        S.emit(en, lambda e, kc=kc: e.tensor_copy(out=xbf[:, kc, 0:ncol], in_=xres[:, kc, 0:ncol]),
                               [r_xres], [r_xbf])
                for l in range(nlayers):
                    if l % 2 == 0:
                        mixer_a(l, p)
                    else:
                        mixer_b(l, p)
                    if "ln" in KPARTS:
                        layernorm(SP_LN1G + l * 8, SP_LN1B + l * 8, p)
                    if "ffn" in KPARTS:
                        ffn(l, p)
                    if "ln" in KPARTS:
                        layernorm(SP_LN2G + l * 8, SP_LN2B + l * 8, p)
                S.dma("sp", ypT[:, p * NCP:(p + 1) * NCP].rearrange("(k q) t -> q k t", q=128), xres[:, :, 0:NCP],
                      [r_xres], [], d_y)
                if p == 1:
                    S.dma("sp", ysT[:, :].rearrange("(k q) t -> q k t", q=128), xres[:, :, NCP:W], [r_xres], [], d_y)
            for j in range(2):
                S.dma("sp", scpT[j], sc_carry[:, j, :, :], [r_scc], [], d_fin)
                S.dma("sp", gcpT[j], gc_carry[:, j, :, :], [r_gcc], [], d_fin)
                S.dma("sp", gsp[j], Sst[:, j, :, :], [r_Sst[j]], [], d_fin)
            for l in range(DEPTH):
                S.dma("sp", ffpT[l], ff_carry[:, l, :, :], [r_ffc], [], d_fin)
            if not S.dry:
                for ds in out_dsems:
                    if ds.cnt > 0:
                        S.eng["sp"]["h"].wait_ge(ds.sem, ds.cnt)

        def conv_taps(p, accv, racc, rawv, rraw, rawsv, rraws, woff, Wd, only=None):
            Hh = Wd - 1
            jjs = [jj for jj in range(Wd) if only is None or (only == "first") == (jj == 0)]
            a_p = accv[:, 0:NCP]
            for jj in jjs:
                src = rawv[:, 3 - Hh + jj:3 - Hh + jj + NCP]
                wap = spv(woff + jj)
                if jj == 0 and CONV_ACT:
                    S.emit("act", lambda e, src=src, wap=wap: e.activation(
                        out=a_p, in_=src, func=AF.Copy, scale=wap), [rraw, r_sp], [racc])
                elif jj == 0:
                    S.emit("dve", lambda e, src=src, wap=wap: e.tensor_scalar(
                        out=a_p, in0=src, scalar1=wap, scalar2=None, op0=ALU.mult), [rraw, r_sp], [racc])
                else:
                    S.emit("dve", lambda e, src=src, wap=wap: e.scalar_tensor_tensor(
                        out=a_p, in0=src, scalar=wap, in1=a_p, op0=ALU.mult, op1=ALU.add), [rraw, r_sp, racc], [racc])
            if p == 1:
                a_s = accv[:, NCP:W].rearrange("p (b t) -> p b t", b=NB)
                rs3 = rawsv.rearrange("p (t b) -> p b t", b=NB)
                for jj in jjs:
                    src = rs3[:, :, 3 - Hh + jj:3 - Hh + jj + TS]
                    wap = spv(woff + jj)
                    if jj == 0 and CONV_ACT:
                        S.emit("act", lambda e, src=src, wap=wap: e.activation(
                            out=a_s, in_=src, func=AF.Copy, scale=wap), [rraws, r_sp], [racc])
                    elif jj == 0:
                        S.emit("dve", lambda e, src=src, wap=wap: e.tensor_scalar(
                            out=a_s, in0=src, scalar1=wap, scalar2=None, op0=ALU.mult), [rraws, r_sp], [racc])
                    else:
                        S.emit("dve", lambda e, src=src, wap=wap: e.scalar_tensor_tensor(
                            out=a_s, in0=src, scalar=wap, in1=a_s, op0=ALU.mult, op1=ALU.add),
                            [rraws, r_sp, racc], [racc])

        def raw_dst(c0, n, rawv, rawsv):
            if is_s(c0):
                return rawsv.rearrange("p (t b) -> p b t", b=NB)[:, :, 3:3 + TS]
            return rawv[:, 3 + c0:3 + c0 + n]

        def ps_src(psap, c0, n):
            if is_s(c0):
                return psap[:, 0:NSC].rearrange("p (b t) -> p b t", b=NB)
            return psap[:, 0:n]

        def attention(l, p, w_in, qcol0):
            scr_phase("att")
            qbuf, r_q = att_qbuf
            qb = qbuf.bitcast(BF16).rearrange("p (c w) -> p c w", c=2)
            KT = att_KT[0].bitcast(BF16).rearrange("p (c m) -> p c m", c=2); r_KT = att_KT[1]
            Vt = att_V[0].bitcast(BF16).rearrange("p (c m) -> p c m", c=2); r_V = att_V[1]
            memb = att_memb[0].bitcast(BF16).rearrange("p (k m) -> p k m", k=8); r_memb = att_memb[1]
            kvn, r_kvn = att_kvn
            S.dma("pool", memb, memT[:, :].rearrange("(k q) m -> q k m", q=128), [], [r_memb], d_memb)
            wt, rw = ws.stage(8, [(0, 512, w_mem_kv[l])])
            if not S.dry:
                for mc in range(2):
                    pt, pr = ps_next()
                    mm_group(pt[:, :], pr, [(memb[:, kc, mc * 128:(mc + 1) * 128], wt[:, kc, :]) for kc in range(8)],
                             [r_memb, rw])
                    S.emit("dve", lambda e, pt=pt, mc=mc: e.tensor_copy(out=Vt[:, mc, :], in_=pt[:, 256:512]), [pr], [r_V])
                    if p == 0:
                        S.emit("act", lambda e, pt=pt: e.activation(out=kvn, in_=pt[:, :], func=AF.Copy), [pr], [r_kvn])
                        S.dma("sp", mk[l, mc * 128:(mc + 1) * 128, :], kvn[:, 0:256], [r_kvn], [], d_kvn)
                        S.dma("sp", mv[l, mc * 128:(mc + 1) * 128, :], kvn[:, 256:512], [r_kvn], [], d_kvn)
                for c in range(2):
                    pt, pr = ps_next()
                    mm_group(pt[:, 0:256], pr, [(wt[:, kc, c * 128:(c + 1) * 128], memb[:, kc, :]) for kc in range(8)],
                             [r_memb, rw])
                    S.emit("act", lambda e, pt=pt, c=c: e.activation(out=KT[:, c, :], in_=pt[:, 0:256], func=AF.Copy),
                           [pr], [r_KT])
            if p == 1:
                biga_phase("ckv")
                for c_ in range(2):
                    S.dma("pool", ckb[:, c_, :, :], ckT[l, c_ * 128:(c_ + 1) * 128, :, :], [], [r_ckv], d_ckv)
                    S.dma("pool", cvb[:, c_, :, :], cv[l, :, c_ * 128:(c_ + 1) * 128, :].rearrange("b q e -> q b e"), [], [r_ckv], d_ckv)
            wt, rw = ws.stage(8, [(0, 256, w_in[:, qcol0:qcol0 + 256])])
            if S.dry:
                return
            for c in range(2):
                for (c0, n) in coltiles(p):
                    pt, pr = ps_next()
                    mm_group(pt[:, 0:n], pr, [(wt[:, kc, c * 128:(c + 1) * 128], xbf[:, kc, c0:c0 + n]) for kc in range(8)],
                             [r_xbf, rw])
                    S.emit("act", lambda e, pt=pt, c=c, c0=c0, n=n: e.activation(
                        out=qb[:, c, c0:c0 + n], in_=pt[:, 0:n], func=AF.Copy, scale=0.125), [pr], [r_q])
            rden, r_rden = att_rden
            for h in range(4 if "att2" in KPARTS else 0):
                c = h // 2
                pb = (h % 2) * 64
                for (c0, n) in ((0, 512), (512, 512)):
                    evs = []
                    for mc in range(2):
                        pt, pr = ps_next()
                        mm_group(pt[:, 0:n], pr, [(KT[pb:pb + 64, c, mc * 128:(mc + 1) * 128], qb[pb:pb + 64, c, c0:c0 + n])],
                                 [r_KT, r_q])
                        ev, r_ev = att_e[mc]
                        evb = ev.bitcast(BF16)
                        S.emit("act", lambda e, pt=pt, evb=evb, n=n: e.activation(out=evb[:, 0:n], in_=pt[:, 0:n], func=AF.Exp),
                               [pr], [r_ev])
                        evs.append((evb, r_ev))
                    pso, pro = ps_next()
                    mm_group(pso[pb:pb + 64, 0:n], pro,
                             [(Vt[:, mc, h * 64:(h + 1) * 64], evs[mc][0][:, 0:n]) for mc in range(2)],
                             [r_V, evs[0][1], evs[1][1]])
                    psd, prd = ps_next()
                    mm_group(psd[pb:pb + 64, 0:n], prd, [(onesb[:, 0:64], evs[mc][0][:, 0:n]) for mc in range(2)],
                             [r_cb, evs[0][1], evs[1][1]])
                    S.emit("dve", lambda e, psd=psd, pb=pb, n=n: e.reciprocal(out=rden[pb:pb + 64, 0:n], in_=psd[pb:pb + 64, 0:n]),
                           [prd], [r_rden])
                    S.emit("dve", lambda e, pso=pso, pb=pb, n=n, c=c, c0=c0: e.tensor_tensor(
                        out=obuf[pb:pb + 64, 6 + c, c0:c0 + n], in0=pso[pb:pb + 64, 0:n], in1=rden[pb:pb + 64, 0:n],
                        op=ALU.mult), [pro, r_rden], [r_obuf])
            if p == 1:
                esb = att_es[0].bitcast(BF16); r_es = att_es[1]
                pt, pr = ps_next()
                items = []
                for b in range(NB):
                    for h in range(4):
                        c = h // 2
                        pb = (h % 2) * 64
                        for mc in range(2):
                            idx = ((b * 4 + h) * 2 + mc) * TS
                            items.append((pt[:, idx:idx + TS],
                                          [(ckb[pb:pb + 64, c, b, mc * 128:(mc + 1) * 128],
                                            qb[pb:pb + 64, c, NCP + b * TS:NCP + (b + 1) * TS])]))
                mm_multi(items, [pr], [r_ckv, r_q])
                S.emit("act", lambda e: e.activation(out=esb[:, :], in_=pt[:, :], func=AF.Exp), [pr], [r_es])
                pso, pro = ps_next()
                psd, prd = ps_next()
                items = []
                for b in range(NB):
                    for h in range(4):
                        c = h // 2
                        pb = (h % 2) * 64
                        oc0 = c * NSC + b * TS
                        prs_o = []
                        prs_d = []
                        for mc in range(2):
                            idx = ((b * 4 + h) * 2 + mc) * TS
                            prs_o.append((cvb[:, mc, b, h * 64:(h + 1) * 64], esb[:, idx:idx + TS]))
                            prs_d.append((onesb[:, 0:64], esb[:, idx:idx + TS]))
                        items.append((pso[pb:pb + 64, oc0:oc0 + TS], prs_o))
                        items.append((psd[pb:pb + 64, oc0:oc0 + TS], prs_d))
                mm_multi(items, [pro, prd], [r_ckv, r_es, r_cb])
                S.emit("dve", lambda e: e.reciprocal(out=rden[:, 0:128], in_=psd[:, 0:128]), [prd], [r_rden])
                S.emit("dve", lambda e: e.tensor_tensor(
                    out=obuf[:, 6:8, NCP:W], in0=pso[:, 0:128].rearrange("p (c t) -> p c t", c=2),
                    in1=rden[:, 0:128].rearrange("p (c t) -> p c t", c=2), op=ALU.mult), [pro, r_rden], [r_obuf])

        def out_proj(w_out, p):
            for st in range(2):
                wt, rw = ws.stage(8, [(0, 512, w_out[:, st * 512:(st + 1) * 512])])
                if S.dry:
                    continue
                for cc in range(4):
                    oc = st * 4 + cc
                    for (c0, n) in coltiles(p):
                        pt, pr = ps_next()
                        mm_group(pt[:, 0:n], pr, [(wt[:, kc, cc * 128:(cc + 1) * 128], obuf[:, kc, c0:c0 + n]) for kc in range(8)],
                                 [r_obuf, rw])
                        S.emit("dve", lambda e, pt=pt, oc=oc, c0=c0, n=n: e.scalar_tensor_tensor(
                            out=xres[:, oc, c0:c0 + n], in0=xres[:, oc, c0:c0 + n], scalar=ALPHA, in1=pt[:, 0:n],
                            op0=ALU.mult, op1=ALU.add), [pr, r_xres], [r_xres])

        def mixer_a(l, p):
            j = l // 2
            Wm = w_in_a[j]
            if "att" in KPARTS:
                attention(l, p, Wm, 3 * SC_DIM)
            scr_phase("conv")
            ncol = NCP + (NSC if p == 1 else 0)

            def stage1(c):
                wt, rw = ws.stage(8, [(0, 128, Wm[:, c * 128:(c + 1) * 128]),
                                      (128, 128, Wm[:, SC_DIM + c * 128:SC_DIM + (c + 1) * 128]),
                                      (256, 128, Wm[:, 2 * SC_DIM + c * 128:2 * SC_DIM + (c + 1) * 128])])
                if S.dry:
                    return
                rawv, rraw = conv_raw[c % 2]
                rawsv, rraws = conv_raws[c % 2]
                accv, racc = conv_acc[c % 2]
                bgb, rbg = conv_bgb[c % 2]
                S.emit("dve", lambda e, c=c: e.tensor_copy(out=rawv[:, 1:3], in_=sc_carry[:, j, c, :]), [r_scc], [rraw])
                if p == 1:
                    S.dma("sp", rawsv[:, NB:3 * NB].rearrange("p (r b) -> p r b", b=NB), scT[j, c * 128:(c + 1) * 128, :, :],
                          [], [rraws], d_h[c % 2])
                for ti, (c0, n) in enumerate(coltiles(p)):
                    pss = []
                    for part in range(3):
                        pt, pr = ps_next()
                        mm_group(pt[:, 0:n], pr, [(wt[:, kc, part * 128:(part + 1) * 128], xbf[:, kc, c0:c0 + n])
                                                 for kc in range(8)], [r_xbf, rw])
                        pss.append((pt, pr))
                    tmpv, rtmp = conv_tmp[ti % 2]
                    S.emit("act", lambda e, pt=pss[0][0], n=n: e.activation(out=tmpv[:, 0:n], in_=pt[:, 0:n], func=AF.Copy),
                           [pss[0][1]], [rtmp])
                    dst = raw_dst(c0, n, rawv, rawsv)
                    in1 = tmpv[:, 0:NSC].rearrange("p (b t) -> p b t", b=NB) if is_s(c0) else tmpv[:, 0:n]
                    S.emit("dve", lambda e, pt=pss[2][0], dst=dst, in1=in1, c0=c0, n=n: e.tensor_tensor(
                        out=dst, in0=ps_src(pt, c0, n), in1=in1, op=ALU.mult),
                        [pss[2][1], rtmp], [rraws if is_s(c0) else rraw])
                    S.emit("act", lambda e, pt=pss[1][0], c0=c0, n=n: e.activation(out=bgb[:, c0:c0 + n], in_=pt[:, 0:n],
                                                                                 func=AF.Copy), [pss[1][1]], [rbg])
                conv_taps(p, accv, racc, rawv, rraw, rawsv, rraws, SP_CONVA + (j * 6 + c) * 3, 3, only="first")

            def stage2(c):
                rawv, rraw = conv_raw[c % 2]
                rawsv, rraws = conv_raws[c % 2]
                accv, racc = conv_acc[c % 2]
                bgb, rbg = conv_bgb[c % 2]
                conv_taps(p, accv, racc, rawv, rraw, rawsv, rraws, SP_CONVA + (j * 6 + c) * 3, 3, only="rest")
                S.emit("dve", lambda e, c=c: e.tensor_copy(out=sc_carry[:, j, c, :], in_=rawv[:, 3 + NCP - 2:3 + NCP]),
                       [rraw], [r_scc])
                if p == 1:
                    S.dma("sp", scsT[j, c * 128:(c + 1) * 128, :, :],
                          rawsv[:, 5 * NB:7 * NB].rearrange("p (r b) -> p r b", b=NB), [rraws], [], d_oh[c % 2])
                S.emit("dve", lambda e, c=c: e.tensor_tensor(out=obuf[:, c, 0:ncol], in0=bgb[:, 0:ncol],
                                                             in1=accv[:, 0:ncol], op=ALU.mult),
                       [rbg, racc], [r_obuf])

            if "mix" in KPARTS:
                stage1(0)
                for c in range(6):
                    if c + 1 < 6:
                        stage1(c + 1)
                    if not S.dry:
                        stage2(c)
            if "out" in KPARTS:
                out_proj(w_out_a[j], p)

        def mixer_b(l, p):
            j = l // 2
            Wm = w_in_b[j]
            attention(l, p, Wm, 4 * SC_DIM + 12)
            scr_phase("conv")
            biga_phase("qkvz")
            ncol = NCP + (NSC if p == 1 else 0)
            S.dma("pool", wba[:, :, :], wbaT[j], [], [r_wba], d_wba)
            S.emit("act", lambda e: e.activation(out=negA[:, :], in_=spv(SP_ALOG + j * 6, 6), func=AF.Exp), [r_sp], [r_negA])
            S.emit("dve", lambda e: e.tensor_scalar(out=negA[:, :], in0=negA[:, :], scalar1=-1.0, scalar2=None, op0=ALU.mult),
                   [r_negA], [r_negA])
            sqv = conv_sq[0].bitcast(BF16); rsq = conv_sq[1]
            rsv, rrs = conv_rs
            wcur = [None]

            def stage1(c):
                st, cc = divmod(c, 4)
                if cc == 0:
                    wcur[0] = ws.stage(8, [(0, 512, Wm[:, st * 512:(st + 1) * 512])])
                if S.dry:
                    return
                wt, rw = wcur[0]
                if c >= 18:
                    for (c0, n) in coltiles(p):
                        pt, pr = ps_next()
                        mm_group(pt[:, 0:n], pr, [(wt[:, kc, cc * 128:(cc + 1) * 128], xbf[:, kc, c0:c0 + n])
                                                 for kc in range(8)], [r_xbf, rw])
                        S.emit("act", lambda e, pt=pt, c=c, c0=c0, n=n: e.activation(
                            out=qkvz[:, c, c0:c0 + n], in_=pt[:, 0:n], func=AF.Silu), [pr], [r_qkvz])
                    return
                rawv, rraw = conv_raw[c % 2]
                rawsv, rraws = conv_raws[c % 2]
                accv, racc = conv_acc[c % 2]
                S.emit("dve", lambda e, c=c: e.tensor_copy(out=rawv[:, 0:3], in_=gc_carry[:, j, c, :]), [r_gcc], [rraw])
                if p == 1:
                    S.dma("sp", rawsv[:, 0:3 * NB].rearrange("p (r b) -> p r b", b=NB),
                          gcT[j, c * 128:(c + 1) * 128, :, :], [], [rraws], d_h[c % 2])
                for (c0, n) in coltiles(p):
                    pt, pr = ps_next()
                    mm_group(pt[:, 0:n], pr, [(wt[:, kc, cc * 128:(cc + 1) * 128], xbf[:, kc, c0:c0 + n])
                                             for kc in range(8)], [r_xbf, rw])
                    dst = raw_dst(c0, n, rawv, rawsv)
                    S.emit("act", lambda e, pt=pt, dst=dst, c0=c0, n=n: e.activation(
                        out=dst, in_=ps_src(pt, c0, n), func=AF.Copy), [pr], [rraws if is_s(c0) else rraw])
                conv_taps(p, accv, racc, rawv, rraw, rawsv, rraws, SP_CONVB + (j * 18 + c) * 4, 4, only="first")

            def stage2(c):
                if c >= 18:
                    return
                rawv, rraw = conv_raw[c % 2]
                rawsv, rraws = conv_raws[c % 2]
                accv, racc = conv_acc[c % 2]
                conv_taps(p, accv, racc, rawv, rraw, rawsv, rraws, SP_CONVB + (j * 18 + c) * 4, 4, only="rest")
                S.emit("dve", lambda e, c=c: e.tensor_copy(out=gc_carry[:, j, c, :], in_=rawv[:, 3 + NCP - 3:3 + NCP]),
                       [rraw], [r_gcc])
                if p == 1:
                    S.dma("sp", gcsT[j, c * 128:(c + 1) * 128, :, :],
                          rawsv[:, 4 * NB:7 * NB].rearrange("p (r b) -> p r b", b=NB), [rraws], [], d_oh[c % 2])
                if c >= 12:
                    S.emit("act", lambda e, c=c: e.activation(out=qkvz[:, c, 0:ncol], in_=accv[:, 0:ncol], func=AF.Silu),
                           [racc], [r_qkvz])
                    return
                S.emit("act", lambda e: e.activation(out=accv[:, 0:ncol], in_=accv[:, 0:ncol], func=AF.Silu),
                       [racc], [racc])
                S.emit("act", lambda e: e.activation(out=sqv[:, 0:ncol], in_=accv[:, 0:ncol], func=AF.Square), [racc], [rsq])
                ebias = float(np.log(128.0 ** -0.5)) if c < 6 else 0.0
                for (c0, n) in coltiles(p):
                    pt, pr = ps_next()
                    mm_group(pt[:, 0:n], pr, [(onesb[:, :], sqv[:, c0:c0 + n])], [r_cb, rsq])
                    S.emit("dve", lambda e, pt=pt, n=n: e.tensor_scalar(out=rsv[:, 0:n], in0=pt[:, 0:n], scalar1=RMS_EPS,
                                                                       scalar2=None, op0=ALU.add), [pr], [rrs])
                    S.emit("act", lambda e, n=n: e.activation(out=rsv[:, 0:n], in_=rsv[:, 0:n], func=AF.Ln), [rrs], [rrs])
                    if ebias != 0.0:
                        S.emit("dve", lambda e, n=n: e.tensor_scalar(out=rsv[:, 0:n], in0=rsv[:, 0:n], scalar1=-0.5,
                                                                    scalar2=ebias, op0=ALU.mult, op1=ALU.add),
                               [rrs], [rrs])
                        S.emit("act", lambda e, n=n: e.activation(out=rsv[:, 0:n], in_=rsv[:, 0:n], func=AF.Exp),
                               [rrs], [rrs])
                    else:
                        S.emit("act", lambda e, n=n: e.activation(out=rsv[:, 0:n], in_=rsv[:, 0:n], func=AF.Exp,
                                                                  scale=-0.5), [rrs], [rrs])
                    S.emit("dve", lambda e, c=c, c0=c0, n=n: e.tensor_tensor(
                        out=qkvz[:, c, c0:c0 + n], in0=accv[:, c0:c0 + n], in1=rsv[:, 0:n], op=ALU.mult),
                        [racc, rrs], [r_qkvz])

            stage1(0)
            for c in range(24):
                if c + 1 < 24:
                    stage1(c + 1)
                if not S.dry:
                    stage2(c)
            if not S.dry:
                scr_phase("gdn")
                gdn_scalars(p, 64, lambda n: n * 64)
                Spb_ap, r_Spb = gd["Spb"]
                Spb = Spb_ap.bitcast(BF16).rearrange("p (h e) -> p h e", h=GDN_H)
                S.emit("act", lambda e: e.activation(out=Spb, in_=Sst[:, j, :, :], func=AF.Copy), [r_Sst[j]], [r_Spb])
                nxt = run_gens(gdn_A(0, 0, 64), None)
                for n in range(16):
                    cur = nxt
                    gA = gdn_A(n + 1, (n + 1) * 64, 64) if n + 1 < 16 else None
                    gB = gdn_B(j, n, n * 64, 64, Sst[:, j, :, :], r_Sst[j], Spb, r_Spb, cur)
                    nxt = run_gens(gA, gB)
                if p == 1:
                    gdn_scalars(p, TS, lambda n: NCP + n * TS)

                    def load_S(b):
                        St, rS = gd[f"Ss{b % 2}"]
                        Sv = St.rearrange("p (h e) -> p h e", h=GDN_H)
                        Sb_ap, rSb = gd[f"Ssb{b % 2}"]
                        Sb = Sb_ap.bitcast(BF16).rearrange("p (h e) -> p h e", h=GDN_H)
                        S.dma("sp", Sv, gs[j, b], [], [rS], d_ss[b % 2])
                        S.emit("act", lambda e: e.activation(out=Sb, in_=Sv, func=AF.Copy), [rS], [rSb])
                        return Sv, rS, Sb, rSb

                    nxt = run_gens(gdn_A(0, NCP, TS), None)
                    nS = load_S(0)
                    for b in range(NB):
                        cur, cS = nxt, nS
                        gA = None
                        if b + 1 < NB:
                            nS = load_S(b + 1)
                            gA = gdn_A(b + 1, NCP + (b + 1) * TS, TS)
                        gB = gdn_B(j, b, NCP + b * TS, TS, cS[0], cS[1], cS[2], cS[3], cur)
                        nxt = run_gens(gA, gB)
                        S.dma("sp", gss[j, b], cS[0], [cS[1]], [], d_so[b % 2])
            out_proj(w_out_b[j], p)

        def gdn_scalars(p, C, colfn):
            pt, pr = ps_next()
            items = []
            for n in range(16):
                col = colfn(n)
                items.append((pt[0:C, n * 12:(n + 1) * 12], [(xbf[:, kc, col:col + C], wba[:, kc, :]) for kc in range(8)]))
            mm_multi(items, [pr], [r_xbf, r_wba])
            pv = pt[0:C, 0:192].rearrange("p (n k) -> p n k", n=16)
            b3 = g_beta[0:C, :].rearrange("p (n h) -> p n h", n=16)
            t3 = g_tmp[0:C, :].rearrange("p (n h) -> p n h", n=16)
            S.emit("act", lambda e: e.activation(out=b3, in_=pv[:, :, 0:6], func=AF.Exp, scale=-1.0), [pr], [r_gsc])
            S.emit("dve", lambda e: e.tensor_scalar(out=g_beta[0:C, :], in0=g_beta[0:C, :], scalar1=1.0, scalar2=None,
                                                    op0=ALU.add), [r_gsc], [r_gsc])
            S.emit("dve", lambda e: e.reciprocal(out=g_beta[0:C, :], in_=g_beta[0:C, :]), [r_gsc], [r_gsc])
            jdt = spv(SP_DTB + cur_j[0] * 6, 6)
            S.emit("dve", lambda e: e.tensor_tensor(out=t3, in0=pv[:, :, 6:12],
                                                    in1=jdt[0:C, :].unsqueeze(1).to_broadcast([C, 16, 6]), op=ALU.add),
                   [pr, r_sp], [r_gsc])
            S.emit("act", lambda e: e.activation(out=g_tmp[0:C, :], in_=g_tmp[0:C, :], func=AF.Exp), [r_gsc], [r_gsc])
            S.emit("act", lambda e: e.activation(out=g_tmp[0:C, :], in_=g_tmp[0:C, :], func=AF.Ln, bias=1.0), [r_gsc], [r_gsc])
            g3 = g_g[0:C, :].rearrange("p (n h) -> p n h", n=16)
            S.emit("dve", lambda e: e.tensor_tensor(out=g3, in0=t3, in1=negA[0:C, :].unsqueeze(1).to_broadcast([C, 16, 6]),
                                                    op=ALU.mult), [r_gsc, r_negA], [r_gsc])
            pg, prg = ps_next()
            mm_multi([(pg[0:C, 0:96], [(tri[0:C, 0:C], g_g[0:C, :])]),
                      (pg[0:C, 96:192], [(ones[0:C, 0:C], g_g[0:C, :])]),
                      (pg[:, 192:288], [(ones[0:C, 0:128], g_g[0:C, :])])], [prg], [r_cst, r_gsc])
            S.emit("act", lambda e: e.activation(out=g_gc[0:C, :], in_=pg[0:C, 0:96], func=AF.Copy), [prg], [r_gsc])
            S.emit("act", lambda e: e.activation(out=g_egc[0:C, :], in_=pg[0:C, 0:96], func=AF.Exp), [prg], [r_gsc])
            S.emit("dve", lambda e: e.tensor_scalar(out=g_negegc[0:C, :], in0=g_egc[0:C, :], scalar1=-1.0, scalar2=None,
                                                    op0=ALU.mult), [r_gsc], [r_gsc])
            S.emit("dve", lambda e: e.tensor_tensor(out=g_kds[0:C, :], in0=pg[0:C, 96:192], in1=g_gc[0:C, :], op=ALU.subtract),
                   [prg, r_gsc], [r_gsc])
            S.emit("act", lambda e: e.activation(out=g_kds[0:C, :], in_=g_kds[0:C, :], func=AF.Exp), [r_gsc], [r_gsc])
            S.emit("act", lambda e: e.activation(out=g_gam[:, :], in_=pg[:, 192:288], func=AF.Exp), [prg], [r_gsc])

        cur_j = [0]

        def run_gens(gA, gB):
            res = None
            live = [g for g in (gA, gB) if g is not None]
            while live:
                for g in list(live):
                    try:
                        next(g)
                    except StopIteration as e:
                        if g is gA:
                            res = e.value
                        live.remove(g)
            return res

        def gdn_A(n, col0, C):
            H = GDN_H
            HC = H * C
            par = n % 2

            def t3(nm, inner, dt=F32):
                ap, r = gd[nm]
                if dt == BF16:
                    ap = ap.bitcast(BF16)
                return ap[0:C, 0:H * inner].rearrange("p (h x) -> p h x", h=H), r

            def kT(h):
                return qkvz[:, 6 + h, col0:col0 + C]

            def qT(h):
                return qkvz[:, h, col0:col0 + C]

            kdec, r_kdec = t3(f"kdec{par}", 128, BF16)
            vtm, r_vtm = t3(f"vtm{par}", 128, BF16)
            pb_, prb = psbf
            pbv = pb_[0:C, 0:768].rearrange("p (h x) -> p h x", h=H)
            tr_multi([(pb_[0:C, h * 128:(h + 1) * 128], kT(h), identb[:, :]) for h in range(H)], [prb], [r_qkvz, r_cb])
            yield
            S.emit("dve", lambda e: e.tensor_tensor(
                out=kdec, in0=pbv, in1=g_kds[0:C, n * 6:n * 6 + 6].unsqueeze(2).to_broadcast([C, H, 128]), op=ALU.mult),
                [prb, r_gsc], [r_kdec])
            yield
            tr_multi([(pb_[0:C, h * 128:(h + 1) * 128], qkvz[:, 12 + h, col0:col0 + C], identb[:, :]) for h in range(H)],
                     [prb], [r_qkvz, r_cb])
            yield
            S.emit("act", lambda e: e.activation(out=vtm, in_=pbv, func=AF.Copy), [prb], [r_vtm])
            yield
            pK, prK = ps_next("A")
            mm_multi([(pK[0:C, h * C:(h + 1) * C], [(kT(h), kT(h))]) for h in range(H)], [prK], [r_qkvz])
            yield
            pQ, prQ = ps_next("A")
            mm_multi([(pQ[0:C, h * C:(h + 1) * C], [(kT(h), qT(h))]) for h in range(H)], [prQ], [r_qkvz])
            yield
            gtri, r_gtri = t3("gtri", C)
            S.emit("dve", lambda e: e.tensor_tensor(
                out=gtri, in0=tri[0:C, 0:C].unsqueeze(1).to_broadcast([C, H, C]),
                in1=g_g[0:C, n * 6:n * 6 + 6].unsqueeze(2).to_broadcast([C, H, C]), op=ALU.mult), [r_cst, r_gsc], [r_gtri])
            yield
            pD, prD = ps_next("A")
            m6 = mask6[0:C, :, 0:C]
            if C == 64:
                m6f = mask6[0:C, :, :].rearrange("p h x -> p (h x)")
                mm_group(pD[0:C, 0:HC], prD, [(ones[0:C, 0:C], gd["gtri"][0][0:C, 0:HC]), (ident[0:C, 0:C], m6f)],
                         [r_cst, r_gtri, r_mask6])
                yield
            else:
                mm_group(pD[0:C, 0:HC], prD, [(ones[0:C, 0:C], gd["gtri"][0][0:C, 0:HC])], [r_cst, r_gtri])
                yield
            dT, r_dT = t3("dT", C)
            pD3 = pD[0:C, 0:HC].rearrange("p (h x) -> p h x", h=H)
            S.emit("dve", lambda e: e.tensor_tensor(
                out=dT, in0=pD3, in1=g_gc[0:C, n * 6:n * 6 + 6].unsqueeze(2).to_broadcast([C, H, C]), op=ALU.subtract),
                [prD, r_gsc], [r_dT])
            yield
            if C != 64:
                S.emit("dve", lambda e: e.tensor_tensor(out=dT, in0=dT, in1=m6, op=ALU.add), [r_dT, r_mask6], [r_dT])
                yield
            S.emit("act", lambda e: e.activation(out=dT, in_=dT, func=AF.Exp), [r_dT], [r_dT])
            yield
            bsm = gtri
            S.emit("dve", lambda e: e.tensor_tensor(
                out=bsm, in0=strU[0:C, 0:C].unsqueeze(1).to_broadcast([C, H, C]),
                in1=g_beta[0:C, n * 6:n * 6 + 6].unsqueeze(2).to_broadcast([C, H, C]), op=ALU.mult),
                [r_cst, r_gsc, r_gtri], [r_gtri])
            yield
            U, r_U = t3("U", C, BF16)
            pK3 = pK[0:C, 0:HC].rearrange("p (h x) -> p h x", h=H)
            pQ3 = pQ[0:C, 0:HC].rearrange("p (h x) -> p h x", h=H)
            S.emit("dve", lambda e: e.tensor_tensor(out=bsm, in0=bsm, in1=dT, op=ALU.mult), [r_dT, r_gtri], [r_gtri])
            yield
            S.emit("dve", lambda e: e.tensor_tensor(out=U, in0=pK3, in1=bsm, op=ALU.mult), [prK, r_gtri], [r_U])
            yield
            qkT, r_qkT = t3(f"qkT{par}", C, BF16)
            S.emit("dve", lambda e: e.tensor_tensor(out=qkT, in0=pQ3, in1=dT, op=ALU.mult), [prQ, r_dT], [r_qkT])
            yield
            idC = ident[0:C, 0:C]
            idCb = identb[0:C, 0:C]
            PT, r_PT = t3("PTa", C, BF16)
            pT_, prT = ps_next("A")
            mm_multi([(pT_[0:C, h * C:(h + 1) * C], [(U[:, h, :], idCb)]) for h in range(H)], [prT], [r_U, r_cb])
            yield
            S.emit("act", lambda e: e.activation(out=PT, in_=pT_[0:C, 0:HC].rearrange("p (h x) -> p h x", h=H), func=AF.Copy),
                   [prT], [r_PT])
            yield
            Vc, r_Vc = t3("Va", C, BF16)
            S.emit("dve", lambda e: e.tensor_tensor(out=Vc, in0=idC.unsqueeze(1).to_broadcast([C, H, C]), in1=U,
                                                     op=ALU.subtract), [r_cst, r_U], [r_Vc])
            yield
            P, r_P = U, r_U
            nlev = {64: 5, 4: 1}[C]
            pnames = [("Pa", "PTb"), ("Pb", "PTa")]
            vnames = ["Vb", "Va"]
            for lev in range(nlev):
                last = lev == nlev - 1
                nP, nPT = pnames[lev % 2]
                PTn, r_PTn = t3(nPT, C, BF16)
                pA, prA = ps_next("A")
                mm_multi([(pA[0:C, h * C:(h + 1) * C], [(P[:, h, :], PT[:, h, :])]) for h in range(H)], [prA], [r_P, r_PT])
                yield
                S.emit("act", lambda e, pA=pA, PTn=PTn: e.activation(
                    out=PTn, in_=pA[0:C, 0:HC].rearrange("p (h x) -> p h x", h=H), func=AF.Copy), [prA], [r_PTn])
                yield
                if not last:
                    Pn, r_Pn = t3(nP, C, BF16)
                    pB, prB = ps_next("A")
                    mm_multi([(pB[0:C, h * C:(h + 1) * C], [(PT[:, h, :], P[:, h, :])]) for h in range(H)], [prB], [r_P, r_PT])
                    yield
                    S.emit("act", lambda e, pB=pB, Pn=Pn: e.activation(
                        out=Pn, in_=pB[0:C, 0:HC].rearrange("p (h x) -> p h x", h=H), func=AF.Copy), [prB], [r_Pn])
                    yield
                Vn, r_Vn = t3(f"Vf{par}" if last else vnames[lev % 2], C, BF16)
                pV, prV = ps_next("A")
                mm_multi([(pV[0:C, h * C:(h + 1) * C], [(PTn[:, h, :], Vc[:, h, :])]) for h in range(H)], [prV], [r_PTn, r_Vc])
                yield
                S.emit("dve", lambda e, pV=pV, Vn=Vn, Vc=Vc: e.tensor_tensor(
                    out=Vn, in0=pV[0:C, 0:HC].rearrange("p (h x) -> p h x", h=H), in1=Vc, op=ALU.add), [prV, r_Vc], [r_Vn])
                yield
                Vc, r_Vc = Vn, r_Vn
                PT, r_PT = PTn, r_PTn
                if not last:
                    P, r_P = Pn, r_Pn
            return dict(V=Vc, r_V=r_Vc, qkT=qkT, r_qkT=r_qkT, kdec=kdec, r_kdec=r_kdec, vtm=vtm, r_vtm=r_vtm)

        def gdn_B(j, n, col0, C, Sv, rS, Sb, rSb, A):
            H = GDN_H
            HC = H * C

            def t3(nm, inner, dt=F32):
                ap, r = gd[nm]
                if dt == BF16:
                    ap = ap.bitcast(BF16)
                return ap[0:C, 0:H * inner].rearrange("p (h x) -> p h x", h=H), r

            def bc(t, hs):
                return t[0:C, n * 6 + hs:n * 6 + hs + 3].unsqueeze(2).to_broadcast([C, 3, 128])

            def kT(h):
                return qkvz[:, 6 + h, col0:col0 + C]

            def qT(h):
                return qkvz[:, h, col0:col0 + C]

            Vc, r_Vc, qkT, r_qkT = A["V"], A["r_V"], A["qkT"], A["r_qkT"]
            kdec, r_kdec, vtm, r_vtm = A["kdec"], A["r_kdec"], A["vtm"], A["r_vtm"]
            Y, r_Y = t3("Y", 128, BF16)
            vnew, r_vn = t3("vnew", 128, BF16)
            ot, r_ot = t3("ot", 128)
            sq, r_sq = t3("sq", 128)
            idC = ident[0:C, 0:C]
            for hg in range(2):
                hs = 3 * hg
                pk, prk = ps_next("B")
                mm_multi([(pk[0:C, hh * 128:(hh + 1) * 128], [(kT(hs + hh), Sb[:, hs + hh, :])]) for hh in range(3)],
                         [prk], [r_qkvz, rSb])
                yield
                pq, prq = ps_next("B")
                mm_multi([(pq[0:C, hh * 128:(hh + 1) * 128], [(qT(hs + hh), Sb[:, hs + hh, :])]) for hh in range(3)],
                         [prq], [r_qkvz, rSb])
                yield
                pk3 = pk[0:C, 0:384].rearrange("p (h x) -> p h x", h=3)
                pq3 = pq[0:C, 0:384].rearrange("p (h x) -> p h x", h=3)
                S.emit("dve", lambda e, pk3=pk3, hs=hs: e.tensor_tensor(out=sq[:, hs:hs + 3, :], in0=pk3,
                                                                         in1=bc(g_negegc, hs), op=ALU.mult),
                       [prk, r_gsc], [r_sq])
                yield
                S.emit("dve", lambda e, hs=hs: e.tensor_tensor(out=Y[:, hs:hs + 3, :], in0=sq[:, hs:hs + 3, :],
                                                                in1=vtm[:, hs:hs + 3, :], op=ALU.add), [r_sq, r_vtm], [r_Y])
                yield
                px, prx = ps_next("B")
                mm_multi([(px[0:C, hh * 128:(hh + 1) * 128], [(Vc[:, hs + hh, :], Y[:, hs + hh, :])]) for hh in range(3)],
                         [prx], [r_Vc, r_Y])
                yield
                px3 = px[0:C, 0:384].rearrange("p (h x) -> p h x", h=3)
                S.emit("dve", lambda e, px3=px3, hs=hs: e.tensor_tensor(out=vnew[:, hs:hs + 3, :], in0=px3,
                                                                         in1=bc(g_beta, hs), op=ALU.mult),
                       [prx, r_gsc], [r_vn])
                yield
                po, pro = ps_next("B")
                mm_multi([(po[0:C, hh * 128:(hh + 1) * 128], [(qkT[:, hs + hh, :], vnew[:, hs + hh, :])]) for hh in range(3)],
                         [pro], [r_qkT, r_vn])
                yield
                psn, prs = ps_next("B")
                mm_multi([(psn[:, hh * 128:(hh + 1) * 128], [(kdec[:, hs + hh, :], vnew[:, hs + hh, :])]) for hh in range(3)],
                         [prs], [r_kdec, r_vn])
                yield
                ps3 = psn[:, 0:384].rearrange("p (h x) -> p h x", h=3)
                gb = g_gam[:, n * 6 + hs:n * 6 + hs + 3].unsqueeze(2).to_broadcast([128, 3, 128])
                S.emit("dve", lambda e, hs=hs, gb=gb: e.tensor_tensor(out=Sv[:, hs:hs + 3, :], in0=Sv[:, hs:hs + 3, :], in1=gb,
                                                                       op=ALU.mult), [rS, r_gsc], [rS])
                yield
                S.emit("dve", lambda e, hs=hs, ps3=ps3: e.tensor_tensor(out=Sv[:, hs:hs + 3, :], in0=Sv[:, hs:hs + 3, :],
                                                                         in1=ps3, op=ALU.add), [rS, prs], [rS])
                yield
                S.emit("act", lambda e, hs=hs: e.activation(out=Sb[:, hs:hs + 3, :], in_=Sv[:, hs:hs + 3, :], func=AF.Copy),
                       [rS], [rSb])
                yield
                po3 = po[0:C, 0:384].rearrange("p (h x) -> p h x", h=3)
                S.emit("dve", lambda e, pq3=pq3, hs=hs: e.tensor_tensor(out=ot[:, hs:hs + 3, :], in0=pq3,
                                                                         in1=bc(g_egc, hs), op=ALU.mult),
                       [prq, r_gsc], [r_ot])
                yield
                S.emit("dve", lambda e, po3=po3, hs=hs: e.tensor_tensor(out=ot[:, hs:hs + 3, :], in0=ot[:, hs:hs + 3, :],
                                                                         in1=po3, op=ALU.add), [pro, r_ot], [r_ot])
                yield
            ssq_ap, r_ssq = gd["ssq"]
            ssq = ssq_ap[0:C, 0:H]
            S.emit("dve", lambda e: e.tensor_tensor(out=sq, in0=ot, in1=ot, op=ALU.mult), [r_ot, r_sq], [r_sq])
            yield
            S.emit("dve", lambda e: e.tensor_reduce(out=ssq, in_=sq, axis=AX.X, op=ALU.add), [r_sq], [r_ssq])
            yield
            S.emit("dve", lambda e: e.tensor_scalar(out=ssq, in0=ssq, scalar1=1.0 / 128.0, scalar2=RMS_EPS, op0=ALU.mult,
                                                    op1=ALU.add), [r_ssq], [r_ssq])
            yield
            S.emit("act", lambda e: e.activation(out=ssq, in_=ssq, func=AF.Ln), [r_ssq], [r_ssq])
            yield
            S.emit("act", lambda e: e.activation(out=ssq, in_=ssq, func=AF.Exp, scale=-0.5), [r_ssq], [r_ssq])
            yield
            S.emit("dve", lambda e: e.tensor_tensor(out=ot, in0=ot, in1=ssq.unsqueeze(2).to_broadcast([C, H, 128]),
                                                     op=ALU.mult), [r_ot, r_ssq], [r_ot])
            yield
            pz, prz = ps_next("B")
            tr_multi([(pz[:, h * C:(h + 1) * C], ot[:, h, :], idC) for h in range(H)], [prz], [r_ot, r_cst])
            yield
            pz3 = pz[:, 0:HC].rearrange("p (h x) -> p h x", h=H)
            S.emit("dve", lambda e: e.scalar_tensor_tensor(
                out=obuf[:, 0:6, col0:col0 + C], in0=pz3, scalar=spv(SP_NORMW + j), in1=qkvz[:, 18:24, col0:col0 + C],
                op0=ALU.mult, op1=ALU.mult), [prz, r_sp, r_qkvz], [r_obuf])
            yield

        def layernorm(goff, boff, p):
            if S.dry:
                return
            biga_phase("ln")
            for (c0, n) in coltiles(p):
                S.emit("act", lambda e, c0=c0, n=n: e.activation(out=ln_zb[:, :, 0:n], in_=xres[:, :, c0:c0 + n], func=AF.Copy),
                       [r_xres], [r_ln])
                S.emit("act", lambda e, c0=c0, n=n: e.activation(out=ln_sqb[:, :, 0:n], in_=xres[:, :, c0:c0 + n],
                                                                func=AF.Square), [r_xres], [r_ln])
                pm, prm = ps_next()
                mm_group(pm[:, 0:n], prm, [(meanb[:, :], ln_zb[:, kc, 0:n]) for kc in range(8)], [r_cb, r_ln])
                pq, prq = ps_next()
                mm_group(pq[:, 0:n], prq, [(meanb[:, :], ln_sqb[:, kc, 0:n]) for kc in range(8)], [r_cb, r_ln])
                S.emit("act", lambda e, pm=pm, n=n: e.activation(out=ln_mean[:, 0:n], in_=pm[:, 0:n], func=AF.Copy), [prm], [r_ln])
                S.emit("dve", lambda e, n=n: e.tensor_tensor(out=ln_rstd[:, 0:n], in0=ln_mean[:, 0:n], in1=ln_mean[:, 0:n],
                                                             op=ALU.mult), [r_ln], [r_ln])
                S.emit("dve", lambda e, pq=pq, n=n: e.tensor_tensor(out=ln_rstd[:, 0:n], in0=pq[:, 0:n], in1=ln_rstd[:, 0:n],
                                                                    op=ALU.subtract), [prq, r_ln], [r_ln])
                S.emit("dve", lambda e, n=n: e.tensor_scalar(out=ln_rstd[:, 0:n], in0=ln_rstd[:, 0:n], scalar1=LN_EPS,
                                                             scalar2=None, op0=ALU.add), [r_ln], [r_ln])
                S.emit("act", lambda e, n=n: e.activation(out=ln_rstd[:, 0:n], in_=ln_rstd[:, 0:n], func=AF.Ln), [r_ln], [r_ln])
                S.emit("act", lambda e, n=n: e.activation(out=ln_rstd[:, 0:n], in_=ln_rstd[:, 0:n], func=AF.Exp, scale=-0.5),
                       [r_ln], [r_ln])
                S.emit("dve", lambda e, c0=c0, n=n: e.tensor_tensor(
                    out=ln_t1[:, :, 0:n], in0=xres[:, :, c0:c0 + n],
                    in1=ln_mean[:, 0:n].unsqueeze(1).to_broadcast([128, 8, n]), op=ALU.subtract), [r_xres, r_ln], [r_ln])
                S.emit("dve", lambda e, n=n: e.tensor_tensor(
                    out=ln_t1[:, :, 0:n], in0=ln_t1[:, :, 0:n],
                    in1=ln_rstd[:, 0:n].unsqueeze(1).to_broadcast([128, 8, n]), op=ALU.mult), [r_ln], [r_ln])
                for kc in range(8):
                    S.emit("act", lambda e, kc=kc, c0=c0, n=n: e.activation(
                        out=xres[:, kc, c0:c0 + n], in_=ln_t1[:, kc, 0:n], func=AF.Identity,
                        scale=spv(goff + kc), bias=spv(boff + kc)), [r_ln, r_sp], [r_xres])
                S.emit("act", lambda e, c0=c0, n=n: e.activation(out=xbf[:, :, c0:c0 + n], in_=xres[:, :, c0:c0 + n], func=AF.Copy),
                       [r_xres], [r_xbf])

        def ffn(l, p):
            scr_phase("ffn")
            biga_phase("act")
            Wu = w_up[l]
            ncol = NCP + (NSC if p == 1 else 0)

            def bufs(i, half):
                return ffn_raw[half][i % 2], ffn_raws[half][i % 2], ffn_acc[half][i % 2]

            def stage1(i):
                wt, rw = ws.stage(8, [(0, 128, Wu[:, i * 128:(i + 1) * 128]),
                                      (128, 128, Wu[:, D_FF + i * 128:D_FF + (i + 1) * 128])])
                if S.dry:
                    return
                for half in range(2):
                    cidx = half * 22 + i
                    (rawv, rraw), (rawsv, rraws), (accv, racc) = bufs(i, half)
                    S.emit("dve", lambda e, cidx=cidx, rawv=rawv: e.tensor_copy(out=rawv[:, 1:3], in_=ff_carry[:, l, cidx, :]),
                           [r_ffc], [rraw])
                    if p == 1:
                        S.dma("sp", rawsv[:, NB:3 * NB].rearrange("p (r b) -> p r b", b=NB),
                              ffT[l, cidx * 128:(cidx + 1) * 128, :, :], [], [rraws], d_hf[half][i % 2])
                    for (c0, n) in coltiles(p):
                        pt, pr = ps_next()
                        mm_group(pt[:, 0:n], pr, [(wt[:, kc, half * 128:(half + 1) * 128], xbf[:, kc, c0:c0 + n])
                                                 for kc in range(8)], [r_xbf, rw])
                        dst = raw_dst(c0, n, rawv, rawsv)
                        S.emit("act", lambda e, pt=pt, dst=dst, c0=c0, n=n: e.activation(
                            out=dst, in_=ps_src(pt, c0, n), func=AF.Copy), [pr], [rraws if is_s(c0) else rraw])
                    conv_taps(p, accv, racc, rawv, rraw, rawsv, rraws, SP_CONVF + (l * 44 + cidx) * 3, 3, only="first")

            def stage2(i):
                for half in range(2):
                    cidx = half * 22 + i
                    (rawv, rraw), (rawsv, rraws), (accv, racc) = bufs(i, half)
                    conv_taps(p, accv, racc, rawv, rraw, rawsv, rraws, SP_CONVF + (l * 44 + cidx) * 3, 3, only="rest")
                    S.emit("dve", lambda e, cidx=cidx, rawv=rawv: e.tensor_copy(out=ff_carry[:, l, cidx, :],
                                                                                in_=rawv[:, 3 + NCP - 2:3 + NCP]), [rraw], [r_ffc])
                    if p == 1:
                        S.dma("sp", ffsT[l, cidx * 128:(cidx + 1) * 128, :, :],
                              rawsv[:, 5 * NB:7 * NB].rearrange("p (r b) -> p r b", b=NB), [rraws], [], d_ohf[half][i % 2])

            def stage3(i):
                (ag, rag) = ffn_acc[0][i % 2]
                (au, rau) = ffn_acc[1][i % 2]
                S.emit("act", lambda e: e.activation(out=ag[:, 0:ncol], in_=ag[:, 0:ncol], func=AF.Silu), [rag], [rag])
                S.emit("dve", lambda e: e.tensor_tensor(out=actb[:, i, 0:ncol], in0=ag[:, 0:ncol], in1=au[:, 0:ncol],
                                                        op=ALU.mult), [rag, rau], [r_actb])

            stage1(0)
            for i in range(22):
                if i + 1 < 22:
                    stage1(i + 1)
                if not S.dry:
                    stage2(i)
                    stage3(i)
            Wd = w_down[l]
            for oc in range(8):
                wt, rw = ws.stage(22, [(0, 128, Wd[:, oc * 128:(oc + 1) * 128])])
                if S.dry:
                    continue
                for (c0, n) in coltiles(p):
                    pt, pr = ps_next()
                    mm_group(pt[:, 0:n], pr, [(wt[:, kc, :], actb[:, kc, c0:c0 + n]) for kc in range(22)], [r_actb, rw])
                    S.emit("dve", lambda e, pt=pt, oc=oc, c0=c0, n=n: e.scalar_tensor_tensor(
                        out=xres[:, oc, c0:c0 + n], in0=xres[:, oc, c0:c0 + n], scalar=ALPHA, in1=pt[:, 0:n],
                        op0=ALU.mult, op1=ALU.add), [pr, r_xres], [r_xres])

        _mixer_b = mixer_b

        def mixer_b(l, p):
            cur_j[0] = l // 2
            _mixer_b(l, p)

        S.dry = True
        ws.planning = True
        program()
        S.dry = False
        ws.planning = False
        program()
        build.stats = dict(ninst=S.ninst, nstages=len(ws.plan), counts={k: v["cnt"] for k, v in S.eng.items()})
    return nc


def _consts():
    c = np.zeros((128, NCST), np.float32)
    c[:, C_ID:C_ID + 128] = np.eye(128, dtype=np.float32)
    c[:, C_ONE:C_ONE + 128] = 1.0
    k = np.arange(64)
    c[0:64, C_TRI:C_TRI + 64] = (k[:, None] <= k[None, :]).astype(np.float32)
    c[0:64, C_MASK:C_MASK + 64] = np.where(k[:, None] <= k[None, :], 0.0, NEG).astype(np.float32)
    c[0:64, C_STR:C_STR + 64] = (k[:, None] < k[None, :]).astype(np.float32)
    return c


def _small_params(conv_a, conv_b, w_conv_ffn, ln1_g, ln1_b, ln2_g, ln2_b, gdn_norm_w, a_log, dt_bias):
    sp = np.zeros((128, NSP), np.float32)

    def fm(w, nchunk):
        L, J, F = w.shape
        return np.ascontiguousarray(w.reshape(L, J, nchunk, 128).transpose(3, 0, 2, 1)).reshape(128, -1)

    sp[:, SP_CONVA:SP_CONVA + 36] = fm(conv_a, 6)
    sp[:, SP_CONVB:SP_CONVB + 144] = fm(conv_b, 18)
    sp[:, SP_CONVF:SP_CONVF + 528] = fm(w_conv_ffn, 44)
    for off, a in ((SP_LN1G, ln1_g), (SP_LN1B, ln1_b), (SP_LN2G, ln2_g), (SP_LN2B, ln2_b)):
        sp[:, off:off + 32] = a.reshape(DEPTH, 8, 128).transpose(2, 0, 1).reshape(128, 32)
    sp[:, SP_NORMW:SP_NORMW + 2] = gdn_norm_w.T
    sp[:, SP_ALOG:SP_ALOG + 12] = a_log.reshape(1, 12)
    sp[:, SP_DTB:SP_DTB + 12] = dt_bias.reshape(1, 12)
    return sp


def make_in_maps(inp, cores):
    f = lambda a: np.ascontiguousarray(np.asarray(a, dtype=np.float32))
    sp = _small_params(f(inp["conv_a"]), f(inp["conv_b"]), f(inp["w_conv_ffn"]), f(inp["ln1_g"]), f(inp["ln1_b"]),
                       f(inp["ln2_g"]), f(inp["ln2_b"]), f(inp["gdn_norm_w"]), f(inp["a_log"]), f(inp["dt_bias"]))
    cst = _consts()
    shared = {k: f(inp[k]) for k in ("w_in_a", "w_out_a", "w_in_b", "w_out_b", "w_mem_kv", "w_up", "w_down")}
    wbaT = f(np.asarray(inp["w_in_b"])[:, :, 3072:3084].reshape(2, 8, 128, 12).transpose(0, 2, 1, 3))
    maps = []
    for c in cores:
        b0, b1 = c * NB, (c + 1) * NB
        m = dict(shared)
        m["xpT"] = f(np.asarray(inp["x_prompt"][c]).T)
        m["xsT"] = f(np.asarray(inp["x_sample"][b0:b1]).reshape(NSC, D).T)
        m["memT"] = f(np.asarray(inp["mem_prompt"][c]).T)
        ck = np.asarray(inp["cache_mem_k"][:, b0:b1]).reshape(DEPTH, NB, NMEM, 256)
        m["ckT"] = f(ck.transpose(0, 3, 1, 2))
        m["cv"] = f(np.asarray(inp["cache_mem_v"][:, b0:b1]).reshape(DEPTH, NB, NMEM, 256))
        m["scT"] = f(np.asarray(inp["state_shortconv"][:, b0:b1]).transpose(0, 3, 2, 1))
        m["gcT"] = f(np.asarray(inp["state_gdn_conv"][:, b0:b1]).transpose(0, 3, 2, 1))
        m["gs"] = f(np.asarray(inp["state_gdn"][:, b0:b1]).transpose(0, 1, 3, 2, 4))
        m["wbaT"] = wbaT
        m["ffT"] = f(np.asarray(inp["state_ffn_conv"][:, b0:b1]).transpose(0, 3, 2, 1))
        m["spd"] = sp
        m["cstd"] = cst
        maps.append(m)
    return maps


def assemble(results, ncores):
    B = ncores
    y_p = np.stack([r["ypT"].T for r in results])
    y_s = np.concatenate([r["ysT"].T.reshape(NB, TS, D) for r in results])
    mk_ = np.stack([r["mk"] for r in results], axis=1).reshape(DEPTH, B, NMEM, 4, 64)
    mv_ = np.stack([r["mv"] for r in results], axis=1).reshape(DEPTH, B, NMEM, 4, 64)
    sc_p = np.stack([r["scpT"].transpose(0, 3, 2, 1).reshape(2, 2, SC_DIM) for r in results], axis=1)
    gc_p = np.stack([r["gcpT"].transpose(0, 3, 2, 1).reshape(2, 3, 2304) for r in results], axis=1)
    gs_p = np.stack([r["gsp"].transpose(0, 2, 1, 3) for r in results], axis=1)
    ff_p = np.stack([r["ffpT"].transpose(0, 3, 2, 1).reshape(DEPTH, 2, 2 * D_FF) for r in results], axis=1)
    sc_s = np.concatenate([r["scsT"].transpose(0, 3, 2, 1) for r in results], axis=1)
    gc_s = np.concatenate([r["gcsT"].transpose(0, 3, 2, 1) for r in results], axis=1)
    gs_s = np.concatenate([r["gss"].transpose(0, 1, 3, 2, 4) for r in results], axis=1)
    ff_s = np.concatenate([r["ffsT"].transpose(0, 3, 2, 1) for r in results], axis=1)
    outs = (y_p, y_s, mk_, mv_, sc_p, gc_p, gs_p, ff_p, sc_s, gc_s, gs_s, ff_s)
    return tuple(np.ascontiguousarray(o, dtype=np.float32) for o in outs)


def kernel(**inputs):
    nc = build()
    maps = make_in_maps(inputs, list(range(8)))
    res = run_bass_kernel_spmd(nc, maps, core_ids=list(range(8)))
    return assemble(res.results, 8)
```

```python
import contextlib
import os
import numpy as np
import concourse.bass as bass
import concourse.mybir as mybir
from concourse.bass_utils import run_bass_kernel_spmd

F32 = mybir.dt.float32
BF16 = mybir.dt.bfloat16
AF = mybir.ActivationFunctionType
ALU = mybir.AluOpType
AX = mybir.AxisListType

D = 1024
SEQ = 2048
NCP = 1024
NSC = 64
W = NCP + NSC
NB = 16
TS = 4
DEPTH = 4
SC_DIM = 768
GDN_H = 6
D_FF = 2816
NMEM = 256
ALPHA = (2.0 * DEPTH) ** 0.25
LN_EPS = 1e-5
RMS_EPS = 1e-6
NEG = -30000.0

SP_CONVA = 0
SP_CONVB = SP_CONVA + 36
SP_CONVF = SP_CONVB + 144
SP_LN1G = SP_CONVF + 528
SP_LN1B = SP_LN1G + 32
SP_LN2G = SP_LN1B + 32
SP_LN2B = SP_LN2G + 32
SP_NORMW = SP_LN2B + 32
SP_ALOG = SP_NORMW + 2
SP_DTB = SP_ALOG + 12
NSP = SP_DTB + 12
C_ID = 0
C_ONE = 128
C_TRI = 256
C_MASK = 320
C_STR = 384
NCST = 448


class Res:
    __slots__ = ("name", "lw", "rd", "excl")

    def __init__(self, name, excl=False):
        self.name = name
        self.lw = None
        self.rd = {}
        self.excl = excl


class DSem:
    def __init__(self, name, sem):
        self.name = name
        self.sem = sem
        self.cnt = 0


def fence(new, olds):
    for n in new:
        for o in olds:
            if o.lw is not None and n.rd.get(o.lw[0], 0) < o.lw[1]:
                n.rd[o.lw[0]] = o.lw[1]
            for s, v in o.rd.items():
                if n.rd.get(s, 0) < v:
                    n.rd[s] = v


class Sched:
    def __init__(self, nc, es):
        self.nc = nc
        self.es = es
        self.eng = {}
        self.sems = {}
        self.dry = False
        for name, h in [("pe", nc.tensor), ("act", nc.scalar), ("dve", nc.vector),
                        ("pool", nc.gpsimd), ("sp", nc.sync)]:
            sem = es.enter_context(nc.semaphore("sem_" + name))
            self.eng[name] = dict(h=h, sem=sem, cnt=0, waited={})
            self.sems[name] = sem
        self.ndsem = 0
        self.dsems = {}
        self.ninst = 0

    def dsem(self):
        name = f"dsem{self.ndsem}"
        self.ndsem += 1
        sem = self.es.enter_context(self.nc.semaphore(name))
        self.sems[name] = sem
        d = DSem(name, sem)
        self.dsems[name] = d
        return d

    def _waits(self, en, reads, writes):
        e = self.eng[en]
        deps = {}
        for r in reads:
            if r.lw is not None and deps.get(r.lw[0], 0) < r.lw[1]:
                deps[r.lw[0]] = r.lw[1]
        for w in writes:
            if w.lw is not None and deps.get(w.lw[0], 0) < w.lw[1]:
                deps[w.lw[0]] = w.lw[1]
            for s, v in w.rd.items():
                if deps.get(s, 0) < v:
                    deps[s] = v
        for s, v in deps.items():
            if en == "pe" and s == "pe":
                continue
            if s in self.dsems:
                v = self.dsems[s].cnt
            if e["waited"].get(s, 0) < v:
                e["h"].wait_ge(self.sems[s], v)
                e["waited"][s] = v

    def emit(self, en, fn, reads=(), writes=()):
        if self.dry:
            return None
        e = self.eng[en]
        if any(r.excl for r in reads):
            writes = list(writes) + [r for r in reads if r.excl]
            reads = [r for r in reads if not r.excl]
        self._waits(en, reads, writes)
        ins = fn(e["h"])
        e["cnt"] += 1
        ins.then_inc(e["sem"], 1)
        c = e["cnt"]
        for r in reads:
            if r.rd.get(en, 0) < c:
                r.rd[en] = c
        for w in writes:
            w.lw = (en, c)
            w.rd = {}
        self.ninst += 1
        return ins

    def dma(self, qn, out, in_, reads, writes, ds):
        if self.dry:
            return None
        e = self.eng[qn]
        self._waits(qn, reads, writes)
        ins = e["h"].dma_start(out=out, in_=in_)
        ds.cnt += 16
        ins.then_inc(ds.sem, 16)
        for r in reads:
            if r.rd.get(ds.name, 0) < ds.cnt:
                r.rd[ds.name] = ds.cnt
        for w in writes:
            w.lw = (ds.name, ds.cnt)
            w.rd = {}
        self.ninst += 1
        return ins


GDN_PIPE = os.environ.get("GDN_PIPE", "1") == "1"
CONV_ACT = os.environ.get("CONV_ACT", "1") == "1"
KPARTS = os.environ.get("KPARTS", "att,att2,mix,out,ln,ffn").split(",")


def build(nlayers=DEPTH, npass=2):
    nc = bass.Bass("TRN2", target_bir_lowering=False)

    def din(name, shape):
        return nc.dram_tensor(name, list(shape), F32, kind="ExternalInput").ap()

    def dout(name, shape):
        return nc.dram_tensor(name, list(shape), F32, kind="ExternalOutput").ap()

    xpT = din("xpT", [D, SEQ])
    xsT = din("xsT", [D, NSC])
    memT = din("memT", [D, NMEM])
    ckT = din("ckT", [DEPTH, 256, NB, NMEM])
    cv = din("cv", [DEPTH, NB, NMEM, 256])
    scT = din("scT", [2, SC_DIM, 2, NB])
    gcT = din("gcT", [2, 2304, 3, NB])
    gs = din("gs", [2, NB, 128, GDN_H, 128])
    wbaT = din("wbaT", [2, 128, 8, 12])
    ffT = din("ffT", [DEPTH, 2 * D_FF, 2, NB])
    w_in_a = din("w_in_a", [2, D, 2560])
    w_out_a = din("w_out_a", [2, D, D])
    w_in_b = din("w_in_b", [2, D, 3340])
    w_out_b = din("w_out_b", [2, D, D])
    w_mem_kv = din("w_mem_kv", [DEPTH, D, 512])
    w_up = din("w_up", [DEPTH, D, 2 * D_FF])
    w_down = din("w_down", [DEPTH, D_FF, D])
    spd = din("spd", [128, NSP])
    cstd = din("cstd", [128, NCST])

    ypT = dout("ypT", [D, SEQ])
    ysT = dout("ysT", [D, NSC])
    mk = dout("mk", [DEPTH, NMEM, 256])
    mv = dout("mv", [DEPTH, NMEM, 256])
    scpT = dout("scpT", [2, 128, 6, 2])
    gcpT = dout("gcpT", [2, 128, 18, 3])
    gsp = dout("gsp", [2, 128, GDN_H, 128])
    ffpT = dout("ffpT", [DEPTH, 128, 44, 2])
    scsT = dout("scsT", [2, SC_DIM, 2, NB])
    gcsT = dout("gcsT", [2, 2304, 3, NB])
    gss = dout("gss", [2, NB, 128, GDN_H, 128])
    ffsT = dout("ffsT", [DEPTH, 2 * D_FF, 2, NB])

    es = contextlib.ExitStack()
    with es:
        S = Sched(nc, es)

        def sb(name, shape, dt=F32):
            return es.enter_context(nc.sbuf_tensor(name, list(shape), dt))

        xres = sb("xres", [128, 8, W]); r_xres = Res("xres")
        xbf = sb("xbf", [128, 8, W], BF16); r_xbf = Res("xbf")
        obuf = sb("obuf", [128, 8, W], BF16); r_obuf = Res("obuf")
        cst = sb("cst", [128, NCST]); r_cst = Res("cst")
        spt = sb("spt", [128, NSP]); r_sp = Res("sp")
        mask6 = sb("mask6", [64, 6, 64]); r_mask6 = Res("mask6")
        identb = sb("identb", [128, 128], BF16)
        onesb = sb("onesb", [128, 128], BF16)
        meanb = sb("meanb", [128, 128], BF16)
        r_cb = Res("constb")
        sc_carry = sb("sc_carry", [128, 2, 6, 2]); r_scc = Res("scc")
        gc_carry = sb("gc_carry", [128, 2, 18, 3]); r_gcc = Res("gcc")
        ff_carry = sb("ff_carry", [128, DEPTH, 44, 2]); r_ffc = Res("ffc")
        Sst = sb("Sst", [128, 2, GDN_H, 128]); r_Sst = [Res("Sst0"), Res("Sst1")]
        wba = sb("wba", [128, 8, 12], BF16); r_wba = Res("wba")
        negA = sb("negA", [128, 6]); r_negA = Res("negA")
        g_beta = sb("g_beta", [64, 96]); g_g = sb("g_g", [64, 96]); g_gc = sb("g_gc", [64, 96])
        g_egc = sb("g_egc", [64, 96]); g_negegc = sb("g_negegc", [64, 96]); g_kds = sb("g_kds", [64, 96])
        g_tmp = sb("g_tmp", [64, 96]); g_gam = sb("g_gam", [128, 96])
        r_gsc = Res("gsc")
        NSLOT = 3
        wslots = [(sb(f"wslot{i}", [128, 4096], BF16), Res(f"wslot{i}"), S.dsem()) for i in range(NSLOT)]
        BIGA = sb("BIGA", [128, 24 * W], BF16)
        SCRN = 10240
        SCR = sb("SCR", [128, SCRN])

        psum = [(es.enter_context(nc.psum_tensor(f"ps{i}", [128, 512], F32)), Res(f"ps{i}", True)) for i in range(7)]
        psbf = (es.enter_context(nc.psum_tensor("psbf", [128, 1024], BF16)), Res("psbf", True))
        psi = [0]

        psg = {"A": [0, (0, 1, 2)], "B": [0, (3, 4, 5, 6)]}

        def ps_next(which=None):
            if which is not None:
                st = psg[which]
                b = psum[st[1][st[0] % len(st[1])]]
                st[0] += 1
                return b
            b = psum[psi[0] % 7]
            psi[0] += 1
            return b

        d_in = S.dsem()
        d_x = S.dsem()
        d_h = [S.dsem(), S.dsem()]
        d_ss = [S.dsem(), S.dsem()]
        d_memb = S.dsem(); d_ckv = S.dsem(); d_wba = S.dsem(); d_kvn = S.dsem()
        d_oh = [S.dsem(), S.dsem()]; d_so = [S.dsem(), S.dsem()]; d_y = S.dsem(); d_fin = S.dsem()
        out_dsems = [d_kvn, d_oh[0], d_oh[1], d_so[0], d_so[1], d_y, d_fin]

        ident = cst[:, C_ID:C_ID + 128]
        ones = cst[:, C_ONE:C_ONE + 128]
        tri = cst[:, C_TRI:C_TRI + 64]
        maskT = cst[:, C_MASK:C_MASK + 64]
        strU = cst[:, C_STR:C_STR + 64]

        def spv(off, n=1):
            return spt[:, off:off + n]

        qkvz = BIGA[:, :].rearrange("p (c w) -> p c w", c=24)
        r_qkvz = Res("qkvz")
        actb = BIGA[:, 0:22 * W].rearrange("p (c w) -> p c w", c=22)
        r_actb = Res("actb")
        ln_zb = BIGA[:, 0:4096].rearrange("p (k n) -> p k n", k=8)
        ln_sqb = BIGA[:, 4096:8192].rearrange("p (k n) -> p k n", k=8)
        ln_f32 = BIGA[:, 8192:8192 + 2 * (4096 + 1024)].bitcast(F32)
        ln_t1 = ln_f32[:, 0:4096].rearrange("p (k n) -> p k n", k=8)
        ln_mean = ln_f32[:, 4096:4608]
        ln_rstd = ln_f32[:, 4608:5120]
        r_zb = Res("ln_zb"); r_sqb = Res("ln_sqb"); r_t1 = Res("ln_t1"); r_mean = Res("ln_mean"); r_rstd = Res("ln_rstd")
        ckb = BIGA[:, 0:8192].rearrange("p (c b m) -> p c b m", c=2, b=NB)
        cvb = BIGA[:, 8192:16384].rearrange("p (c b m) -> p c b m", c=2, b=NB)
        r_ckv = Res("ckv")
        biga_groups = {"qkvz": [r_qkvz], "act": [r_actb], "ln": [r_zb, r_sqb, r_t1, r_mean, r_rstd], "ckv": [r_ckv]}
        biga_cur = [None]

        def biga_phase(name):
            if biga_cur[0] is not None and biga_cur[0] != name:
                fence(biga_groups[name], biga_groups[biga_cur[0]])
            biga_cur[0] = name

        scr_res = {}

        def scrv(phase, name, off, n):
            key = (phase, name)
            if key not in scr_res:
                scr_res[key] = Res(f"scr_{phase}_{name}")
            assert off + n <= SCRN, (phase, name, off, n)
            return SCR[:, off:off + n], scr_res[key]

        scr_cur = [None]

        def scr_phase(name):
            if scr_cur[0] is not None and scr_cur[0] != name:
                new = [r for (ph, _), r in scr_res.items() if ph == name]
                old = [r for (ph, _), r in scr_res.items() if ph == scr_cur[0]]
                fence(new, old)
            scr_cur[0] = name

        RAWN = 3 + NCP + 1
        conv_raw = [scrv("conv", f"raw{i}", i * RAWN, RAWN) for i in range(2)]
        o = 2 * RAWN
        conv_raws = [scrv("conv", f"raws{i}", o + i * 112, 112) for i in range(2)]
        o += 224
        conv_acc = [scrv("conv", f"acc{i}", o + i * W, W) for i in range(2)]
        o += 2 * W
        conv_tmp = [scrv("conv", f"tmp{i}", o + i * 512, 512) for i in range(2)]
        o += 1024
        conv_bgb = [scrv("conv", f"bgb{i}", o + i * W, W) for i in range(2)]
        o += 2 * W
        conv_sq = scrv("conv", "sq", o, W // 2)
        o += W // 2
        conv_rs = scrv("conv", "rs", o, 512)
        o += 512
        assert o <= SCRN, o
        o = 0
        ffn_raw = [[None, None], [None, None]]; ffn_raws = [[None, None], [None, None]]; ffn_acc = [[None, None], [None, None]]
        for hf in range(2):
            for pr_ in range(2):
                ffn_raw[hf][pr_] = scrv("ffn", f"raw{hf}{pr_}", o, RAWN); o += RAWN
                ffn_raws[hf][pr_] = scrv("ffn", f"raws{hf}{pr_}", o, 112); o += 112
                ffn_acc[hf][pr_] = scrv("ffn", f"acc{hf}{pr_}", o, W); o += W
        assert o <= SCRN, o
        d_hf = [[S.dsem(), S.dsem()], [S.dsem(), S.dsem()]]
        d_ohf = [[S.dsem(), S.dsem()], [S.dsem(), S.dsem()]]
        out_dsems += [d_ohf[0][0], d_ohf[0][1], d_ohf[1][0], d_ohf[1][1]]
        o = 0
        att_qbuf = scrv("att", "qbuf", o, W); o += W
        att_e = [scrv("att", f"e{i}", o + i * 256, 256) for i in range(2)]; o += 512
        att_rden = scrv("att", "rden", o, 512); o += 512
        att_kvn = scrv("att", "kvn", o, 512); o += 512
        att_KT = scrv("att", "KT", o, 256); o += 256
        att_V = scrv("att", "V", o, 256); o += 256
        att_memb = scrv("att", "memb", o, 1024); o += 1024
        att_es = scrv("att", "es", o, 256); o += 256
        o = 0
        gd = {}
        for nm, n in [("gtri", 384), ("dT", 384), ("U", 192), ("Pa", 192), ("Pb", 192), ("PTa", 192), ("PTb", 192),
                      ("Va", 192), ("Vb", 192),
                      ("Vf0", 192), ("Vf1", 192), ("qkT0", 192), ("qkT1", 192), ("kdec0", 384), ("kdec1", 384),
                      ("vtm0", 384), ("vtm1", 384),
                      ("Y", 384), ("vnew", 384), ("ot", 768), ("sq", 768), ("ssq", 16),
                      ("Ss0", 768), ("Ss1", 768), ("Ssb0", 384), ("Ssb1", 384), ("Spb", 384)]:
            gd[nm] = scrv("gdn", nm, o, n)
            o += n
        assert o <= SCRN, o

        class WS:
            def __init__(self):
                self.plan = []
                self.issued = 0
                self.popped = 0
                self.planning = True
                self.PREF = 2

            def _issue(self, s):
                tile_, res, ds = wslots[s % NSLOT]
                KC, parts = self.plan[s]
                nw = sum(n for _, n, _ in parts)
                v = tile_[:, 0:KC * nw].rearrange("p (k n) -> p k n", k=KC)
                for off, n, src in parts:
                    S.dma("pool", v[:, :, off:off + n], src.rearrange("(k p) n -> p k n", p=128), [], [res], ds)

            def stage(self, KC, parts):
                if self.planning:
                    self.plan.append((KC, parts))
                    return None, None
                while self.issued < min(len(self.plan), self.popped + self.PREF + 1):
                    self._issue(self.issued)
                    self.issued += 1
                s = self.popped
                self.popped += 1
                KCp, partsp = self.plan[s]
                assert KCp == KC and len(partsp) == len(parts)
                tile_, res, ds = wslots[s % NSLOT]
                nw = sum(n for _, n, _ in parts)
                return tile_[:, 0:KC * nw].rearrange("p (k n) -> p k n", k=KC), res

        ws = WS()

        def mm_group(out_ap, pres, pairs, reads):
            def fn(e):
                ins = None
                n = len(pairs)
                for i, (l, r) in enumerate(pairs):
                    ins = e.matmul(out_ap, lhsT=l, rhs=r, start=(i == 0), stop=(i == n - 1))
                return ins
            S.emit("pe", fn, reads, [pres])

        def mm_multi(items, pres_list, reads):
            def fn(e):
                ins = None
                for out_ap, pairs in items:
                    n = len(pairs)
                    for i, (l, r) in enumerate(pairs):
                        ins = e.matmul(out_ap, lhsT=l, rhs=r, start=(i == 0), stop=(i == n - 1))
                return ins
            S.emit("pe", fn, reads, pres_list)

        def tr_multi(items, pres_list, reads):
            def fn(e):
                ins = None
                for out_ap, in_ap, id_ap in items:
                    ins = e.transpose(out_ap, in_ap, id_ap)
                return ins
            S.emit("pe", fn, reads, pres_list)

        def coltiles(p):
            ct = [(0, 512), (512, 512)]
            if p == 1:
                ct.append((NCP, NSC))
            return ct

        def is_s(c0):
            return c0 >= NCP

        def program():
            psi[0] = 0
            biga_cur[0] = None
            scr_cur[0] = None
            S.dma("sp", cst[:, :], cstd[:, :], [], [r_cst], d_in)
            S.dma("sp", spt[:, :], spd[:, :], [], [r_sp], d_in)
            S.emit("dve", lambda e: e.tensor_copy(out=identb[:, :], in_=ident), [r_cst], [r_cb])
            S.emit("dve", lambda e: e.tensor_copy(out=onesb[:, :], in_=ones), [r_cst], [r_cb])
            S.emit("dve", lambda e: e.tensor_scalar(out=meanb[:, :], in0=ones, scalar1=1.0 / D, scalar2=None,
                                                    op0=ALU.mult), [r_cst], [r_cb])
            S.emit("dve", lambda e: e.tensor_copy(
                out=mask6[:, :, :], in_=maskT[0:64, :].unsqueeze(1).to_broadcast([64, 6, 64])), [r_cst], [r_mask6])
            S.emit("dve", lambda e: e.memset(obuf[:, :, :], 0.0), [], [r_obuf])
            S.emit("dve", lambda e: e.memset(sc_carry[:, :, :, :], 0.0), [], [r_scc])
            S.emit("dve", lambda e: e.memset(gc_carry[:, :, :, :], 0.0), [], [r_gcc])
            S.emit("dve", lambda e: e.memset(ff_carry[:, :, :, :], 0.0), [], [r_ffc])
            for j in range(2):
                S.emit("dve", lambda e, j=j: e.memset(Sst[:, j, :, :], 0.0), [], [r_Sst[j]])

            for p in range(npass):
                ct = coltiles(p)
                ncol = NCP + (NSC if p == 1 else 0)
                S.dma("sp", xres[:, :, 0:NCP], xpT[:, p * NCP:(p + 1) * NCP].rearrange("(k q) t -> q k t", q=128),
                      [], [r_xres], d_x)
                if p == 1:
                    S.dma("sp", xres[:, :, NCP:W], xsT[:, :].rearrange("(k q) t -> q k t", q=128), [], [r_xres], d_x)
                for kc in range(8):
                    en = ("act", "dve")[kc % 2]
                    if en == "act":
                        S.emit("act", lambda e, kc=kc: e.activation(out=xbf[:, kc, 0:ncol], in_=xres[:, kc, 0:ncol],
                                                                  func=AF.Copy), [r_xres], [r_xbf])
                    else:
                        S.emit(en, lambda e, kc=kc: e.tensor_copy(out=xbf[:, kc, 0:ncol], in_=xres[:, kc, 0:ncol]),
                               [r_xres], [r_xbf])
                for l in range(nlayers):
                    if l % 2 == 0:
                        mixer_a(l, p)
                    else:
                        mixer_b(l, p)
                    if "ln" in KPARTS:
                        layernorm(SP_LN1G + l * 8, SP_LN1B + l * 8, p)
                    if "ffn" in KPARTS:
                        ffn(l, p)
                    if "ln" in KPARTS:
                        layernorm(SP_LN2G + l * 8, SP_LN2B + l * 8, p)
                S.dma("sp", ypT[:, p * NCP:(p + 1) * NCP].rearrange("(k q) t -> q k t", q=128), xres[:, :, 0:NCP],
                      [r_xres], [], d_y)
                if p == 1:
                    S.dma("sp", ysT[:, :].rearrange("(k q) t -> q k t", q=128), xres[:, :, NCP:W], [r_xres], [], d_y)
            for j in range(2):
                S.dma("sp", scpT[j], sc_carry[:, j, :, :], [r_scc], [], d_fin)
                S.dma("sp", gcpT[j], gc_carry[:, j, :, :], [r_gcc], [], d_fin)
                S.dma("sp", gsp[j], Sst[:, j, :, :], [r_Sst[j]], [], d_fin)
            for l in range(DEPTH):
                S.dma("sp", ffpT[l], ff_carry[:, l, :, :], [r_ffc], [], d_fin)
            if not S.dry:
                for ds in out_dsems:
                    if ds.cnt > 0:
                        S.eng["sp"]["h"].wait_ge(ds.sem, ds.cnt)

        def conv_taps(p, accv, racc, rawv, rraw, rawsv, rraws, woff, Wd, only=None):
            Hh = Wd - 1
            jjs = [jj for jj in range(Wd) if only is None or (only == "first") == (jj == 0)]
            a_p = accv[:, 0:NCP]
            for jj in jjs:
                src = rawv[:, 3 - Hh + jj:3 - Hh + jj + NCP]
                wap = spv(woff + jj)
                if jj == 0 and CONV_ACT:
                    S.emit("act", lambda e, src=src, wap=wap: e.activation(
                        out=a_p, in_=src, func=AF.Copy, scale=wap), [rraw, r_sp], [racc])
                elif jj == 0:
                    S.emit("dve", lambda e, src=src, wap=wap: e.tensor_scalar(
                        out=a_p, in0=src, scalar1=wap, scalar2=None, op0=ALU.mult), [rraw, r_sp], [racc])
                else:
                    S.emit("dve", lambda e, src=src, wap=wap: e.scalar_tensor_tensor(
                        out=a_p, in0=src, scalar=wap, in1=a_p, op0=ALU.mult, op1=ALU.add), [rraw, r_sp, racc], [racc])
            if p == 1:
                a_s = accv[:, NCP:W].rearrange("p (b t) -> p b t", b=NB)
                rs3 = rawsv.rearrange("p (t b) -> p b t", b=NB)
                for jj in jjs:
                    src = rs3[:, :, 3 - Hh + jj:3 - Hh + jj + TS]
                    wap = spv(woff + jj)
                    if jj == 0 and CONV_ACT:
                        S.emit("act", lambda e, src=src, wap=wap: e.activation(
                            out=a_s, in_=src, func=AF.Copy, scale=wap), [rraws, r_sp], [racc])
                    elif jj == 0:
                        S.emit("dve", lambda e, src=src, wap=wap: e.tensor_scalar(
                            out=a_s, in0=src, scalar1=wap, scalar2=None, op0=ALU.mult), [rraws, r_sp], [racc])
                    else:
                        S.emit("dve", lambda e, src=src, wap=wap: e.scalar_tensor_tensor(
                            out=a_s, in0=src, scalar=wap, in1=a_s, op0=ALU.mult, op1=ALU.add),
                            [rraws, r_sp, racc], [racc])

        def raw_dst(c0, n, rawv, rawsv):
            if is_s(c0):
                return rawsv.rearrange("p (t b) -> p b t", b=NB)[:, :, 3:3 + TS]
            return rawv[:, 3 + c0:3 + c0 + n]

        def ps_src(psap, c0, n):
            if is_s(c0):
                return psap[:, 0:NSC].rearrange("p (b t) -> p b t", b=NB)
            return psap[:, 0:n]

        def attention(l, p, w_in, qcol0):
            scr_phase("att")
            qbuf, r_q = att_qbuf
            qb = qbuf.bitcast(BF16).rearrange("p (c w) -> p c w", c=2)
            KT = att_KT[0].bitcast(BF16).rearrange("p (c m) -> p c m", c=2); r_KT = att_KT[1]
            Vt = att_V[0].bitcast(BF16).rearrange("p (c m) -> p c m", c=2); r_V = att_V[1]
            memb = att_memb[0].bitcast(BF16).rearrange("p (k m) -> p k m", k=8); r_memb = att_memb[1]
            kvn, r_kvn = att_kvn
            S.dma("pool", memb, memT[:, :].rearrange("(k q) m -> q k m", q=128), [], [r_memb], d_memb)
            wt, rw = ws.stage(8, [(0, 512, w_mem_kv[l])])
            if not S.dry:
                for mc in range(2):
                    pt, pr = ps_next()
                    mm_group(pt[:, :], pr, [(memb[:, kc, mc * 128:(mc + 1) * 128], wt[:, kc, :]) for kc in range(8)],
                             [r_memb, rw])
                    S.emit("dve", lambda e, pt=pt, mc=mc: e.tensor_copy(out=Vt[:, mc, :], in_=pt[:, 256:512]), [pr], [r_V])
                    if p == 0:
                        S.emit("act", lambda e, pt=pt: e.activation(out=kvn, in_=pt[:, :], func=AF.Copy), [pr], [r_kvn])
                        S.dma("sp", mk[l, mc * 128:(mc + 1) * 128, :], kvn[:, 0:256], [r_kvn], [], d_kvn)
                        S.dma("sp", mv[l, mc * 128:(mc + 1) * 128, :], kvn[:, 256:512], [r_kvn], [], d_kvn)
                for c in range(2):
                    pt, pr = ps_next()
                    mm_group(pt[:, 0:256], pr, [(wt[:, kc, c * 128:(c + 1) * 128], memb[:, kc, :]) for kc in range(8)],
                             [r_memb, rw])
                    S.emit("act", lambda e, pt=pt, c=c: e.activation(out=KT[:, c, :], in_=pt[:, 0:256], func=AF.Copy),
                           [pr], [r_KT])
            if p == 1:
                biga_phase("ckv")
                for c_ in range(2):
                    S.dma("pool", ckb[:, c_, :, :], ckT[l, c_ * 128:(c_ + 1) * 128, :, :], [], [r_ckv], d_ckv)
                    S.dma("pool", cvb[:, c_, :, :], cv[l, :, c_ * 128:(c_ + 1) * 128, :].rearrange("b q e -> q b e"), [], [r_ckv], d_ckv)
            wt, rw = ws.stage(8, [(0, 256, w_in[:, qcol0:qcol0 + 256])])
            if S.dry:
                return
            for c in range(2):
                for (c0, n) in coltiles(p):
                    pt, pr = ps_next()
                    mm_group(pt[:, 0:n], pr, [(wt[:, kc, c * 128:(c + 1) * 128], xbf[:, kc, c0:c0 + n]) for kc in range(8)],
                             [r_xbf, rw])
                    S.emit("act", lambda e, pt=pt, c=c, c0=c0, n=n: e.activation(
                        out=qb[:, c, c0:c0 + n], in_=pt[:, 0:n], func=AF.Copy, scale=0.125), [pr], [r_q])
            rden, r_rden = att_rden
            for h in range(4 if "att2" in KPARTS else 0):
                c = h // 2
                pb = (h % 2) * 64
                for (c0, n) in ((0, 512), (512, 512)):
                    evs = []
                    for mc in range(2):
                        pt, pr = ps_next()
                        mm_group(pt[:, 0:n], pr, [(KT[pb:pb + 64, c, mc * 128:(mc + 1) * 128], qb[pb:pb + 64, c, c0:c0 + n])],
                                 [r_KT, r_q])
                        ev, r_ev = att_e[mc]
                        evb = ev.bitcast(BF16)
                        S.emit("act", lambda e, pt=pt, evb=evb, n=n: e.activation(out=evb[:, 0:n], in_=pt[:, 0:n], func=AF.Exp),
                               [pr], [r_ev])
                        evs.append((evb, r_ev))
                    pso, pro = ps_next()
                    mm_group(pso[pb:pb + 64, 0:n], pro,
                             [(Vt[:, mc, h * 64:(h + 1) * 64], evs[mc][0][:, 0:n]) for mc in range(2)],
                             [r_V, evs[0][1], evs[1][1]])
                    psd, prd = ps_next()
                    mm_group(psd[pb:pb + 64, 0:n], prd, [(onesb[:, 0:64], evs[mc][0][:, 0:n]) for mc in range(2)],
                             [r_cb, evs[0][1], evs[1][1]])
                    S.emit("dve", lambda e, psd=psd, pb=pb, n=n: e.reciprocal(out=rden[pb:pb + 64, 0:n], in_=psd[pb:pb + 64, 0:n]),
                           [prd], [r_rden])
                    S.emit("dve", lambda e, pso=pso, pb=pb, n=n, c=c, c0=c0: e.tensor_tensor(
                        out=obuf[pb:pb + 64, 6 + c, c0:c0 + n], in0=pso[pb:pb + 64, 0:n], in1=rden[pb:pb + 64, 0:n],
                        op=ALU.mult), [pro, r_rden], [r_obuf])
            if p == 1:
                esb = att_es[0].bitcast(BF16); r_es = att_es[1]
                pt, pr = ps_next()
                items = []
                for b in range(NB):
                    for h in range(4):
                        c = h // 2
                        pb = (h % 2) * 64
                        for mc in range(2):
                            idx = ((b * 4 + h) * 2 + mc) * TS
                            items.append((pt[:, idx:idx + TS],
                                          [(ckb[pb:pb + 64, c, b, mc * 128:(mc + 1) * 128],
                                            qb[pb:pb + 64, c, NCP + b * TS:NCP + (b + 1) * TS])]))
                mm_multi(items, [pr], [r_ckv, r_q])
                S.emit("act", lambda e: e.activation(out=esb[:, :], in_=pt[:, :], func=AF.Exp), [pr], [r_es])
                pso, pro = ps_next()
                psd, prd = ps_next()
                items = []
                for b in range(NB):
                    for h in range(4):
                        c = h // 2
                        pb = (h % 2) * 64
                        oc0 = c * NSC + b * TS
                        prs_o = []
                        prs_d = []
                        for mc in range(2):
                            idx = ((b * 4 + h) * 2 + mc) * TS
                            prs_o.append((cvb[:, mc, b, h * 64:(h + 1) * 64], esb[:, idx:idx + TS]))
                            prs_d.append((onesb[:, 0:64], esb[:, idx:idx + TS]))
                        items.append((pso[pb:pb + 64, oc0:oc0 + TS], prs_o))
                        items.append((psd[pb:pb + 64, oc0:oc0 + TS], prs_d))
                mm_multi(items, [pro, prd], [r_ckv, r_es, r_cb])
                S.emit("dve", lambda e: e.reciprocal(out=rden[:, 0:128], in_=psd[:, 0:128]), [prd], [r_rden])
                S.emit("dve", lambda e: e.tensor_tensor(
                    out=obuf[:, 6:8, NCP:W], in0=pso[:, 0:128].rearrange("p (c t) -> p c t", c=2),
                    in1=rden[:, 0:128].rearrange("p (c t) -> p c t", c=2), op=ALU.mult), [pro, r_rden], [r_obuf])

        def out_proj(w_out, p):
            for st in range(2):
                wt, rw = ws.stage(8, [(0, 512, w_out[:, st * 512:(st + 1) * 512])])
                if S.dry:
                    continue
                for cc in range(4):
                    oc = st * 4 + cc
                    for (c0, n) in coltiles(p):
                        pt, pr = ps_next()
                        mm_group(pt[:, 0:n], pr, [(wt[:, kc, cc * 128:(cc + 1) * 128], obuf[:, kc, c0:c0 + n]) for kc in range(8)],
                                 [r_obuf, rw])
                        S.emit("dve", lambda e, pt=pt, oc=oc, c0=c0, n=n: e.scalar_tensor_tensor(
                            out=xres[:, oc, c0:c0 + n], in0=xres[:, oc, c0:c0 + n], scalar=ALPHA, in1=pt[:, 0:n],
                            op0=ALU.mult, op1=ALU.add), [pr, r_xres], [r_xres])

        def mixer_a(l, p):
            j = l // 2
            Wm = w_in_a[j]
            if "att" in KPARTS:
                attention(l, p, Wm, 3 * SC_DIM)
            scr_phase("conv")
            ncol = NCP + (NSC if p == 1 else 0)

            def stage1(c):
                wt, rw = ws.stage(8, [(0, 128, Wm[:, c * 128:(c + 1) * 128]),
                                      (128, 128, Wm[:, SC_DIM + c * 128:SC_DIM + (c + 1) * 128]),
                                      (256, 128, Wm[:, 2 * SC_DIM + c * 128:2 * SC_DIM + (c + 1) * 128])])
                if S.dry:
                    return
                rawv, rraw = conv_raw[c % 2]
                rawsv, rraws = conv_raws[c % 2]
                accv, racc = conv_acc[c % 2]
                bgb, rbg = conv_bgb[c % 2]
                S.emit("dve", lambda e, c=c: e.tensor_copy(out=rawv[:, 1:3], in_=sc_carry[:, j, c, :]), [r_scc], [rraw])
                if p == 1:
                    S.dma("sp", rawsv[:, NB:3 * NB].rearrange("p (r b) -> p r b", b=NB), scT[j, c * 128:(c + 1) * 128, :, :],
                          [], [rraws], d_h[c % 2])
                for ti, (c0, n) in enumerate(coltiles(p)):
                    pss = []
                    for part in range(3):
                        pt, pr = ps_next()
                        mm_group(pt[:, 0:n], pr, [(wt[:, kc, part * 128:(part + 1) * 128], xbf[:, kc, c0:c0 + n])
                                                 for kc in range(8)], [r_xbf, rw])
                        pss.append((pt, pr))
                    tmpv, rtmp = conv_tmp[ti % 2]
                    S.emit("act", lambda e, pt=pss[0][0], n=n: e.activation(out=tmpv[:, 0:n], in_=pt[:, 0:n], func=AF.Copy),
                           [pss[0][1]], [rtmp])
                    dst = raw_dst(c0, n, rawv, rawsv)
                    in1 = tmpv[:, 0:NSC].rearrange("p (b t) -> p b t", b=NB) if is_s(c0) else tmpv[:, 0:n]
                    S.emit("dve", lambda e, pt=pss[2][0], dst=dst, in1=in1, c0=c0, n=n: e.tensor_tensor(
                        out=dst, in0=ps_src(pt, c0, n), in1=in1, op=ALU.mult),
                        [pss[2][1], rtmp], [rraws if is_s(c0) else rraw])
                    S.emit("act", lambda e, pt=pss[1][0], c0=c0, n=n: e.activation(out=bgb[:, c0:c0 + n], in_=pt[:, 0:n],
                                                                                 func=AF.Copy), [pss[1][1]], [rbg])
                conv_taps(p, accv, racc, rawv, rraw, rawsv, rraws, SP_CONVA + (j * 6 + c) * 3, 3, only="first")

            def stage2(c):
                rawv, rraw = conv_raw[c % 2]
                rawsv, rraws = conv_raws[c % 2]
                accv, racc = conv_acc[c % 2]
                bgb, rbg = conv_bgb[c % 2]
                conv_taps(p, accv, racc, rawv, rraw, rawsv, rraws, SP_CONVA + (j * 6 + c) * 3, 3, only="rest")
                S.emit("dve", lambda e, c=c: e.tensor_copy(out=sc_carry[:, j, c, :], in_=rawv[:, 3 + NCP - 2:3 + NCP]),
                       [rraw], [r_scc])
                if p == 1:
                    S.dma("sp", scsT[j, c * 128:(c + 1) * 128, :, :],
                          rawsv[:, 5 * NB:7 * NB].rearrange("p (r b) -> p r b", b=NB), [rraws], [], d_oh[c % 2])
                S.emit("dve", lambda e, c=c: e.tensor_tensor(out=obuf[:, c, 0:ncol], in0=bgb[:, 0:ncol],
                                                             in1=accv[:, 0:ncol], op=ALU.mult),
                       [rbg, racc], [r_obuf])

            if "mix" in KPARTS:
                stage1(0)
                for c in range(6):
                    if c + 1 < 6:
                        stage1(c + 1)
                    if not S.dry:
                        stage2(c)
            if "out" in KPARTS:
                out_proj(w_out_a[j], p)

        def mixer_b(l, p):
            j = l // 2
            Wm = w_in_b[j]
            attention(l, p, Wm, 4 * SC_DIM + 12)
            scr_phase("conv")
            biga_phase("qkvz")
            ncol = NCP + (NSC if p == 1 else 0)
            S.dma("pool", wba[:, :, :], wbaT[j], [], [r_wba], d_wba)
            S.emit("act", lambda e: e.activation(out=negA[:, :], in_=spv(SP_ALOG + j * 6, 6), func=AF.Exp), [r_sp], [r_negA])
            S.emit("dve", lambda e: e.tensor_scalar(out=negA[:, :], in0=negA[:, :], scalar1=-1.0, scalar2=None, op0=ALU.mult),
                   [r_negA], [r_negA])
            sqv = conv_sq[0].bitcast(BF16); rsq = conv_sq[1]
            rsv, rrs = conv_rs
            wcur = [None]

            def stage1(c):
                st, cc = divmod(c, 4)
                if cc == 0:
                    wcur[0] = ws.stage(8, [(0, 512, Wm[:, st * 512:(st + 1) * 512])])
                if S.dry:
                    return
                wt, rw = wcur[0]
                if c >= 18:
                    for (c0, n) in coltiles(p):
                        pt, pr = ps_next()
                        mm_group(pt[:, 0:n], pr, [(wt[:, kc, cc * 128:(cc + 1) * 128], xbf[:, kc, c0:c0 + n])
                                                 for kc in range(8)], [r_xbf, rw])
                        S.emit("act", lambda e, pt=pt, c=c, c0=c0, n=n: e.activation(
                            out=qkvz[:, c, c0:c0 + n], in_=pt[:, 0:n], func=AF.Silu), [pr], [r_qkvz])
                    return
                rawv, rraw = conv_raw[c % 2]
                rawsv, rraws = conv_raws[c % 2]
                accv, racc = conv_acc[c % 2]
                S.emit("dve", lambda e, c=c: e.tensor_copy(out=rawv[:, 0:3], in_=gc_carry[:, j, c, :]), [r_gcc], [rraw])
                if p == 1:
                    S.dma("sp", rawsv[:, 0:3 * NB].rearrange("p (r b) -> p r b", b=NB),
                          gcT[j, c * 128:(c + 1) * 128, :, :], [], [rraws], d_h[c % 2])
                for (c0, n) in coltiles(p):
                    pt, pr = ps_next()
                    mm_group(pt[:, 0:n], pr, [(wt[:, kc, cc * 128:(cc + 1) * 128], xbf[:, kc, c0:c0 + n])
                                             for kc in range(8)], [r_xbf, rw])
                    dst = raw_dst(c0, n, rawv, rawsv)
                    S.emit("act", lambda e, pt=pt, dst=dst, c0=c0, n=n: e.activation(
                        out=dst, in_=ps_src(pt, c0, n), func=AF.Copy), [pr], [rraws if is_s(c0) else rraw])
                conv_taps(p, accv, racc, rawv, rraw, rawsv, rraws, SP_CONVB + (j * 18 + c) * 4, 4, only="first")

            def stage2(c):
                if c >= 18:
                    return
                rawv, rraw = conv_raw[c % 2]
                rawsv, rraws = conv_raws[c % 2]
                accv, racc = conv_acc[c % 2]
                conv_taps(p, accv, racc, rawv, rraw, rawsv, rraws, SP_CONVB + (j * 18 + c) * 4, 4, only="rest")
                S.emit("dve", lambda e, c=c: e.tensor_copy(out=gc_carry[:, j, c, :], in_=rawv[:, 3 + NCP - 3:3 + NCP]),
                       [rraw], [r_gcc])
                if p == 1:
                    S.dma("sp", gcsT[j, c * 128:(c + 1) * 128, :, :],
                          rawsv[:, 4 * NB:7 * NB].rearrange("p (r b) -> p r b", b=NB), [rraws], [], d_oh[c % 2])
                if c >= 12:
                    S.emit("act", lambda e, c=c: e.activation(out=qkvz[:, c, 0:ncol], in_=accv[:, 0:ncol], func=AF.Silu),
                           [racc], [r_qkvz])
                    return
                S.emit("act", lambda e: e.activation(out=accv[:, 0:ncol], in_=accv[:, 0:ncol], func=AF.Silu),
                       [racc], [racc])
                S.emit("act", lambda e: e.activation(out=sqv[:, 0:ncol], in_=accv[:, 0:ncol], func=AF.Square), [racc], [rsq])
                ebias = float(np.log(128.0 ** -0.5)) if c < 6 else 0.0
                for (c0, n) in coltiles(p):
                    pt, pr = ps_next()
                    mm_group(pt[:, 0:n], pr, [(onesb[:, :], sqv[:, c0:c0 + n])], [r_cb, rsq])
                    S.emit("dve", lambda e, pt=pt, n=n: e.tensor_scalar(out=rsv[:, 0:n], in0=pt[:, 0:n], scalar1=RMS_EPS,
                                                                       scalar2=None, op0=ALU.add), [pr], [rrs])
                    S.emit("act", lambda e, n=n: e.activation(out=rsv[:, 0:n], in_=rsv[:, 0:n], func=AF.Ln), [rrs], [rrs])
                    if ebias != 0.0:
                        S.emit("dve", lambda e, n=n: e.tensor_scalar(out=rsv[:, 0:n], in0=rsv[:, 0:n], scalar1=-0.5,
                                                                    scalar2=ebias, op0=ALU.mult, op1=ALU.add),
                               [rrs], [rrs])
                        S.emit("act", lambda e, n=n: e.activation(out=rsv[:, 0:n], in_=rsv[:, 0:n], func=AF.Exp),
                               [rrs], [rrs])
                    else:
                        S.emit("act", lambda e, n=n: e.activation(out=rsv[:, 0:n], in_=rsv[:, 0:n], func=AF.Exp,
                                                                  scale=-0.5), [rrs], [rrs])
                    S.emit("dve", lambda e, c=c, c0=c0, n=n: e.tensor_tensor(
                        out=qkvz[:, c, c0:c0 + n], in0=accv[:, c0:c0 + n], in1=rsv[:, 0:n], op=ALU.mult),
                        [racc, rrs], [r_qkvz])

            stage1(0)
            for c in range(24):
                if c + 1 < 24:
                    stage1(c + 1)
                if not S.dry:
                    stage2(c)
            if not S.dry:
                scr_phase("gdn")
                gdn_scalars(p, 64, lambda n: n * 64)
                Spb_ap, r_Spb = gd["Spb"]
                Spb = Spb_ap.bitcast(BF16).rearrange("p (h e) -> p h e", h=GDN_H)
                S.emit("act", lambda e: e.activation(out=Spb, in_=Sst[:, j, :, :], func=AF.Copy), [r_Sst[j]], [r_Spb])
                nxt = run_gens(gdn_A(0, 0, 64), None)
                for n in range(16):
                    cur = nxt
                    gA = gdn_A(n + 1, (n + 1) * 64, 64) if n + 1 < 16 else None
                    gB = gdn_B(j, n, n * 64, 64, Sst[:, j, :, :], r_Sst[j], Spb, r_Spb, cur)
                    nxt = run_gens(gA, gB)
                if p == 1:
                    gdn_scalars(p, TS, lambda n: NCP + n * TS)

                    def load_S(b):
                        St, rS = gd[f"Ss{b % 2}"]
                        Sv = St.rearrange("p (h e) -> p h e", h=GDN_H)
                        Sb_ap, rSb = gd[f"Ssb{b % 2}"]
                        Sb = Sb_ap.bitcast(BF16).rearrange("p (h e) -> p h e", h=GDN_H)
                        S.dma("sp", Sv, gs[j, b], [], [rS], d_ss[b % 2])
                        S.emit("act", lambda e: e.activation(out=Sb, in_=Sv, func=AF.Copy), [rS], [rSb])
                        return Sv, rS, Sb, rSb

                    nxt = run_gens(gdn_A(0, NCP, TS), None)
                    nS = load_S(0)
                    for b in range(NB):
                        cur, cS = nxt, nS
                        gA = None
                        if b + 1 < NB:
                            nS = load_S(b + 1)
                            gA = gdn_A(b + 1, NCP + (b + 1) * TS, TS)
                        gB = gdn_B(j, b, NCP + b * TS, TS, cS[0], cS[1], cS[2], cS[3], cur)
                        nxt = run_gens(gA, gB)
                        S.dma("sp", gss[j, b], cS[0], [cS[1]], [], d_so[b % 2])
            out_proj(w_out_b[j], p)

        def gdn_scalars(p, C, colfn):
            pt, pr = ps_next()
            items = []
            for n in range(16):
                col = colfn(n)
                items.append((pt[0:C, n * 12:(n + 1) * 12], [(xbf[:, kc, col:col + C], wba[:, kc, :]) for kc in range(8)]))
            mm_multi(items, [pr], [r_xbf, r_wba])
            pv = pt[0:C, 0:192].rearrange("p (n k) -> p n k", n=16)
            b3 = g_beta[0:C, :].rearrange("p (n h) -> p n h", n=16)
            t3 = g_tmp[0:C, :].rearrange("p (n h) -> p n h", n=16)
            S.emit("act", lambda e: e.activation(out=b3, in_=pv[:, :, 0:6], func=AF.Exp, scale=-1.0), [pr], [r_gsc])
            S.emit("dve", lambda e: e.tensor_scalar(out=g_beta[0:C, :], in0=g_beta[0:C, :], scalar1=1.0, scalar2=None,
                                                    op0=ALU.add), [r_gsc], [r_gsc])
            S.emit("dve", lambda e: e.reciprocal(out=g_beta[0:C, :], in_=g_beta[0:C, :]), [r_gsc], [r_gsc])
            jdt = spv(SP_DTB + cur_j[0] * 6, 6)
            S.emit("dve", lambda e: e.tensor_tensor(out=t3, in0=pv[:, :, 6:12],
                                                    in1=jdt[0:C, :].unsqueeze(1).to_broadcast([C, 16, 6]), op=ALU.add),
                   [pr, r_sp], [r_gsc])
            S.emit("act", lambda e: e.activation(out=g_tmp[0:C, :], in_=g_tmp[0:C, :], func=AF.Exp), [r_gsc], [r_gsc])
            S.emit("act", lambda e: e.activation(out=g_tmp[0:C, :], in_=g_tmp[0:C, :], func=AF.Ln, bias=1.0), [r_gsc], [r_gsc])
            g3 = g_g[0:C, :].rearrange("p (n h) -> p n h", n=16)
            S.emit("dve", lambda e: e.tensor_tensor(out=g3, in0=t3, in1=negA[0:C, :].unsqueeze(1).to_broadcast([C, 16, 6]),
                                                    op=ALU.mult), [r_gsc, r_negA], [r_gsc])
            pg, prg = ps_next()
            mm_multi([(pg[0:C, 0:96], [(tri[0:C, 0:C], g_g[0:C, :])]),
                      (pg[0:C, 96:192], [(ones[0:C, 0:C], g_g[0:C, :])]),
                      (pg[:, 192:288], [(ones[0:C, 0:128], g_g[0:C, :])])], [prg], [r_cst, r_gsc])
            S.emit("act", lambda e: e.activation(out=g_gc[0:C, :], in_=pg[0:C, 0:96], func=AF.Copy), [prg], [r_gsc])
            S.emit("act", lambda e: e.activation(out=g_egc[0:C, :], in_=pg[0:C, 0:96], func=AF.Exp), [prg], [r_gsc])
            S.emit("dve", lambda e: e.tensor_scalar(out=g_negegc[0:C, :], in0=g_egc[0:C, :], scalar1=-1.0, scalar2=None,
                                                    op0=ALU.mult), [r_gsc], [r_gsc])
            S.emit("dve", lambda e: e.tensor_tensor(out=g_kds[0:C, :], in0=pg[0:C, 96:192], in1=g_gc[0:C, :], op=ALU.subtract),
                   [prg, r_gsc], [r_gsc])
            S.emit("act", lambda e: e.activation(out=g_kds[0:C, :], in_=g_kds[0:C, :], func=AF.Exp), [r_gsc], [r_gsc])
            S.emit("act", lambda e: e.activation(out=g_gam[:, :], in_=pg[:, 192:288], func=AF.Exp), [prg], [r_gsc])

        cur_j = [0]

        def run_gens(gA, gB):
            res = None
            live = [g for g in (gA, gB) if g is not None]
            while live:
                for g in list(live):
                    try:
                        next(g)
                    except StopIteration as e:
                        if g is gA:
                            res = e.value
                        live.remove(g)
            return res

        def gdn_A(n, col0, C):
            H = GDN_H
            HC = H * C
            par = n % 2

            def t3(nm, inner, dt=F32):
                ap, r = gd[nm]
                if dt == BF16:
                    ap = ap.bitcast(BF16)
                return ap[0:C, 0:H * inner].rearrange("p (h x) -> p h x", h=H), r

            def kT(h):
                return qkvz[:, 6 + h, col0:col0 + C]

            def qT(h):
                return qkvz[:, h, col0:col0 + C]

            kdec, r_kdec = t3(f"kdec{par}", 128, BF16)
            vtm, r_vtm = t3(f"vtm{par}", 128, BF16)
            pb_, prb = psbf
            pbv = pb_[0:C, 0:768].rearrange("p (h x) -> p h x", h=H)
            tr_multi([(pb_[0:C, h * 128:(h + 1) * 128], kT(h), identb[:, :]) for h in range(H)], [prb], [r_qkvz, r_cb])
            yield
            S.emit("dve", lambda e: e.tensor_tensor(
                out=kdec, in0=pbv, in1=g_kds[0:C, n * 6:n * 6 + 6].unsqueeze(2).to_broadcast([C, H, 128]), op=ALU.mult),
                [prb, r_gsc], [r_kdec])
            yield
            tr_multi([(pb_[0:C, h * 128:(h + 1) * 128], qkvz[:, 12 + h, col0:col0 + C], identb[:, :]) for h in range(H)],
                     [prb], [r_qkvz, r_cb])
            yield
            S.emit("act", lambda e: e.activation(out=vtm, in_=pbv, func=AF.Copy), [prb], [r_vtm])
            yield
            pK, prK = ps_next("A")
            mm_multi([(pK[0:C, h * C:(h + 1) * C], [(kT(h), kT(h))]) for h in range(H)], [prK], [r_qkvz])
            yield
            pQ, prQ = ps_next("A")
            mm_multi([(pQ[0:C, h * C:(h + 1) * C], [(kT(h), qT(h))]) for h in range(H)], [prQ], [r_qkvz])
            yield
            gtri, r_gtri = t3("gtri", C)
            S.emit("dve", lambda e: e.tensor_tensor(
                out=gtri, in0=tri[0:C, 0:C].unsqueeze(1).to_broadcast([C, H, C]),
                in1=g_g[0:C, n * 6:n * 6 + 6].unsqueeze(2).to_broadcast([C, H, C]), op=ALU.mult), [r_cst, r_gsc], [r_gtri])
            yield
            pD, prD = ps_next("A")
            m6 = mask6[0:C, :, 0:C]
            if C == 64:
                m6f = mask6[0:C, :, :].rearrange("p h x -> p (h x)")
                mm_group(pD[0:C, 0:HC], prD, [(ones[0:C, 0:C], gd["gtri"][0][0:C, 0:HC]), (ident[0:C, 0:C], m6f)],
                         [r_cst, r_gtri, r_mask6])
                yield
            else:
                mm_group(pD[0:C, 0:HC], prD, [(ones[0:C, 0:C], gd["gtri"][0][0:C, 0:HC])], [r_cst, r_gtri])
                yield
            dT, r_dT = t3("dT", C)
            pD3 = pD[0:C, 0:HC].rearrange("p (h x) -> p h x", h=H)
            S.emit("dve", lambda e: e.tensor_tensor(
                out=dT, in0=pD3, in1=g_gc[0:C, n * 6:n * 6 + 6].unsqueeze(2).to_broadcast([C, H, C]), op=ALU.subtract),
                [prD, r_gsc], [r_dT])
            yield
            if C != 64:
                S.emit("dve", lambda e: e.tensor_tensor(out=dT, in0=dT, in1=m6, op=ALU.add), [r_dT, r_mask6], [r_dT])
                yield
            S.emit("act", lambda e: e.activation(out=dT, in_=dT, func=AF.Exp), [r_dT], [r_dT])
            yield
            bsm = gtri
            S.emit("dve", lambda e: e.tensor_tensor(
                out=bsm, in0=strU[0:C, 0:C].unsqueeze(1).to_broadcast([C, H, C]),
                in1=g_beta[0:C, n * 6:n * 6 + 6].unsqueeze(2).to_broadcast([C, H, C]), op=ALU.mult),
                [r_cst, r_gsc, r_gtri], [r_gtri])
            yield
            U, r_U = t3("U", C, BF16)
            pK3 = pK[0:C, 0:HC].rearrange("p (h x) -> p h x", h=H)
            pQ3 = pQ[0:C, 0:HC].rearrange("p (h x) -> p h x", h=H)
            S.emit("dve", lambda e: e.tensor_tensor(out=bsm, in0=bsm, in1=dT, op=ALU.mult), [r_dT, r_gtri], [r_gtri])
            yield
            S.emit("dve", lambda e: e.tensor_tensor(out=U, in0=pK3, in1=bsm, op=ALU.mult), [prK, r_gtri], [r_U])
            yield
            qkT, r_qkT = t3(f"qkT{par}", C, BF16)
            S.emit("dve", lambda e: e.tensor_tensor(out=qkT, in0=pQ3, in1=dT, op=ALU.mult), [prQ, r_dT], [r_qkT])
            yield
            idC = ident[0:C, 0:C]
            idCb = identb[0:C, 0:C]
            PT, r_PT = t3("PTa", C, BF16)
            pT_, prT = ps_next("A")
            mm_multi([(pT_[0:C, h * C:(h + 1) * C], [(U[:, h, :], idCb)]) for h in range(H)], [prT], [r_U, r_cb])
            yield
            S.emit("act", lambda e: e.activation(out=PT, in_=pT_[0:C, 0:HC].rearrange("p (h x) -> p h x", h=H), func=AF.Copy),
                   [prT], [r_PT])
            yield
            Vc, r_Vc = t3("Va", C, BF16)
            S.emit("dve", lambda e: e.tensor_tensor(out=Vc, in0=idC.unsqueeze(1).to_broadcast([C, H, C]), in1=U,
                                                     op=ALU.subtract), [r_cst, r_U], [r_Vc])
            yield
            P, r_P = U, r_U
            nlev = {64: 5, 4: 1}[C]
            pnames = [("Pa", "PTb"), ("Pb", "PTa")]
            vnames = ["Vb", "Va"]
            for lev in range(nlev):
                last = lev == nlev - 1
                nP, nPT = pnames[lev % 2]
                PTn, r_PTn = t3(nPT, C, BF16)
                pA, prA = ps_next("A")
                mm_multi([(pA[0:C, h * C:(h + 1) * C], [(P[:, h, :], PT[:, h, :])]) for h in range(H)], [prA], [r_P, r_PT])
                yield
                S.emit("act", lambda e, pA=pA, PTn=PTn: e.activation(
                    out=PTn, in_=pA[0:C, 0:HC].rearrange("p (h x) -> p h x", h=H), func=AF.Copy), [prA], [r_PTn])
                yield
                if not last:
                    Pn, r_Pn = t3(nP, C, BF16)
                    pB, prB = ps_next("A")
                    mm_multi([(pB[0:C, h * C:(h + 1) * C], [(PT[:, h, :], P[:, h, :])]) for h in range(H)], [prB], [r_P, r_PT])
                    yield
                    S.emit("act", lambda e, pB=pB, Pn=Pn: e.activation(
                        out=Pn, in_=pB[0:C, 0:HC].rearrange("p (h x) -> p h x", h=H), func=AF.Copy), [prB], [r_Pn])
                    yield
                Vn, r_Vn = t3(f"Vf{par}" if last else vnames[lev % 2], C, BF16)
                pV, prV = ps_next("A")
                mm_multi([(pV[0:C, h * C:(h + 1) * C], [(PTn[:, h, :], Vc[:, h, :])]) for h in range(H)], [prV], [r_PTn, r_Vc])
                yield
                S.emit("dve", lambda e, pV=pV, Vn=Vn, Vc=Vc: e.tensor_tensor(
                    out=Vn, in0=pV[0:C, 0:HC].rearrange("p (h x) -> p h x", h=H), in1=Vc, op=ALU.add), [prV, r_Vc], [r_Vn])
                yield
                Vc, r_Vc = Vn, r_Vn
                PT, r_PT = PTn, r_PTn
                if not last:
                    P, r_P = Pn, r_Pn
            return dict(V=Vc, r_V=r_Vc, qkT=qkT, r_qkT=r_qkT, kdec=kdec, r_kdec=r_kdec, vtm=vtm, r_vtm=r_vtm)

        def gdn_B(j, n, col0, C, Sv, rS, Sb, rSb, A):
            H = GDN_H
            HC = H * C

            def t3(nm, inner, dt=F32):
                ap, r = gd[nm]
                if dt == BF16:
                    ap = ap.bitcast(BF16)
                return ap[0:C, 0:H * inner].rearrange("p (h x) -> p h x", h=H), r

            def bc(t, hs):
                return t[0:C, n * 6 + hs:n * 6 + hs + 3].unsqueeze(2).to_broadcast([C, 3, 128])

            def kT(h):
                return qkvz[:, 6 + h, col0:col0 + C]

            def qT(h):
                return qkvz[:, h, col0:col0 + C]

            Vc, r_Vc, qkT, r_qkT = A["V"], A["r_V"], A["qkT"], A["r_qkT"]
            kdec, r_kdec, vtm, r_vtm = A["kdec"], A["r_kdec"], A["vtm"], A["r_vtm"]
            Y, r_Y = t3("Y", 128, BF16)
            vnew, r_vn = t3("vnew", 128, BF16)
            ot, r_ot = t3("ot", 128)
            sq, r_sq = t3("sq", 128)
            idC = ident[0:C, 0:C]
            for hg in range(2):
                hs = 3 * hg
                pk, prk = ps_next("B")
                mm_multi([(pk[0:C, hh * 128:(hh + 1) * 128], [(kT(hs + hh), Sb[:, hs + hh, :])]) for hh in range(3)],
                         [prk], [r_qkvz, rSb])
                yield
                pq, prq = ps_next("B")
                mm_multi([(pq[0:C, hh * 128:(hh + 1) * 128], [(qT(hs + hh), Sb[:, hs + hh, :])]) for hh in range(3)],
                         [prq], [r_qkvz, rSb])
                yield
                pk3 = pk[0:C, 0:384].rearrange("p (h x) -> p h x", h=3)
                pq3 = pq[0:C, 0:384].rearrange("p (h x) -> p h x", h=3)
                S.emit("dve", lambda e, pk3=pk3, hs=hs: e.tensor_tensor(out=sq[:, hs:hs + 3, :], in0=pk3,
                                                                         in1=bc(g_negegc, hs), op=ALU.mult),
                       [prk, r_gsc], [r_sq])
                yield
                S.emit("dve", lambda e, hs=hs: e.tensor_tensor(out=Y[:, hs:hs + 3, :], in0=sq[:, hs:hs + 3, :],
                                                                in1=vtm[:, hs:hs + 3, :], op=ALU.add), [r_sq, r_vtm], [r_Y])
                yield
                px, prx = ps_next("B")
                mm_multi([(px[0:C, hh * 128:(hh + 1) * 128], [(Vc[:, hs + hh, :], Y[:, hs + hh, :])]) for hh in range(3)],
                         [prx], [r_Vc, r_Y])
                yield
                px3 = px[0:C, 0:384].rearrange("p (h x) -> p h x", h=3)
                S.emit("dve", lambda e, px3=px3, hs=hs: e.tensor_tensor(out=vnew[:, hs:hs + 3, :], in0=px3,
                                                                         in1=bc(g_beta, hs), op=ALU.mult),
                       [prx, r_gsc], [r_vn])
                yield
                po, pro = ps_next("B")
                mm_multi([(po[0:C, hh * 128:(hh + 1) * 128], [(qkT[:, hs + hh, :], vnew[:, hs + hh, :])]) for hh in range(3)],
                         [pro], [r_qkT, r_vn])
                yield
                psn, prs = ps_next("B")
                mm_multi([(psn[:, hh * 128:(hh + 1) * 128], [(kdec[:, hs + hh, :], vnew[:, hs + hh, :])]) for hh in range(3)],
                         [prs], [r_kdec, r_vn])
                yield
                ps3 = psn[:, 0:384].rearrange("p (h x) -> p h x", h=3)
                gb = g_gam[:, n * 6 + hs:n * 6 + hs + 3].unsqueeze(2).to_broadcast([128, 3, 128])
                S.emit("dve", lambda e, hs=hs, gb=gb: e.tensor_tensor(out=Sv[:, hs:hs + 3, :], in0=Sv[:, hs:hs + 3, :], in1=gb,
                                                                       op=ALU.mult), [rS, r_gsc], [rS])
                yield
                S.emit("dve", lambda e, hs=hs, ps3=ps3: e.tensor_tensor(out=Sv[:, hs:hs + 3, :], in0=Sv[:, hs:hs + 3, :],
                                                                         in1=ps3, op=ALU.add), [rS, prs], [rS])
                yield
                S.emit("act", lambda e, hs=hs: e.activation(out=Sb[:, hs:hs + 3, :], in_=Sv[:, hs:hs + 3, :], func=AF.Copy),
                       [rS], [rSb])
                yield
                po3 = po[0:C, 0:384].rearrange("p (h x) -> p h x", h=3)
                S.emit("dve", lambda e, pq3=pq3, hs=hs: e.tensor_tensor(out=ot[:, hs:hs + 3, :], in0=pq3,
                                                                         in1=bc(g_egc, hs), op=ALU.mult),
                       [prq, r_gsc], [r_ot])
                yield
                S.emit("dve", lambda e, po3=po3, hs=hs: e.tensor_tensor(out=ot[:, hs:hs + 3, :], in0=ot[:, hs:hs + 3, :],
                                                                         in1=po3, op=ALU.add), [pro, r_ot], [r_ot])
                yield
            ssq_ap, r_ssq = gd["ssq"]
            ssq = ssq_ap[0:C, 0:H]
            S.emit("dve", lambda e: e.tensor_tensor(out=sq, in0=ot, in1=ot, op=ALU.mult), [r_ot, r_sq], [r_sq])
            yield
            S.emit("dve", lambda e: e.tensor_reduce(out=ssq, in_=sq, axis=AX.X, op=ALU.add), [r_sq], [r_ssq])
            yield
            S.emit("dve", lambda e: e.tensor_scalar(out=ssq, in0=ssq, scalar1=1.0 / 128.0, scalar2=RMS_EPS, op0=ALU.mult,
                                                    op1=ALU.add), [r_ssq], [r_ssq])
            yield
            S.emit("act", lambda e: e.activation(out=ssq, in_=ssq, func=AF.Ln), [r_ssq], [r_ssq])
            yield
            S.emit("act", lambda e: e.activation(out=ssq, in_=ssq, func=AF.Exp, scale=-0.5), [r_ssq], [r_ssq])
            yield
            S.emit("dve", lambda e: e.tensor_tensor(out=ot, in0=ot, in1=ssq.unsqueeze(2).to_broadcast([C, H, 128]),
                                                     op=ALU.mult), [r_ot, r_ssq], [r_ot])
            yield
            pz, prz = ps_next("B")
            tr_multi([(pz[:, h * C:(h + 1) * C], ot[:, h, :], idC) for h in range(H)], [prz], [r_ot, r_cst])
            yield
            pz3 = pz[:, 0:HC].rearrange("p (h x) -> p h x", h=H)
            S.emit("dve", lambda e: e.scalar_tensor_tensor(
                out=obuf[:, 0:6, col0:col0 + C], in0=pz3, scalar=spv(SP_NORMW + j), in1=qkvz[:, 18:24, col0:col0 + C],
                op0=ALU.mult, op1=ALU.mult), [prz, r_sp, r_qkvz], [r_obuf])
            yield

        def layernorm(goff, boff, p):
            if S.dry:
                return
            biga_phase("ln")
            for (c0, n) in coltiles(p):
                S.emit("act", lambda e, c0=c0, n=n: e.activation(out=ln_zb[:, :, 0:n], in_=xres[:, :, c0:c0 + n], func=AF.Copy),
                       [r_xres], [r_zb])
                S.emit("act", lambda e, c0=c0, n=n: e.activation(out=ln_sqb[:, :, 0:n], in_=xres[:, :, c0:c0 + n],
                                                                func=AF.Square), [r_xres], [r_sqb])
                pm, prm = ps_next()
                mm_group(pm[:, 0:n], prm, [(meanb[:, :], ln_zb[:, kc, 0:n]) for kc in range(8)], [r_cb, r_zb])
                pq, prq = ps_next()
                mm_group(pq[:, 0:n], prq, [(meanb[:, :], ln_sqb[:, kc, 0:n]) for kc in range(8)], [r_cb, r_sqb])
                S.emit("act", lambda e, pm=pm, n=n: e.activation(out=ln_mean[:, 0:n], in_=pm[:, 0:n], func=AF.Copy), [prm], [r_mean])
                S.emit("dve", lambda e, n=n: e.tensor_tensor(out=ln_rstd[:, 0:n], in0=ln_mean[:, 0:n], in1=ln_mean[:, 0:n],
                                                             op=ALU.mult), [r_mean], [r_rstd])
                S.emit("dve", lambda e, pq=pq, n=n: e.tensor_tensor(out=ln_rstd[:, 0:n], in0=pq[:, 0:n], in1=ln_rstd[:, 0:n],
                                                                    op=ALU.subtract), [prq, r_rstd], [r_rstd])
                S.emit("dve", lambda e, n=n: e.tensor_scalar(out=ln_rstd[:, 0:n], in0=ln_rstd[:, 0:n], scalar1=LN_EPS,
                                                             scalar2=None, op0=ALU.add), [r_rstd], [r_rstd])
                S.emit("act", lambda e, n=n: e.activation(out=ln_rstd[:, 0:n], in_=ln_rstd[:, 0:n], func=AF.Ln), [r_rstd], [r_rstd])
                S.emit("act", lambda e, n=n: e.activation(out=ln_rstd[:, 0:n], in_=ln_rstd[:, 0:n], func=AF.Exp, scale=-0.5),
                       [r_rstd], [r_rstd])
                S.emit("dve", lambda e, c0=c0, n=n: e.tensor_tensor(
                    out=ln_t1[:, :, 0:n], in0=xres[:, :, c0:c0 + n],
                    in1=ln_mean[:, 0:n].unsqueeze(1).to_broadcast([128, 8, n]), op=ALU.subtract), [r_xres, r_mean], [r_t1])
                S.emit("dve", lambda e, n=n: e.tensor_tensor(
                    out=ln_t1[:, :, 0:n], in0=ln_t1[:, :, 0:n],
                    in1=ln_rstd[:, 0:n].unsqueeze(1).to_broadcast([128, 8, n]), op=ALU.mult), [r_t1, r_rstd], [r_t1])
                for kc in range(8):
                    S.emit("act", lambda e, kc=kc, c0=c0, n=n: e.activation(
                        out=xres[:, kc, c0:c0 + n], in_=ln_t1[:, kc, 0:n], func=AF.Identity,
                        scale=spv(goff + kc), bias=spv(boff + kc)), [r_t1, r_sp], [r_xres])
                S.emit("act", lambda e, c0=c0, n=n: e.activation(out=xbf[:, :, c0:c0 + n], in_=xres[:, :, c0:c0 + n], func=AF.Copy),
                       [r_xres], [r_xbf])

        def ffn(l, p):
            scr_phase("ffn")
            biga_phase("act")
            Wu = w_up[l]
            ncol = NCP + (NSC if p == 1 else 0)

            def bufs(i, half):
                return ffn_raw[half][i % 2], ffn_raws[half][i % 2], ffn_acc[half][i % 2]

            def stage1(i):
                wt, rw = ws.stage(8, [(0, 128, Wu[:, i * 128:(i + 1) * 128]),
                                      (128, 128, Wu[:, D_FF + i * 128:D_FF + (i + 1) * 128])])
                if S.dry:
                    return
                for half in range(2):
                    cidx = half * 22 + i
                    (rawv, rraw), (rawsv, rraws), (accv, racc) = bufs(i, half)
                    S.emit("dve", lambda e, cidx=cidx, rawv=rawv: e.tensor_copy(out=rawv[:, 1:3], in_=ff_carry[:, l, cidx, :]),
                           [r_ffc], [rraw])
                    if p == 1:
                        S.dma("sp", rawsv[:, NB:3 * NB].rearrange("p (r b) -> p r b", b=NB),
                              ffT[l, cidx * 128:(cidx + 1) * 128, :, :], [], [rraws], d_hf[half][i % 2])
                    for (c0, n) in coltiles(p):
                        pt, pr = ps_next()
                        mm_group(pt[:, 0:n], pr, [(wt[:, kc, half * 128:(half + 1) * 128], xbf[:, kc, c0:c0 + n])
                                                 for kc in range(8)], [r_xbf, rw])
                        dst = raw_dst(c0, n, rawv, rawsv)
                        S.emit("act", lambda e, pt=pt, dst=dst, c0=c0, n=n: e.activation(
                            out=dst, in_=ps_src(pt, c0, n), func=AF.Copy), [pr], [rraws if is_s(c0) else rraw])
                    conv_taps(p, accv, racc, rawv, rraw, rawsv, rraws, SP_CONVF + (l * 44 + cidx) * 3, 3, only="first")

            def stage2(i):
                for half in range(2):
                    cidx = half * 22 + i
                    (rawv, rraw), (rawsv, rraws), (accv, racc) = bufs(i, half)
                    conv_taps(p, accv, racc, rawv, rraw, rawsv, rraws, SP_CONVF + (l * 44 + cidx) * 3, 3, only="rest")
                    S.emit("dve", lambda e, cidx=cidx, rawv=rawv: e.tensor_copy(out=ff_carry[:, l, cidx, :],
                                                                                in_=rawv[:, 3 + NCP - 2:3 + NCP]), [rraw], [r_ffc])
                    if p == 1:
                        S.dma("sp", ffsT[l, cidx * 128:(cidx + 1) * 128, :, :],
                              rawsv[:, 5 * NB:7 * NB].rearrange("p (r b) -> p r b", b=NB), [rraws], [], d_ohf[half][i % 2])

            def stage3(i):
                (ag, rag) = ffn_acc[0][i % 2]
                (au, rau) = ffn_acc[1][i % 2]
                S.emit("act", lambda e: e.activation(out=ag[:, 0:ncol], in_=ag[:, 0:ncol], func=AF.Silu), [rag], [rag])
                S.emit("dve", lambda e: e.tensor_tensor(out=actb[:, i, 0:ncol], in0=ag[:, 0:ncol], in1=au[:, 0:ncol],
                                                        op=ALU.mult), [rag, rau], [r_actb])

            stage1(0)
            for i in range(22):
                if i + 1 < 22:
                    stage1(i + 1)
                if not S.dry:
                    stage2(i)
                    stage3(i)
            Wd = w_down[l]
            for oc in range(8):
                wt, rw = ws.stage(22, [(0, 128, Wd[:, oc * 128:(oc + 1) * 128])])
                if S.dry:
                    continue
                for (c0, n) in coltiles(p):
                    pt, pr = ps_next()
                    mm_group(pt[:, 0:n], pr, [(wt[:, kc, :], actb[:, kc, c0:c0 + n]) for kc in range(22)], [r_actb, rw])
                    S.emit("dve", lambda e, pt=pt, oc=oc, c0=c0, n=n: e.scalar_tensor_tensor(
                        out=xres[:, oc, c0:c0 + n], in0=xres[:, oc, c0:c0 + n], scalar=ALPHA, in1=pt[:, 0:n],
                        op0=ALU.mult, op1=ALU.add), [pr, r_xres], [r_xres])

        _mixer_b = mixer_b

        def mixer_b(l, p):
            cur_j[0] = l // 2
            _mixer_b(l, p)

        S.dry = True
        ws.planning = True
        program()
        S.dry = False
        ws.planning = False
        program()
        build.stats = dict(ninst=S.ninst, nstages=len(ws.plan), counts={k: v["cnt"] for k, v in S.eng.items()})
    return nc


def _consts():
    c = np.zeros((128, NCST), np.float32)
    c[:, C_ID:C_ID + 128] = np.eye(128, dtype=np.float32)
    c[:, C_ONE:C_ONE + 128] = 1.0
    k = np.arange(64)
    c[0:64, C_TRI:C_TRI + 64] = (k[:, None] <= k[None, :]).astype(np.float32)
    c[0:64, C_MASK:C_MASK + 64] = np.where(k[:, None] <= k[None, :], 0.0, NEG).astype(np.float32)
    c[0:64, C_STR:C_STR + 64] = (k[:, None] < k[None, :]).astype(np.float32)
    return c


def _small_params(conv_a, conv_b, w_conv_ffn, ln1_g, ln1_b, ln2_g, ln2_b, gdn_norm_w, a_log, dt_bias):
    sp = np.zeros((128, NSP), np.float32)

    def fm(w, nchunk):
        L, J, F = w.shape
        return np.ascontiguousarray(w.reshape(L, J, nchunk, 128).transpose(3, 0, 2, 1)).reshape(128, -1)

    sp[:, SP_CONVA:SP_CONVA + 36] = fm(conv_a, 6)
    sp[:, SP_CONVB:SP_CONVB + 144] = fm(conv_b, 18)
    sp[:, SP_CONVF:SP_CONVF + 528] = fm(w_conv_ffn, 44)
    for off, a in ((SP_LN1G, ln1_g), (SP_LN1B, ln1_b), (SP_LN2G, ln2_g), (SP_LN2B, ln2_b)):
        sp[:, off:off + 32] = a.reshape(DEPTH, 8, 128).transpose(2, 0, 1).reshape(128, 32)
    sp[:, SP_NORMW:SP_NORMW + 2] = gdn_norm_w.T
    sp[:, SP_ALOG:SP_ALOG + 12] = a_log.reshape(1, 12)
    sp[:, SP_DTB:SP_DTB + 12] = dt_bias.reshape(1, 12)
    return sp


def make_in_maps(inp, cores):
    f = lambda a: np.ascontiguousarray(np.asarray(a, dtype=np.float32))
    sp = _small_params(f(inp["conv_a"]), f(inp["conv_b"]), f(inp["w_conv_ffn"]), f(inp["ln1_g"]), f(inp["ln1_b"]),
                       f(inp["ln2_g"]), f(inp["ln2_b"]), f(inp["gdn_norm_w"]), f(inp["a_log"]), f(inp["dt_bias"]))
    cst = _consts()
    shared = {k: f(inp[k]) for k in ("w_in_a", "w_out_a", "w_in_b", "w_out_b", "w_mem_kv", "w_up", "w_down")}
    wbaT = f(np.asarray(inp["w_in_b"])[:, :, 3072:3084].reshape(2, 8, 128, 12).transpose(0, 2, 1, 3))
    maps = []
    for c in cores:
        b0, b1 = c * NB, (c + 1) * NB
        m = dict(shared)
        m["xpT"] = f(np.asarray(inp["x_prompt"][c]).T)
        m["xsT"] = f(np.asarray(inp["x_sample"][b0:b1]).reshape(NSC, D).T)
        m["memT"] = f(np.asarray(inp["mem_prompt"][c]).T)
        ck = np.asarray(inp["cache_mem_k"][:, b0:b1]).reshape(DEPTH, NB, NMEM, 256)
        m["ckT"] = f(ck.transpose(0, 3, 1, 2))
        m["cv"] = f(np.asarray(inp["cache_mem_v"][:, b0:b1]).reshape(DEPTH, NB, NMEM, 256))
        m["scT"] = f(np.asarray(inp["state_shortconv"][:, b0:b1]).transpose(0, 3, 2, 1))
        m["gcT"] = f(np.asarray(inp["state_gdn_conv"][:, b0:b1]).transpose(0, 3, 2, 1))
        m["gs"] = f(np.asarray(inp["state_gdn"][:, b0:b1]).transpose(0, 1, 3, 2, 4))
        m["wbaT"] = wbaT
        m["ffT"] = f(np.asarray(inp["state_ffn_conv"][:, b0:b1]).transpose(0, 3, 2, 1))
        m["spd"] = sp
        m["cstd"] = cst
        maps.append(m)
    return maps


def assemble(results, ncores):
    B = ncores
    y_p = np.stack([r["ypT"].T for r in results])
    y_s = np.concatenate([r["ysT"].T.reshape(NB, TS, D) for r in results])
    mk_ = np.stack([r["mk"] for r in results], axis=1).reshape(DEPTH, B, NMEM, 4, 64)
    mv_ = np.stack([r["mv"] for r in results], axis=1).reshape(DEPTH, B, NMEM, 4, 64)
    sc_p = np.stack([r["scpT"].transpose(0, 3, 2, 1).reshape(2, 2, SC_DIM) for r in results], axis=1)
    gc_p = np.stack([r["gcpT"].transpose(0, 3, 2, 1).reshape(2, 3, 2304) for r in results], axis=1)
    gs_p = np.stack([r["gsp"].transpose(0, 2, 1, 3) for r in results], axis=1)
    ff_p = np.stack([r["ffpT"].transpose(0, 3, 2, 1).reshape(DEPTH, 2, 2 * D_FF) for r in results], axis=1)
    sc_s = np.concatenate([r["scsT"].transpose(0, 3, 2, 1) for r in results], axis=1)
    gc_s = np.concatenate([r["gcsT"].transpose(0, 3, 2, 1) for r in results], axis=1)
    gs_s = np.concatenate([r["gss"].transpose(0, 1, 3, 2, 4) for r in results], axis=1)
    ff_s = np.concatenate([r["ffsT"].transpose(0, 3, 2, 1) for r in results], axis=1)
    outs = (y_p, y_s, mk_, mv_, sc_p, gc_p, gs_p, ff_p, sc_s, gc_s, gs_s, ff_s)
    return tuple(np.ascontiguousarray(o, dtype=np.float32) for o in outs)


def kernel(**inputs):
    nc = build()
    maps = make_in_maps(inputs, list(range(8)))
    res = run_bass_kernel_spmd(nc, maps, core_ids=list(range(8)))
    return assemble(res.results, 8)
```
